# Optimizing a Trainium2 kernel written in Bass

```python
import math
import jax, jax.numpy as jnp
from jax import lax
import numpy as np

D_MODEL = 2048
BATCH = 4
SEQ = 4096
DEPTH = 2

D_MIX = D_MODEL
HEAD_DIM = 128
ATT_WIDTH = D_MIX // 2
ATT_HEADS = ATT_WIDTH // HEAD_DIM
ATT_KV_HEADS = 2
ATT_GROUP = ATT_HEADS // ATT_KV_HEADS
KV_WIDTH = ATT_KV_HEADS * HEAD_DIM
S5_WIDTH = D_MIX // 4
POOL_WIDTH = D_MIX - ATT_WIDTH - S5_WIDTH

CMP_BLOCK = 32
CMP_STRIDE = 16
CMP_HIDDEN = 256
SEL_BLOCK = 64
SEL_TOP = 16
WINDOW = 512
Q_BLOCK = 128
SEL_Q_CHUNK = 64
FORCE_SCORE = 1e4
ROPE_THETA = 10000.0

S5_GROUP = 16
S5_GROUPS = S5_WIDTH // S5_GROUP
S5_STATE = 64
DT_MIN = 0.001
DT_MAX = 0.1

POOL_WINDOWS = (2, 4, 8, 16)
POOL_GROUP = POOL_WIDTH // len(POOL_WINDOWS)

RMS_EPS = 1e-6
NEG_INF = -1e30

SPLIT_SIZES = (ATT_WIDTH, KV_WIDTH, KV_WIDTH, KV_WIDTH, KV_WIDTH, KV_WIDTH, KV_WIDTH,
               3 * ATT_HEADS, ATT_WIDTH, S5_WIDTH, S5_WIDTH, POOL_WIDTH, POOL_WIDTH)
IN_WIDTH = sum(SPLIT_SIZES)

kernel_name = "hymba_nsa_s5_pool_hybrid"


def rms_norm(x, g):
    xf = x.astype(jnp.float32)
    y = xf * lax.rsqrt(jnp.mean(xf * xf, axis=-1, keepdims=True) + RMS_EPS)
    return (y * g.astype(jnp.float32)).astype(x.dtype)


def apply_rope(x):
    s = x.shape[1]
    half = HEAD_DIM // 2
    inv_freq = ROPE_THETA ** (-jnp.arange(half, dtype=jnp.float32) / half)
    ang = jnp.arange(s, dtype=jnp.float32)[:, None] * inv_freq[None, :]
    cos = jnp.cos(ang)[None, :, None, :]
    sin = jnp.sin(ang)[None, :, None, :]
    xf = x.astype(jnp.float32)
    x1, x2 = xf[..., :half], xf[..., half:]
    return jnp.concatenate([x1 * cos - x2 * sin, x2 * cos + x1 * sin], axis=-1).astype(x.dtype)


def masked_softmax(s, mask, axis):
    s = jnp.where(mask, s.astype(jnp.float32), NEG_INF)
    m = jnp.max(s, axis=axis, keepdims=True)
    e = jnp.where(mask, jnp.exp(s - m), 0.0)
    return e / jnp.maximum(jnp.sum(e, axis=axis, keepdims=True), 1e-30)


def compress_blocks(t, pos_emb, w1, w2):
    b, s, g, d = t.shape
    ch = t.reshape(b, s // CMP_STRIDE, CMP_STRIDE, g, d)
    blocks = jnp.concatenate([ch[:, :-1], ch[:, 1:]], axis=2)
    blocks = blocks + pos_emb[None, None, :, None, :]
    h = jax.nn.silu(jnp.einsum('bnlgd,ldh->bngh', blocks, w1))
    return jnp.einsum('bngh,hd->bngd', h, w2)


def banded_blocks(t):
    b, s, g, d = t.shape
    nqb = s // Q_BLOCK
    n_prev = WINDOW // Q_BLOCK
    tp = jnp.pad(t, ((0, 0), (WINDOW, 0), (0, 0), (0, 0))).reshape(b, nqb + n_prev, Q_BLOCK, g, d)
    return jnp.concatenate([tp[:, j:j + nqb] for j in range(n_prev + 1)], axis=2)


def nsa_mixer(q, k_cmp, v_cmp, k_sel, v_sel, k_win, v_win, gate_logits,
              cmp_pos_k, cmp_pos_v, cmp_w1_k, cmp_w2_k, cmp_w1_v, cmp_w2_v):
    b, s = q.shape[:2]
    g, r, d = ATT_KV_HEADS, ATT_GROUP, HEAD_DIM
    scale = d ** -0.5
    pos = jnp.arange(s)
    qg = q.reshape(b, s, g, r, d)

    kc = compress_blocks(k_cmp, cmp_pos_k, cmp_w1_k, cmp_w2_k)
    vc = compress_blocks(v_cmp, cmp_pos_v, cmp_w1_v, cmp_w2_v)
    n_cmp = kc.shape[1]
    c_start = jnp.arange(n_cmp) * CMP_STRIDE
    s_c = jnp.einsum('bsgrd,bngd->bgrsn', qg, kc) * scale
    mask_c = (c_start + CMP_BLOCK - 1)[None, :] <= pos[:, None]
    p_c = masked_softmax(s_c, mask_c, -1)
    o_c = jnp.einsum('bgrsn,bngd->bsgrd', p_c.astype(vc.dtype), vc)

    n_sel = s // SEL_BLOCK
    sel_start = jnp.arange(n_sel) * SEL_BLOCK
    overlap = ((c_start[:, None] < sel_start[None, :] + SEL_BLOCK)
               & (c_start[:, None] + CMP_BLOCK > sel_start[None, :])).astype(jnp.float32)
    imp = jnp.einsum('bgrsn,nj->bgsj', p_c, overlap)
    cur = pos // SEL_BLOCK
    blk = jnp.arange(n_sel)
    forced = (blk[None, :] == cur[:, None]) | (blk[None, :] == 0)
    causal_blk = blk[None, :] <= cur[:, None]
    imp = jnp.where(forced, FORCE_SCORE, jnp.where(causal_blk, imp, -1.0))
    n_top = min(SEL_TOP, n_sel)
    _, idx = lax.top_k(imp, n_top)

    q_rot = apply_rope(q).reshape(b, s, g, r, d)
    ks = apply_rope(k_sel).reshape(b, n_sel, SEL_BLOCK, g, d).transpose(0, 3, 1, 2, 4)
    vs = v_sel.reshape(b, n_sel, SEL_BLOCK, g, d).transpose(0, 3, 1, 2, 4)
    n_ch = s // SEL_Q_CHUNK
    q_ch = q_rot.reshape(b, n_ch, SEL_Q_CHUNK, g, r, d).transpose(1, 0, 2, 3, 4, 5)
    idx_ch = idx.reshape(b, g, n_ch, SEL_Q_CHUNK, n_top).transpose(2, 0, 1, 3, 4)
    pos_ch = pos.reshape(n_ch, SEL_Q_CHUNK)
    gather = jax.vmap(jax.vmap(lambda blocks, ids: blocks[ids]))

    def sel_chunk(args):
        qc, ic, tc = args
        kg = gather(ks, ic)
        vg = gather(vs, ic)
        sc = jnp.einsum('bcgrd,bgcnkd->bgrcnk', qc, kg) * scale
        key_pos = ic[..., None] * SEL_BLOCK + jnp.arange(SEL_BLOCK)
        mask = (key_pos <= tc[None, None, :, None, None])[:, :, None]
        p = masked_softmax(sc, mask, (-2, -1))
        return jnp.einsum('bgrcnk,bgcnkd->bcgrd', p.astype(vg.dtype), vg)

    o_s = lax.map(sel_chunk, (q_ch, idx_ch, pos_ch))
    o_s = o_s.transpose(1, 0, 2, 3, 4, 5).reshape(b, s, g, r, d)

    nqb = s // Q_BLOCK
    kw = banded_blocks(apply_rope(k_win))
    vw = banded_blocks(v_win)
    qb = q_rot.reshape(b, nqb, Q_BLOCK, g, r, d)
    s_w = jnp.einsum('biqgrd,bikgd->bgriqk', qb, kw) * scale
    q_pos = pos.reshape(nqb, Q_BLOCK)
    k_pos = jnp.arange(nqb)[:, None] * Q_BLOCK - WINDOW + jnp.arange(Q_BLOCK + WINDOW)[None, :]
    kp = k_pos[:, None, :]
    mask_w = (kp >= 0) & (kp <= q_pos[:, :, None]) & (kp > q_pos[:, :, None] - WINDOW)
    p_w = masked_softmax(s_w, mask_w, -1)
    o_w = jnp.einsum('bgriqk,bikgd->biqgrd', p_w.astype(vw.dtype), vw).reshape(b, s, g, r, d)

    gates = jax.nn.sigmoid(gate_logits.astype(jnp.float32)).reshape(b, s, g, r, 3, 1)
    o = gates[..., 0, :] * o_c + gates[..., 1, :] * o_s + gates[..., 2, :] * o_w
    return o.astype(q.dtype).reshape(b, s, ATT_WIDTH)


def _ssm_combine(e1, e2):
    a1r, a1i, b1r, b1i = e1
    a2r, a2i, b2r, b2i = e2
    return (a2r * a1r - a2i * a1i, a2r * a1i + a2i * a1r,
            a2r * b1r - a2i * b1i + b2r, a2r * b1i + a2i * b1r + b2i)


def s5_mixer(u, a_re, a_im, log_dt, b_re, b_im, c_re, c_im, d_skip, glu_w, glu_b):
    bsz, s = u.shape[:2]
    uf = u.astype(jnp.float32).reshape(bsz, s, S5_GROUPS, S5_GROUP)
    ar, ai = a_re.astype(jnp.float32), a_im.astype(jnp.float32)
    dt = jnp.exp(log_dt.astype(jnp.float32))[:, None]
    mag = jnp.exp(ar * dt)
    abr, abi = mag * jnp.cos(ai * dt), mag * jnp.sin(ai * dt)
    den = ar * ar + ai * ai
    nr, ni = abr - 1.0, abi
    cr, ci = (nr * ar + ni * ai) / den, (ni * ar - nr * ai) / den
    br, bi = b_re.astype(jnp.float32), b_im.astype(jnp.float32)
    bbr = cr[..., None] * br - ci[..., None] * bi
    bbi = cr[..., None] * bi + ci[..., None] * br
    xr = jnp.einsum('bsgc,gnc->bsgn', uf, bbr)
    xi = jnp.einsum('bsgc,gnc->bsgn', uf, bbi)
    ar_t = jnp.broadcast_to(abr, xr.shape)
    ai_t = jnp.broadcast_to(abi, xr.shape)
    _, _, hr, hi = lax.associative_scan(_ssm_combine, (ar_t, ai_t, xr, xi), axis=1)
    y = (jnp.einsum('bsgn,gcn->bsgc', hr, c_re.astype(jnp.float32))
         - jnp.einsum('bsgn,gcn->bsgc', hi, c_im.astype(jnp.float32))
         + d_skip.astype(jnp.float32) * uf)
    y = jax.nn.gelu(y.reshape(bsz, s, S5_WIDTH)).astype(u.dtype)
    ab = jnp.einsum('bsc,ce->bse', y, glu_w) + glu_b
    a, gte = jnp.split(ab, 2, axis=-1)
    return a * jax.nn.sigmoid(gte)


def pool_mixer(u, pool_w, pool_scale):
    bsz, s = u.shape[:2]
    uf = u.astype(jnp.float32)
    cs = jnp.cumsum(uf, axis=1)
    pos = jnp.arange(s)
    outs = []
    for gi, w in enumerate(POOL_WINDOWS):
        sl = slice(gi * POOL_GROUP, (gi + 1) * POOL_GROUP)
        c = cs[..., sl]
        lower = jnp.pad(c, ((0, 0), (w, 0), (0, 0)))[:, :s]
        cnt = jnp.minimum(pos + 1, w).astype(jnp.float32)[None, :, None]
        outs.append((c - lower) / cnt - uf[..., sl])
    pooled = jnp.stack(outs, axis=2).astype(u.dtype)
    y = jnp.einsum('bskc,kcd->bskd', pooled, pool_w).reshape(bsz, s, POOL_WIDTH)
    return y * pool_scale


def setup_inputs(seed: int = 0) -> dict:
    key = jax.random.key(seed)
    ks = jax.random.split(key, 24)
    L = DEPTH

    def nrm(k, shape, scale):
        return jax.random.normal(k, shape, jnp.float32) * scale

    n_idx = jnp.arange(S5_STATE, dtype=jnp.float32)
    return {
        "x": nrm(ks[0], (BATCH, SEQ, D_MODEL), 1.0),
        "norm_pre": 1.0 + nrm(ks[1], (L, D_MODEL), 0.01),
        "norm_post": 1.0 + nrm(ks[2], (L, D_MODEL), 0.01),
        "w_in": nrm(ks[3], (L, D_MODEL, IN_WIDTH), D_MODEL ** -0.5),
        "w_out": nrm(ks[4], (L, D_MIX, D_MODEL), D_MIX ** -0.5),
        "cmp_pos_k": nrm(ks[5], (L, CMP_BLOCK, HEAD_DIM), 0.02),
        "cmp_pos_v": nrm(ks[6], (L, CMP_BLOCK, HEAD_DIM), 0.02),
        "cmp_w1_k": nrm(ks[7], (L, CMP_BLOCK, HEAD_DIM, CMP_HIDDEN), (CMP_BLOCK * HEAD_DIM) ** -0.5),
        "cmp_w2_k": nrm(ks[8], (L, CMP_HIDDEN, HEAD_DIM), CMP_HIDDEN ** -0.5),
        "cmp_w1_v": nrm(ks[9], (L, CMP_BLOCK, HEAD_DIM, CMP_HIDDEN), (CMP_BLOCK * HEAD_DIM) ** -0.5),
        "cmp_w2_v": nrm(ks[10], (L, CMP_HIDDEN, HEAD_DIM), CMP_HIDDEN ** -0.5),
        "s5_a_re": -0.5 + nrm(ks[11], (L, S5_GROUPS, S5_STATE), 0.01),
        "s5_a_im": math.pi * n_idx + nrm(ks[12], (L, S5_GROUPS, S5_STATE), 0.01),
        "s5_log_dt": jax.random.uniform(ks[13], (L, S5_GROUPS), jnp.float32,
                                        math.log(DT_MIN), math.log(DT_MAX)),
        "s5_b_re": nrm(ks[14], (L, S5_GROUPS, S5_STATE, S5_GROUP), (2 * S5_GROUP) ** -0.5),
        "s5_b_im": nrm(ks[15], (L, S5_GROUPS, S5_STATE, S5_GROUP), (2 * S5_GROUP) ** -0.5),
        "s5_c_re": nrm(ks[16], (L, S5_GROUPS, S5_GROUP, S5_STATE), S5_STATE ** -0.5),
        "s5_c_im": nrm(ks[17], (L, S5_GROUPS, S5_GROUP, S5_STATE), S5_STATE ** -0.5),
        "s5_d": nrm(ks[18], (L, S5_GROUPS, S5_GROUP), 1.0),
        "s5_glu_w": nrm(ks[19], (L, S5_WIDTH, 2 * S5_WIDTH), S5_WIDTH ** -0.5),
        "s5_glu_b": nrm(ks[20], (L, 2 * S5_WIDTH), 0.01),
        "pool_w": nrm(ks[21], (L, len(POOL_WINDOWS), POOL_GROUP, POOL_GROUP), POOL_GROUP ** -0.5),
        "pool_scale": 1.0 + nrm(ks[22], (L, POOL_WIDTH), 0.1),
    }


def reference(x, norm_pre, norm_post, w_in, w_out, cmp_pos_k, cmp_pos_v, cmp_w1_k, cmp_w2_k,
              cmp_w1_v, cmp_w2_v, s5_a_re, s5_a_im, s5_log_dt, s5_b_re, s5_b_im, s5_c_re, s5_c_im,
              s5_d, s5_glu_w, s5_glu_b, pool_w, pool_scale):
    bsz, s = x.shape[:2]
    split_points = [int(v) for v in np.cumsum(SPLIT_SIZES)[:-1]]
    for l in range(DEPTH):
        h = rms_norm(x, norm_pre[l])
        proj = jnp.einsum('bsd,de->bse', h, w_in[l])
        (q, kc, vc, ksl, vsl, kwn, vwn, gl, z_att,
         u_s5, z_s5, u_pool, z_pool) = jnp.split(proj, split_points, axis=-1)
        heads = lambda t, n: t.reshape(bsz, s, n, HEAD_DIM)
        att = nsa_mixer(heads(q, ATT_HEADS),
                        heads(kc, ATT_KV_HEADS), heads(vc, ATT_KV_HEADS),
                        heads(ksl, ATT_KV_HEADS), heads(vsl, ATT_KV_HEADS),
                        heads(kwn, ATT_KV_HEADS), heads(vwn, ATT_KV_HEADS), gl,
                        cmp_pos_k[l], cmp_pos_v[l], cmp_w1_k[l], cmp_w2_k[l], cmp_w1_v[l], cmp_w2_v[l])
        att = att * jax.nn.silu(z_att)
        ssm = s5_mixer(u_s5, s5_a_re[l], s5_a_im[l], s5_log_dt[l], s5_b_re[l], s5_b_im[l],
                       s5_c_re[l], s5_c_im[l], s5_d[l], s5_glu_w[l], s5_glu_b[l]) * jax.nn.silu(z_s5)
        pool = pool_mixer(u_pool, pool_w[l], pool_scale[l]) * jax.nn.silu(z_pool)
        mix = jnp.concatenate([att, ssm, pool], axis=-1)
        out = jnp.einsum('bse,ed->bsd', mix, w_out[l])
        x = x + rms_norm(out, norm_post[l])
    return x
```

```python
import contextlib
import math
import numpy as np
import ml_dtypes
import concourse.bass as bass
import concourse.mybir as mybir
from concourse.bass_utils import run_bass_kernel_spmd

F32 = mybir.dt.float32
BF16 = mybir.dt.bfloat16
AF = mybir.ActivationFunctionType
ALU = mybir.AluOpType
AX = mybir.AxisListType

EPOCH = 7990
NDMASEM = 24
NPOOLSEM = 8

S = 4096
D = 2048
NT = S // 128
NQ = S // 512
DEPTH = 2
INW = 5656
HD = 128
SCALE = HD ** -0.5
NEGB = -30000.0


class Res:
    __slots__ = ("name", "lw", "rd", "excl")

    def __init__(self, name="", excl=False):
        self.name = name
        self.lw = None
        self.rd = {}
        self.excl = excl


class T:
    def __init__(self, t, name="", excl=False):
        self.t = t
        self.r = Res(name, excl)

    def __getitem__(self, k):
        return self.t[k]


def _res(x):
    return x.r if isinstance(x, T) else x


class Prog:
    ENGS = ("pe", "act", "dve", "pool", "sp")

    def __init__(self, nc, stack):
        self.nc = nc
        self.stack = stack
        self.ops = {e: [] for e in self.ENGS}
        self.sems = {e: [stack.enter_context(nc.semaphore(f"s_{e}_0"))] for e in self.ENGS}
        self.cnt = {e: 0 for e in self.ENGS}
        self.seen = {e: {} for e in self.ENGS}
        self.dsems = {"sp": [stack.enter_context(nc.semaphore(f"s_dma_{i}")) for i in range(NDMASEM)],
                      "pool": [stack.enter_context(nc.semaphore(f"s_dmap_{i}")) for i in range(NPOOLSEM)]}
        self.ndma = {"sp": 0, "pool": 0}
        self.strict = True
        self.nops = 0

    def _deps(self, reads, writes):
        deps = []
        for r in reads:
            r = _res(r)
            if r.lw is not None:
                deps.append(r.lw)
        for w in writes:
            w = _res(w)
            if w.lw is not None:
                deps.append(w.lw)
            deps.extend(w.rd.values())
        return deps

    def _waits(self, eng, deps):
        waits = {}
        seen = self.seen[eng]
        for (sem, val, src) in deps:
            if src == eng and (eng in ("pe", "sp") or not self.strict):
                continue
            k = id(sem)
            if seen.get(k, 0) >= val:
                continue
            if k not in waits or waits[k][1] < val:
                waits[k] = (sem, val)
        for k, (sem, val) in waits.items():
            seen[k] = val
        return list(waits.values())

    def _record(self, ev, reads, writes):
        key = id(ev[0])
        for r in reads:
            _res(r).rd[key] = ev
        for w in writes:
            w = _res(w)
            w.lw = ev
            w.rd = {}

    @staticmethod
    def _split(reads, writes):
        rd, wr = [], list(writes)
        for r in reads:
            if _res(r).excl:
                wr.append(r)
            else:
                rd.append(r)
        return rd, wr

    def op(self, eng, fn, reads=(), writes=()):
        reads, writes = self._split(reads, writes)
        deps = self._deps(reads, writes)
        waits = self._waits(eng, deps)
        if self.cnt[eng] >= EPOCH:
            self.sems[eng].append(
                self.stack.enter_context(self.nc.semaphore(f"s_{eng}_{len(self.sems[eng])}")))
            self.cnt[eng] = 0
        sem = self.sems[eng][-1]
        self.cnt[eng] += 1
        ev = (sem, self.cnt[eng], eng)
        self.ops[eng].append((waits, fn, sem, 1))
        self._record(ev, reads, writes)
        self.nops += 1
        return ev

    def dma(self, q, out, in_, reads=(), writes=(), **kw):
        reads, writes = self._split(reads, writes)
        deps = self._deps(reads, writes)
        i = self.ndma[q]
        self.ndma[q] += 1
        nsem = len(self.dsems[q])
        sem = self.dsems[q][i % nsem]
        tgt = 16 * (i // nsem + 1)
        if i >= nsem:
            deps.append((sem, tgt - 16, None))
        waits = self._waits(q, deps)
        fn = lambda e, out=out, in_=in_, kw=kw: e.dma_start(out=out, in_=in_, **kw)
        self.ops[q].append((waits, fn, sem, 16))
        ev = (sem, tgt, None)
        self._record(ev, reads, writes)
        self.nops += 1
        return ev

    def flush(self):
        deps = []
        for q in ("sp", "pool"):
            nsem = len(self.dsems[q])
            for j, sem in enumerate(self.dsems[q]):
                n = (self.ndma[q] - j + nsem - 1) // nsem if self.ndma[q] > j else 0
                if n > 0:
                    deps.append((sem, 16 * n, None))
        waits = self._waits("sp", deps)
        self.ops["sp"].append((waits, None, None, 0))
        nc = self.nc
        ops = self.ops
        self.ops = {e: [] for e in self.ENGS}
        with nc.Block() as block:
            def run(e, lst):
                for (waits, fn, sem, inc) in lst:
                    for (s, v) in waits:
                        e.wait_ge(s, v)
                    if fn is not None:
                        fn(e).then_inc(sem, inc)

            @block.tensor
            def _(e):
                run(e, ops["pe"])

            @block.scalar
            def _(e):
                run(e, ops["act"])

            @block.vector
            def _(e):
                run(e, ops["dve"])

            @block.gpsimd
            def _(e):
                run(e, ops["pool"])

            @block.sync
            def _(e):
                run(e, ops["sp"])


class B:
    def __init__(self, nc, P):
        self.nc = nc
        self.P = P
        self.flip = 0

    def act(self, out, in_, func, reads, writes, **kw):
        self.P.op("act", lambda e: e.activation(out=out, in_=in_, func=func, **kw), reads, writes)

    def mm(self, out, lhsT, rhs, start, stop, reads, writes, **kw):
        self.P.op("pe", lambda e: e.matmul(out, lhsT=lhsT, rhs=rhs, start=start, stop=stop, **kw),
                  reads, writes)

    def memset(self, eng, ap, val, writes):
        self.P.op(eng, lambda e: e.memset(ap, val), [], writes)

    def tr(self, out, in_, ident, reads, writes):
        self.P.op("pe", lambda e: e.transpose(out=out, in_=in_, identity=ident), reads, writes)

    def tt(self, eng, out, in0, in1, op, reads, writes):
        self.P.op(eng, lambda e: e.tensor_tensor(out=out, in0=in0, in1=in1, op=op), reads, writes)

    def ts(self, eng, out, in0, s1, s2, op0, op1, reads, writes):
        if op1 is None:
            self.P.op(eng, lambda e: e.tensor_scalar(out=out, in0=in0, scalar1=s1, scalar2=None, op0=op0),
                      reads, writes)
        else:
            self.P.op(eng, lambda e: e.tensor_scalar(out=out, in0=in0, scalar1=s1, scalar2=s2,
                                                      op0=op0, op1=op1), reads, writes)

    def stt(self, out, in0, scalar, in1, op0, op1, reads, writes):
        self.P.op("dve", lambda e: e.scalar_tensor_tensor(out=out, in0=in0, scalar=scalar, in1=in1,
                                                           op0=op0, op1=op1), reads, writes)

    def copy(self, eng, out, in_, reads, writes):
        if eng == "act":
            self.act(out, in_, AF.Copy, reads, writes)
        else:
            self.P.op(eng, lambda e: e.tensor_copy(out=out, in_=in_), reads, writes)

    def evac(self, out, in_, reads, writes):
        self.flip ^= 1
        self.copy("act" if self.flip else "dve", out, in_, reads, writes)

    def dma(self, q, out, in_, reads, writes, **kw):
        self.P.dma(q, out, in_, reads, writes, **kw)


OFF = {"q": 0, "kc": 1024, "vc": 1280, "ksl": 1536, "vsl": 1792, "kwn": 2048, "vwn": 2304,
       "gl": 2560, "zatt": 2584, "us5": 3608, "zs5": 4120, "upool": 4632, "zpool": 5144}
NWG = 12
WCOLS = 11 * 512 + 32


def _perm_cols():
    fm = []
    for h in range(8):
        fm.append(OFF["q"] + 128 * h)
    for nm in ("kc", "vc", "ksl", "kwn"):
        fm += [OFF[nm], OFF[nm] + 128]
    for nm in ("us5", "zs5", "upool", "zpool"):
        fm += [OFF[nm] + 128 * i for i in range(4)]
    cols = []
    for c in fm:
        cols += list(range(c, c + 128))
    cols += list(range(OFF["vsl"], OFF["vsl"] + 256)) + list(range(OFF["vwn"], OFF["vwn"] + 256))
    cols += list(range(OFF["zatt"], OFF["zatt"] + 1024))
    cols += list(range(OFF["gl"], OFF["gl"] + 24)) + [-1] * 8
    return np.array(cols)


def _consts():
    c = {}
    c["ident"] = np.eye(128, dtype=np.float32).astype(ml_dtypes.bfloat16)
    sw = np.zeros((128, 128), np.float32)
    for m in range(128):
        sw[(m + 64) % 128, m] = 1.0
    c["swapi"] = sw.astype(ml_dtypes.bfloat16)
    half = 64
    inv = 10000.0 ** (-np.arange(half, dtype=np.float32) / half)
    ang = np.arange(S, dtype=np.float32)[:, None] * inv[None, :]
    cos = np.cos(ang).astype(np.float32).T
    sin = np.sin(ang).astype(np.float32).T
    c["ropec"] = np.ascontiguousarray(np.concatenate([cos, cos], 0))
    c["ropes"] = np.ascontiguousarray(np.concatenate([-sin, sin], 0))
    bf = ml_dtypes.bfloat16
    p = np.arange(128)
    c["tribc"] = np.where(p[:, None] > p[None, :], NEGB, 0.0).astype(bf)
    c["tribw"] = np.where(p[:, None] <= p[None, :], NEGB, 0.0).astype(bf)
    u = np.arange(2560)
    c["tbc"] = np.where(16 * p[:, None] + 31 > u[None, :], NEGB, 0.0).astype(bf)
    c["blkexp"] = (np.arange(S)[None, :] // 64 == np.arange(64)[:, None]).astype(np.float32).astype(bf)
    n = np.arange(256)
    cs = n * 16
    js = np.arange(64) * 64
    ov = ((cs[:, None] < js[None, :] + 64) & (cs[:, None] + 32 > js[None, :])).astype(np.float32)
    ov = np.concatenate([ov, np.ones((256, 1), np.float32)], 1)
    ov[255] = 0.0
    c["ovl1"] = np.ascontiguousarray(ov.reshape(2, 128, 65).transpose(1, 0, 2)).astype(bf)
    q = np.arange(S)
    cur = q // 64
    j = np.arange(64)
    forced = (j[None, :] == cur[:, None]) | (j[None, :] == 0)
    causal = j[None, :] <= cur[:, None]
    sg = np.ones((128, 2), np.float32)
    sg[64:, 0] = -1.0
    sg[:64, 1] = -1.0
    c["sgn"] = sg
    t16 = np.arange(16)
    rc = np.stack([1.0 / np.minimum(t16 + 1, w) for w in (2, 4, 8, 16)], 0).astype(np.float32)
    c["poolrc"] = np.ascontiguousarray(np.broadcast_to(rc[None], (128, 4, 16))).astype(np.float32)
    c["impA"] = (causal & ~forced).astype(np.float32)
    c["impB"] = np.where(forced, 1e4, np.where(causal, 0.0, -1.0)).astype(np.float32)
    return c


class Kern:
    def __init__(self, debug=(), nlayers=DEPTH, phases=None, bstop=99):
        self.bstop = bstop
        self.debug = set(debug)
        self.nlayers = nlayers
        self.phases = phases
        self.nc = bass.Bass("TRN2", target_bir_lowering=False)
        self.gst = contextlib.ExitStack()
        self.P = None

    def din(self, name, shape, dt=F32):
        return self.nc.dram_tensor(name, list(shape), dt, kind="ExternalInput").ap()

    def dscr(self, name, shape, dt):
        kind = "ExternalOutput" if name in self.debug else "Internal"
        t = self.nc.dram_tensor(name, list(shape), dt, kind=kind).ap()
        return T(t, name)

    def build(self):
        nc = self.nc
        with self.gst as gst:
            self.P = Prog(nc, gst)
            self.b = B(nc, self.P)
            I = self.I = {}
            I["x"] = self.din("x", [S, D])
            I["norm_pre"] = self.din("norm_pre", [DEPTH, D])
            I["norm_post"] = self.din("norm_post", [DEPTH, D])
            I["w_in"] = self.din("w_in", [DEPTH, D, WCOLS])
            I["w_out"] = self.din("w_out", [DEPTH, D, D])
            I["ident"] = self.din("ident", [128, 128], BF16)
            I["swapi"] = self.din("swapi", [128, 128], BF16)
            I["ropec"] = self.din("ropec", [128, S])
            I["ropes"] = self.din("ropes", [128, S])
            for nm in ("tribc", "tribw"):
                I[nm] = self.din(nm, [128, 128], BF16)
            I["tbc"] = self.din("tbc", [128, 2560], BF16)
            I["blkexp"] = self.din("blkexp", [64, S], BF16)
            I["ovl1"] = self.din("ovl1", [128, 2, 65], BF16)
            I["impA"] = self.din("impA", [S, 64])
            I["impB"] = self.din("impB", [S, 64])
            for kv in ("k", "v"):
                I[f"cmp_w1_{kv}"] = self.din(f"cmp_w1_{kv}", [DEPTH, 32, 128, 256])
                I[f"cmp_w2_{kv}"] = self.din(f"cmp_w2_{kv}", [DEPTH, 256, 128])
                I[f"cmp_posT_{kv}"] = self.din(f"cmp_posT_{kv}", [DEPTH, 128, 32])
            I["sgn"] = self.din("sgn", [128, 2])
            I["s5_a_reT"] = self.din("s5_a_reT", [DEPTH, 64, 32])
            I["s5_a_imT"] = self.din("s5_a_imT", [DEPTH, 64, 32])
            I["s5_log_dt"] = self.din("s5_log_dt", [DEPTH, 32])
            for nm in ("s5_b_reT", "s5_b_imT", "s5_c_reT", "s5_c_imT"):
                I[nm] = self.din(nm, [DEPTH, 64, 32, 16])
            I["s5_dT"] = self.din("s5_dT", [DEPTH, 128, 4])
            I["s5_glu_bT"] = self.din("s5_glu_bT", [DEPTH, 128, 8])
            I["s5_glu_w"] = self.din("s5_glu_w", [DEPTH, 512, 1024])
            I["pool_w"] = self.din("pool_w", [DEPTH, 4, 128, 128])
            I["pool_scaleT"] = self.din("pool_scaleT", [DEPTH, 128, 4])
            I["poolrc"] = self.din("poolrc", [128, 4, 16])
            self.y = T(nc.dram_tensor("y", [S, D], F32, kind="ExternalOutput").ap(), "y")
            sc = self.sc = {}
            sc["qT"] = self.dscr("qT", [8, 128, S], BF16)
            sc["qrT"] = self.dscr("qrT", [8, 128, S], BF16)
            sc["kcT"] = self.dscr("kcT", [2, 128, S], BF16)
            sc["vcT"] = self.dscr("vcT", [2, 128, S], BF16)
            sc["kselT"] = self.dscr("kselT", [2, 128, S], BF16)
            sc["kwinT"] = self.dscr("kwinT", [2, 128, S], BF16)
            sc["us5T"] = self.dscr("us5T", [512, S], BF16)
            sc["zs5T"] = self.dscr("zs5T", [512, S], BF16)
            sc["upoolT"] = self.dscr("upoolT", [512, S], F32)
            sc["zpoolT"] = self.dscr("zpoolT", [512, S], BF16)
            sc["v"] = self.dscr("v", [S, 512], BF16)
            sc["zatt"] = self.dscr("zatt", [S, 1024], BF16)
            sc["gl"] = self.dscr("gl", [S, 32], F32)
            sc["mixT"] = self.dscr("mixT", [2048, S], BF16)
            sc["x1"] = self.dscr("x1", [S, D], F32)
            for l in range(self.nlayers):
                xin = I["x"] if l == 0 else sc["x1"]
                xout = self.y if l == self.nlayers - 1 else sc["x1"]
                if self.phases is None or "A" in self.phases:
                    self.phase_A(l, xin)
                if self.phases is None or "B" in self.phases:
                    self.phase_B(l)
                if self.phases is None or "C" in self.phases:
                    self.phase_C(l)
                if self.phases is None or "D" in self.phases:
                    self.phase_D(l)
                if self.phases is None or "E" in self.phases:
                    self.phase_E(l, xin, xout)
        return nc

    def phase_A(self, l, xin):
        nc, P, b, I, sc = self.nc, self.P, self.b, self.I, self.sc
        xin_ap = xin.t if isinstance(xin, T) else xin
        xin_res = [xin] if isinstance(xin, T) else []
        with contextlib.ExitStack() as st:
            def sb(name, shape, dt):
                return T(st.enter_context(nc.sbuf_tensor(f"A{l}_{name}", shape, dt)), name)

            def ps(name, shape, dt):
                return T(st.enter_context(nc.psum_tensor(f"A{l}_{name}", shape, dt)), name, True)

            hT = sb("hT", [128, 16, S], BF16)
            ident = sb("ident", [128, 128], BF16)
            swapi = sb("swapi", [128, 128], BF16)
            st_outer = st
            st = st_outer.enter_context(contextlib.ExitStack())
            gb = sb("gb", [128, D], F32)
            xt = [sb(f"xt{i}", [128, D], F32) for i in range(2)]
            sq = sb("sq", [128, D], BF16)
            hb = [sb(f"hb{i}", [128, D], BF16) for i in range(2)]
            st1 = [sb(f"st{i}", [128, 4], F32) for i in range(2)]
            ptr = [ps(f"ptr{i}", [128, 8, 128], BF16) for i in range(2)]

            b.dma("sp", ident[:], I["ident"][:, :], [], [ident])
            b.dma("sp", swapi[:], I["swapi"][:, :], [], [swapi])
            b.dma("sp", gb[:], I["norm_pre"][l:l + 1, :].to_broadcast([128, D]), [], [gb])

            for t in range(NT):
                x_t, h_t, s_t = xt[t % 2], hb[t % 2], st1[t % 2]
                b.dma("sp", x_t[:], xin_ap[t * 128:(t + 1) * 128, :], xin_res, [x_t])
                b.act(sq[:], x_t[:], AF.Square, [x_t], [sq, s_t], accum_out=s_t[:, 0:1])
                b.ts("dve", s_t[:, 1:2], s_t[:, 0:1], 1.0 / D, 1e-6, ALU.mult, ALU.add, [s_t], [s_t])
                b.act(s_t[:, 2:3], s_t[:, 1:2], AF.Sqrt, [s_t], [s_t])
                P.op("dve", lambda e, s_t=s_t: e.reciprocal(out=s_t[:, 3:4], in_=s_t[:, 2:3]), [s_t], [s_t])
                b.stt(h_t[:], x_t[:], s_t[:, 3:4], gb[:], ALU.mult, ALU.mult, [x_t, s_t, gb], [h_t])
                for half in range(2):
                    p_t = ptr[half]
                    for j in range(8):
                        kc = half * 8 + j
                        b.tr(p_t[:, j, :], h_t[:, kc * 128:(kc + 1) * 128], ident[:], [h_t, ident], [p_t])
                    b.evac(hT[:, half * 8:(half + 1) * 8, t * 128:(t + 1) * 128], p_t[:], [p_t], [hT])

            P.flush()
            st.close()
            st = st_outer
            wb = [sb(f"wb{i}", [128, 16, 512], BF16) for i in range(2)]
            rope = [sb(f"rope{i}", [128, 2, 512], F32) for i in range(2)]
            pacc = [ps(f"pacc{i}", [128, 512], F32) for i in range(3)]
            prot = [ps(f"prot{i}", [128, 512], F32) for i in range(2)]
            qun = [sb(f"qun{i}", [128, 512], BF16) for i in range(3)]
            t1 = [sb(f"t1{i}", [128, 512], F32) for i in range(2)]
            t2 = [sb(f"t2{i}", [128, 512], F32) for i in range(2)]
            qro = [sb(f"qro{i}", [128, 512], BF16) for i in range(2)]
            o32 = [sb(f"o32{i}", [128, 512], F32) for i in range(2)]
            ogl = [sb(f"ogl{i}", [128, 32], F32) for i in range(2)]
            win = I["w_in"]
            nacc = [0]
            nrot = [0]

            def load_w(gi):
                w_t = wb[gi % 2]
                ncol = 512 if gi < 11 else 32
                src = win[l, :, gi * 512:gi * 512 + ncol].rearrange("(kc p) n -> p kc n", p=128)
                b.dma("pool", w_t[:, :, 0:ncol], src, [], [w_t])

            load_w(0)
            for gi in range(NWG):
                if gi + 1 < NWG:
                    load_w(gi + 1)
                w_t = wb[gi % 2]
                if gi < 8:
                    do_rope = gi in (0, 1, 3)
                    for tq in range(NQ):
                        tsl = slice(tq * 512, (tq + 1) * 512)
                        if do_rope:
                            rp = rope[tq % 2]
                            b.dma("sp", rp[:, 0, :], I["ropec"][:, tsl], [], [rp])
                            b.dma("sp", rp[:, 1, :], I["ropes"][:, tsl], [], [rp])
                        for c in range(4):
                            pa = pacc[nacc[0] % 3]
                            nacc[0] += 1
                            for kc in range(16):
                                b.mm(pa[:], w_t[:, kc, c * 128:(c + 1) * 128], hT[:, kc, tsl],
                                     kc == 0, kc == 15, [w_t, hT], [pa])
                            ch = gi * 4 + c
                            if gi in (0, 1):
                                qu = qun[nacc[0] % 3]
                                b.act(qu[:], pa[:], AF.Copy, [pa], [qu])
                                b.dma("sp", sc["qT"][ch, :, tsl], qu[:], [qu], [sc["qT"]])
                                self._rope(b, qu, rp, swapi, prot, t1, t2, qro, nrot,
                                           sc["qrT"], sc["qrT"][ch, :, tsl])
                            elif gi == 2:
                                qu = qun[nacc[0] % 3]
                                b.act(qu[:], pa[:], AF.Copy, [pa], [qu])
                                dst = sc["kcT"] if c < 2 else sc["vcT"]
                                b.dma("sp", dst[c % 2, :, tsl], qu[:], [qu], [dst])
                            elif gi == 3:
                                qu = qun[nacc[0] % 3]
                                b.act(qu[:], pa[:], AF.Copy, [pa], [qu])
                                dst = sc["kselT"] if c < 2 else sc["kwinT"]
                                self._rope(b, qu, rp, swapi, prot, t1, t2, qro, nrot, dst, dst[c % 2, :, tsl])
                            elif gi in (4, 5, 7):
                                qu = qun[nacc[0] % 3]
                                dst = {4: sc["us5T"], 5: sc["zs5T"], 7: sc["zpoolT"]}[gi]
                                b.act(qu[:], pa[:], AF.Copy if gi == 4 else AF.Silu, [pa], [qu])
                                b.dma("sp", dst[c * 128:(c + 1) * 128, tsl], qu[:], [qu], [dst])
                            else:
                                o = o32[nacc[0] % 2]
                                b.act(o[:], pa[:], AF.Copy, [pa], [o])
                                b.dma("sp", sc["upoolT"][c * 128:(c + 1) * 128, tsl], o[:], [o], [sc["upoolT"]])
                else:
                    ncol = 512 if gi < 11 else 32
                    for t in range(NT):
                        tsl = slice(t * 128, (t + 1) * 128)
                        pa = pacc[nacc[0] % 3]
                        nacc[0] += 1
                        for kc in range(16):
                            b.mm(pa[:, 0:ncol], hT[:, kc, tsl], w_t[:, kc, 0:ncol], kc == 0, kc == 15,
                                 [w_t, hT], [pa])
                        if gi == 8:
                            qu = qun[nacc[0] % 3]
                            b.act(qu[:], pa[:], AF.Copy, [pa], [qu])
                            b.dma("sp", sc["v"][tsl, :], qu[:], [qu], [sc["v"]])
                        elif gi in (9, 10):
                            qu = qun[nacc[0] % 3]
                            b.act(qu[:], pa[:], AF.Silu, [pa], [qu])
                            b.dma("sp", sc["zatt"][tsl, (gi - 9) * 512:(gi - 8) * 512], qu[:], [qu], [sc["zatt"]])
                        else:
                            o = ogl[nacc[0] % 2]
                            b.act(o[:], pa[:, 0:32], AF.Sigmoid, [pa], [o])
                            b.dma("sp", sc["gl"][tsl, :], o[:], [o], [sc["gl"]])
            P.flush()

    def _rope(self, b, qu, rp, swapi, prot, t1, t2, qro, nrot, dst_res, dst_ap):
        i = nrot[0]
        nrot[0] += 1
        pr, a1, a2, qo = prot[i % 2], t1[i % 2], t2[i % 2], qro[i % 2]
        b.mm(pr[:], swapi[:], qu[:], True, True, [swapi, qu], [pr])
        b.tt("dve", a1[:], pr[:], rp[:, 1, :], ALU.mult, [pr, rp], [a1])
        b.tt("pool", a2[:], qu[:], rp[:, 0, :], ALU.mult, [qu, rp], [a2])
        b.tt("dve", qo[:], a1[:], a2[:], ALU.add, [a1, a2], [qo])
        b.dma("sp", dst_ap, qo[:], [qo], [dst_res])


    def phase_B(self, l):
        nc, P, b, I, sc = self.nc, self.P, self.b, self.I, self.sc
        with contextlib.ExitStack() as st:
            def sb(name, shape, dt):
                return T(st.enter_context(nc.sbuf_tensor(f"B{l}_{name}", shape, dt)), name)

            def ps(name, shape, dt):
                return T(st.enter_context(nc.psum_tensor(f"B{l}_{name}", shape, dt)), name, True)

            ident = sb("ident", [128, 128], BF16)
            tribc = sb("tribc", [128, 128], BF16)
            tribw = sb("tribw", [128, 128], BF16)
            tbc = sb("tbc", [128, 2560], BF16)
            blkexp = sb("blkexp", [128, S], BF16)
            zl = sb("zl", [128, 128], BF16)
            zr = sb("zr", [128, 512], BF16)
            for (t_, nm) in ((ident, "ident"), (tribc, "tribc"), (tribw, "tribw"), (tbc, "tbc")):
                b.dma("sp", t_[:], I[nm][:, :], [], [t_])
            b.memset("pool", blkexp[:], 0.0, [blkexp])
            b.dma("sp", blkexp[0:64, :], I["blkexp"][:, :], [], [blkexp])
            b.memset("pool", zl[:], 0.0, [zl])
            b.memset("pool", zr[:], 0.0, [zr])

            pS = [ps(f"pS{i}", [128, 512], F32) for i in range(2)]
            pO = [[ps(f"pO{a}{j}", [128, 2, 256], F32) for j in range(2)] for a in range(2)]
            pT = [ps(f"pT{i}", [128, 1024], BF16) for i in range(2)]

            qT = sb("qT", [128, 4, S], BF16)
            qrT = sb("qrT", [128, 4, S], BF16)
            kselT = sb("kselT", [128, S], BF16)
            kwinT = sb("kwinT", [128, S], BF16)
            vsel = sb("vsel", [128, NT, 129], BF16)
            vwin = sb("vwin", [128, NT, 129], BF16)
            kin = sb("kin", [128, S], BF16)
            vin = sb("vin", [128, S], BF16)
            w1k = sb("w1k", [128, 32, 256], BF16)
            w1v = sb("w1v", [128, 32, 256], BF16)
            w2k = sb("w2k", [128, 2, 128], BF16)
            w2v = sb("w2v", [128, 2, 128], BF16)
            posk = sb("posk", [128, 32], BF16)
            posv = sb("posv", [128, 32], BF16)
            hk = sb("hk", [128, 2, 256], BF16)
            hv = sb("hv", [128, 2, 256], BF16)
            cbias = sb("cbias", [128, 4], F32)
            kcmp = sb("kcmp", [128, 256], BF16)
            vcx = sb("vcx", [128, 2, 193], BF16)
            pt = [sb(f"pt{i}", [128, 512], BF16) for i in range(3)]
            zat = [sb(f"zat{i}", [128, 4, 512], BF16) for i in range(2)]
            glt = [sb(f"glt{i}", [128, 4, 32], F32) for i in range(2)]
            iA = [sb(f"iA{i}", [128, 4, 64], F32) for i in range(2)]
            iB = [sb(f"iB{i}", [128, 4, 64], F32) for i in range(2)]
            accO = [sb(f"accO{i}", [128, 4, 4, 128], F32) for i in range(2)]
            impacc = [sb(f"impacc{i}", [128, 4, 64], F32) for i in range(2)]
            selbT = [sb(f"selbT{i}", [128, 512], BF16) for i in range(2)]
            imp2 = sb("imp2", [128, 64], F32)
            impw = sb("impw", [128, 64], F32)
            m8 = sb("m8", [128, 16], F32)
            s01 = sb("s01", [128, 64], F32)
            sb128 = sb("sb128", [128, 128], BF16)
            dn = [sb(f"dn{i}", [128, 8], F32) for i in range(4)]
            attb = [sb(f"attb{i}", [128, 512], BF16) for i in range(2)]
            attT = [sb(f"attT{i}", [128, 4, 128], BF16) for i in range(2)]

            b.memset("pool", sb128[:], 0.0, [sb128])
            b.memset("pool", hk[:], 0.0, [hk])
            b.memset("pool", hv[:], 0.0, [hv])
            b.memset("pool", kcmp[:], 0.0, [kcmp])
            b.memset("pool", vsel[:, :, 128:129], 1.0, [vsel])
            b.memset("pool", vwin[:, :, 128:129], 1.0, [vwin])
            b.dma("sp", vcx[:, :, 128:193], I["ovl1"][:, :, :], [], [vcx])

            cnt = {"S": 0, "O": 0, "P": 0, "T": 0, "dn": 0}

            for g in range(2):
                b.dma("sp", qT[:], sc["qT"].t[g * 4:(g + 1) * 4].rearrange("r p n -> p r n"), [sc["qT"]], [qT])
                b.dma("sp", qrT[:], sc["qrT"].t[g * 4:(g + 1) * 4].rearrange("r p n -> p r n"), [sc["qrT"]], [qrT])
                b.dma("sp", kselT[:], sc["kselT"].t[g], [sc["kselT"]], [kselT])
                b.dma("sp", kwinT[:], sc["kwinT"].t[g], [sc["kwinT"]], [kwinT])
                b.dma("sp", kin[:], sc["kcT"].t[g], [sc["kcT"]], [kin])
                b.dma("sp", vin[:], sc["vcT"].t[g], [sc["vcT"]], [vin])
                b.dma("sp", vsel[:, :, 0:128],
                      sc["v"].t[:, g * 128:(g + 1) * 128].rearrange("(kt p) d -> p kt d", p=128), [sc["v"]], [vsel])
                b.dma("sp", vwin[:, :, 0:128],
                      sc["v"].t[:, 256 + g * 128:256 + (g + 1) * 128].rearrange("(kt p) d -> p kt d", p=128),
                      [sc["v"]], [vwin])
                if g == 0:
                    b.dma("pool", w1k[:], I["cmp_w1_k"][l].rearrange("l d h -> d l h"), [], [w1k])
                    b.dma("pool", w1v[:], I["cmp_w1_v"][l].rearrange("l d h -> d l h"), [], [w1v])
                    b.dma("pool", w2k[:], I["cmp_w2_k"][l].rearrange("(c p) d -> p c d", p=128), [], [w2k])
                    b.dma("pool", w2v[:], I["cmp_w2_v"][l].rearrange("(c p) d -> p c d", p=128), [], [w2v])
                    b.dma("pool", posk[:], I["cmp_posT_k"][l], [], [posk])
                    b.dma("pool", posv[:], I["cmp_posT_v"][l], [], [posv])
                    for kv, (w1, pos) in enumerate(((w1k, posk), (w1v, posv))):
                        for hc in range(2):
                            pa = pS[cnt["S"] % 2]
                            cnt["S"] += 1
                            for li in range(32):
                                b.mm(pa[:, 0:1], w1[:, li, hc * 128:(hc + 1) * 128], pos[:, li:li + 1],
                                     li == 0, li == 31, [w1, pos], [pa])
                            b.copy("dve", cbias[:, kv * 2 + hc:kv * 2 + hc + 1], pa[:, 0:1], [pa], [cbias])

                if self.bstop <= 1:
                    break
                for kv, (src, w1, w2, hbuf) in enumerate(((kin, w1k, w2k, hk), (vin, w1v, w2v, hv))):
                    srcv = src[:].rearrange("p (n s) -> p n s", s=16)
                    for hc in range(2):
                        pa = pS[cnt["S"] % 2]
                        cnt["S"] += 1
                        for li in range(32):
                            rhs = srcv[:, 0:255, li] if li < 16 else srcv[:, 1:256, li - 16]
                            b.mm(pa[:, 0:255], w1[:, li, hc * 128:(hc + 1) * 128], rhs, li == 0, li == 31,
                                 [w1, src], [pa])
                        b.act(hbuf[:, hc, 0:255], pa[:, 0:255], AF.Silu, [pa, cbias], [hbuf],
                              bias=cbias[:, kv * 2 + hc:kv * 2 + hc + 1])
                    if kv == 0:
                        pa = pS[cnt["S"] % 2]
                        cnt["S"] += 1
                        for hc in range(2):
                            b.mm(pa[:, 0:255], w2[:, hc, :], hbuf[:, hc, 0:255], hc == 0, hc == 1, [w2, hbuf], [pa])
                        b.copy("dve", kcmp[:, 0:255], pa[:, 0:255], [pa], [kcmp])
                    else:
                        for a in range(2):
                            pa = pS[cnt["S"] % 2]
                            cnt["S"] += 1
                            for hc in range(2):
                                b.mm(pa[:, 0:128], hbuf[:, hc, a * 128:(a + 1) * 128], w2[:, hc, :], hc == 0, hc == 1,
                                     [w2, hbuf], [pa])
                            b.copy("dve", vcx[:, a, 0:128], pa[:, 0:128], [pa], [vcx])

                if self.bstop <= 2:
                    break
                for i in range(NQ):
                    if self.bstop <= 7 and i >= 1:
                        break
                    q0 = i * 512
                    za, gt, A_, B_ = zat[i % 2], glt[i % 2], iA[i % 2], iB[i % 2]
                    ao, ia, sbt = accO[i % 2], impacc[i % 2], selbT[i % 2]

                    def tile_loads(ii):
                        qq = ii * 512
                        b.dma("sp", zat[ii % 2][:],
                              sc["zatt"].t[qq:qq + 512, g * 512:(g + 1) * 512].rearrange("(b p) c -> p b c", p=128),
                              [sc["zatt"]], [zat[ii % 2]])
                        b.dma("sp", glt[ii % 2][:], sc["gl"].t[qq:qq + 512, :].rearrange("(b p) c -> p b c", p=128),
                              [sc["gl"]], [glt[ii % 2]])
                        b.dma("sp", iA[ii % 2][:], I["impA"][qq:qq + 512, :].rearrange("(b p) c -> p b c", p=128),
                              [], [iA[ii % 2]])
                        b.dma("sp", iB[ii % 2][:], I["impB"][qq:qq + 512, :].rearrange("(b p) c -> p b c", p=128),
                              [], [iB[ii % 2]])

                    if i == 0:
                        tile_loads(0)

                    def norm_scales(Oset, width, gcol):
                        d_ = dn[cnt["dn"] % 4]
                        cnt["dn"] += 1
                        for j in range(2):
                            b.ts("dve", d_[:, 2 * j:2 * j + 2], Oset[j][:, :, width - 1], 1e-30, None, ALU.max, None,
                                 [Oset[j]], [d_])
                        P.op("dve", lambda e, d_=d_: e.reciprocal(out=d_[:, 0:4], in_=d_[:, 0:4]), [d_], [d_])
                        b.tt("dve", d_[:, 4:8], d_[:, 0:4], gt[:, :, gcol], ALU.mult, [d_, gt], [d_])
                        return d_

                    if self.bstop <= 2.5:
                        break
                    for r in range(4):
                        Oset = pO[cnt["O"] % 2]
                        cnt["O"] += 1
                        ets = []
                        for a in range(2):
                            u0 = q0 - 2048 * a
                            if u0 + 511 < 31:
                                continue
                            pa = pS[cnt["S"] % 2]
                            cnt["S"] += 1
                            partial = u0 < 2063
                            b.mm(pa[:], kcmp[:, a * 128:(a + 1) * 128], qT[:, r, q0:q0 + 512], True, not partial,
                                 [kcmp, qT], [pa])
                            if partial:
                                b.mm(pa[:], ident[:], tbc[:, u0:u0 + 512], False, True, [ident, tbc], [pa])
                            e_ = pt[cnt["P"] % 3]
                            cnt["P"] += 1
                            b.act(e_[:], pa[:], AF.Exp, [pa], [e_], scale=SCALE)
                            ets.append((a, e_))
                        if self.bstop <= 2.6:
                            continue
                        for bb in range(4):
                            for k_, (a, e_) in enumerate(ets):
                                b.mm(Oset[bb // 2][:, bb % 2, 0:193], e_[:, bb * 128:(bb + 1) * 128], vcx[:, a, :],
                                     k_ == 0, k_ == len(ets) - 1, [e_, vcx], [Oset[bb // 2]], skip_group_check=True)
                        if self.bstop <= 2.7:
                            continue
                        d_ = norm_scales(Oset, 193, (g * 4 + r) * 3 + 0)
                        if self.bstop <= 2.8:
                            continue
                        for bb in range(4):
                            Ob = Oset[bb // 2]
                            if self.bstop == 2.95:
                                pass
                            elif r == 0:
                                b.ts("dve", ia[:, bb, :], Ob[:, bb % 2, 128:192], d_[:, bb:bb + 1], None, ALU.mult, None,
                                     [Ob, d_], [ia])
                            else:
                                b.stt(ia[:, bb, :], Ob[:, bb % 2, 128:192], d_[:, bb:bb + 1], ia[:, bb, :],
                                      ALU.mult, ALU.add, [Ob, d_, ia], [ia])
                            if self.bstop == 2.9:
                                continue
                            b.act(ao[:, bb, r, :], Ob[:, bb % 2, 0:128], AF.Identity, [Ob, d_], [ao],
                                  scale=d_[:, 4 + bb:5 + bb])

                    if i + 1 < NQ:
                        tile_loads(i + 1)
                    if self.bstop <= 3:
                        break
                    for bb in range(4):
                        b.tt("dve", imp2[:], ia[:, bb, :], A_[:, bb, :], ALU.mult, [ia, A_], [imp2])
                        b.tt("dve", imp2[:], imp2[:], B_[:, bb, :], ALU.add, [imp2, B_], [imp2])
                        P.op("dve", lambda e: e.max(out=m8[:, 0:8], in_=imp2[:]), [imp2], [m8])
                        P.op("dve", lambda e: e.match_replace(out=impw[:], in_to_replace=m8[:, 0:8], in_values=imp2[:],
                                                              imm_value=-2.0), [imp2, m8], [impw])
                        P.op("dve", lambda e: e.max(out=m8[:, 8:16], in_=impw[:]), [impw], [m8])
                        b.ts("dve", s01[:], imp2[:], m8[:, 15:16], None, ALU.is_ge, None, [imp2, m8], [s01])
                        b.ts("dve", sb128[:, 0:64], s01[:], -NEGB, NEGB, ALU.mult, ALU.add, [s01], [sb128])
                        ptr_ = pT[cnt["T"] % 2]
                        cnt["T"] += 1
                        b.tr(ptr_[:, 0:128], sb128[:], ident[:], [sb128, ident], [ptr_])
                        b.copy("dve", sbt[:, bb * 128:(bb + 1) * 128], ptr_[:, 0:128], [ptr_], [sbt])

                    if self.bstop <= 4:
                        break
                    def branch(r, kts, rng, masks, kT, vext, sel, gcol):
                        Oset = pO[cnt["O"] % 2]
                        cnt["O"] += 1
                        for j in range(2):
                            b.mm(Oset[j][:].rearrange("p a c -> p (a c)"), zl[:], zr[:], True, False, [zl, zr], [Oset[j]],
                                 skip_group_check=True)
                        last_kt = {}
                        for kt in kts:
                            lo, hi = rng(kt)
                            for bb in range(lo, hi + 1):
                                last_kt[bb] = kt

                        def qk(kt):
                            lo, hi = rng(kt)
                            c0, c1 = lo * 128, (hi + 1) * 128
                            pa = pS[cnt["S"] % 2]
                            cnt["S"] += 1
                            mms = [(pa[:, c0:c1], kT[:, kt * 128:(kt + 1) * 128], qrT[:, r, q0 + c0:q0 + c1], [kT, qrT])]
                            if sel:
                                mms.append((pa[:, c0:c1], blkexp[:, kt * 128:(kt + 1) * 128], sbt[:, c0:c1], [blkexp, sbt]))
                            for (bb, tri) in masks(kt):
                                mms.append((pa[:, bb * 128:(bb + 1) * 128], ident[:], tri[:], [ident, tri]))
                            for n_, (o_, l_, r_, rd) in enumerate(mms):
                                b.mm(o_, l_, r_, n_ == 0, n_ == len(mms) - 1, rd, [pa])
                            return pa, lo, hi

                        pend = qk(kts[0])
                        for n_, kt in enumerate(kts):
                            pa, lo, hi = pend
                            if n_ + 1 < len(kts):
                                pend = qk(kts[n_ + 1])
                            c0, c1 = lo * 128, (hi + 1) * 128
                            p_ = pt[cnt["P"] % 3]
                            cnt["P"] += 1
                            b.act(p_[:, c0:c1], pa[:, c0:c1], AF.Exp, [pa], [p_], scale=SCALE)
                            for bb in range(lo, hi + 1):
                                b.mm(Oset[bb // 2][:, bb % 2, 0:129], p_[:, bb * 128:(bb + 1) * 128], vext[:, kt, :],
                                     False, last_kt[bb] == kt, [p_, vext], [Oset[bb // 2]], skip_group_check=True)
                        d_ = norm_scales(Oset, 129, gcol)
                        for bb in range(4):
                            Ob = Oset[bb // 2]
                            b.stt(ao[:, bb, r, :], Ob[:, bb % 2, 0:128], d_[:, 4 + bb:5 + bb], ao[:, bb, r, :],
                                  ALU.mult, ALU.add, [Ob, d_, ao], [ao])

                    for r in range(4):
                        kts = list(range(max(0, 4 * i - 4), 4 * i + 4))
                        rng = lambda kt: (max(0, kt - 4 * i), min(3, kt - 4 * i + 4))

                        def masks(kt):
                            m = []
                            if 0 <= kt - 4 * i <= 3:
                                m.append((kt - 4 * i, tribc))
                            if 0 <= kt + 4 - 4 * i <= 3:
                                m.append((kt + 4 - 4 * i, tribw))
                            return m
                        branch(r, kts, rng, masks, kwinT, vwin, False, (g * 4 + r) * 3 + 2)
                    if self.bstop <= 5:
                        break
                    for r in range(4):
                        kts = list(range(0, 4 * i + 4))
                        rng = lambda kt: (max(0, kt - 4 * i), 3)
                        masks = lambda kt: [(kt - 4 * i, tribc)] if kt >= 4 * i else []
                        branch(r, kts, rng, masks, kselT, vsel, True, (g * 4 + r) * 3 + 1)

                    if self.bstop <= 6:
                        break
                    for bb in range(4):
                        ab, aT = attb[bb % 2], attT[bb % 2]
                        b.tt("pool", ab[:], ao[:, bb, :, :].rearrange("p r d -> p (r d)"), za[:, bb, :], ALU.mult,
                             [ao, za], [ab])
                        ptr_ = pT[cnt["T"] % 2]
                        cnt["T"] += 1
                        for r in range(4):
                            b.tr(ptr_[:, r * 128:(r + 1) * 128], ab[:, r * 128:(r + 1) * 128], ident[:], [ab, ident], [ptr_])
                        b.evac(aT[:].rearrange("p r q -> p (r q)"), ptr_[:, 0:512], [ptr_], [aT])
                        c0 = q0 + bb * 128
                        b.dma("sp", sc["mixT"].t[g * 512:(g + 1) * 512, c0:c0 + 128].rearrange("(r p) n -> p r n", p=128),
                              aT[:], [aT], [sc["mixT"]])
                if self.bstop <= 8:
                    break
            P.flush()


    def phase_C(self, l):
        nc, P, b, I, sc = self.nc, self.P, self.b, self.I, self.sc
        TWO_PI = 2.0 * math.pi
        with contextlib.ExitStack() as st:
            def sb(name, shape, dt):
                return T(st.enter_context(nc.sbuf_tensor(f"C{l}_{name}", shape, dt)), name)

            def ps(name, shape, dt):
                return T(st.enter_context(nc.psum_tensor(f"C{l}_{name}", shape, dt)), name, True)

            ident = sb("ident", [128, 128], BF16)
            identf = sb("identf", [128, 128], F32)
            swapf = sb("swapf", [128, 128], F32)
            sgn = sb("sgn", [128, 2], F32)
            b.dma("sp", ident[:], I["ident"][:, :], [], [ident])
            b.dma("pool", identf[:], I["ident"][:, :], [], [identf])
            b.dma("pool", swapf[:], I["swapi"][:, :], [], [swapf])
            b.dma("sp", sgn[:], I["sgn"][:, :], [], [sgn])

            ar = sb("ar", [128, 32], F32)
            ai = sb("ai", [128, 32], F32)
            dt_ = sb("dt", [128, 32], F32)
            tA = sb("tA", [128, 32], F32)
            tB = sb("tB", [128, 32], F32)
            tC = sb("tC", [128, 32], F32)
            th = sb("th", [128, 32], F32)
            mag = sb("mag", [128, 32], F32)
            abr = sb("abr", [128, 32], F32)
            abi = sb("abi", [128, 32], F32)
            cr = sb("cr", [128, 32], F32)
            ci = sb("ci", [128, 32], F32)
            for h in range(2):
                b.dma("sp", ar[h * 64:(h + 1) * 64, :], I["s5_a_reT"][l], [], [ar])
                b.dma("sp", ai[h * 64:(h + 1) * 64, :], I["s5_a_imT"][l], [], [ai])
            b.dma("sp", dt_[:], I["s5_log_dt"][l:l + 1, :].to_broadcast([128, 32]), [], [dt_])
            b.act(dt_[:], dt_[:], AF.Exp, [dt_], [dt_])
            b.tt("dve", tA[:], ar[:], dt_[:], ALU.mult, [ar, dt_], [tA])
            b.act(mag[:], tA[:], AF.Exp, [tA], [mag])
            b.tt("dve", th[:], ai[:], dt_[:], ALU.mult, [ai, dt_], [th])

            def sin_of(dst, src, shift):
                b.ts("dve", tB[:], src[:], shift, None, ALU.add, None, [src], [tB])
                b.copy("dve", tC[:], tB[:], [tB], [tC])
                for m in range(1, 9):
                    b.ts("dve", tA[:], tB[:], (2 * m - 1) * math.pi, TWO_PI, ALU.is_gt, ALU.mult, [tB], [tA])
                    b.tt("dve", tC[:], tC[:], tA[:], ALU.subtract, [tC, tA], [tC])
                b.act(dst[:], tC[:], AF.Sin, [tC], [dst])

            sin_of(abi, th, 0.0)
            sin_of(abr, th, 0.5 * math.pi)
            b.tt("dve", abr[:], abr[:], mag[:], ALU.mult, [abr, mag], [abr])
            b.tt("dve", abi[:], abi[:], mag[:], ALU.mult, [abi, mag], [abi])
            b.tt("dve", tA[:], ar[:], ar[:], ALU.mult, [ar], [tA])
            b.tt("dve", tB[:], ai[:], ai[:], ALU.mult, [ai], [tB])
            b.tt("dve", tA[:], tA[:], tB[:], ALU.add, [tA, tB], [tA])
            P.op("dve", lambda e: e.reciprocal(out=tA[:], in_=tA[:]), [tA], [tA])
            b.ts("dve", tB[:], abr[:], -1.0, None, ALU.add, None, [abr], [tB])
            b.tt("dve", cr[:], tB[:], ar[:], ALU.mult, [tB, ar], [cr])
            b.tt("dve", tC[:], abi[:], ai[:], ALU.mult, [abi, ai], [tC])
            b.tt("dve", cr[:], cr[:], tC[:], ALU.add, [cr, tC], [cr])
            b.tt("dve", cr[:], cr[:], tA[:], ALU.mult, [cr, tA], [cr])
            b.tt("dve", ci[:], abi[:], ar[:], ALU.mult, [abi, ar], [ci])
            b.tt("dve", tC[:], tB[:], ai[:], ALU.mult, [tB, ai], [tC])
            b.tt("dve", ci[:], ci[:], tC[:], ALU.subtract, [ci, tC], [ci])
            b.tt("dve", ci[:], ci[:], tA[:], ALU.mult, [ci, tA], [ci])
            b.ts("dve", ci[:], ci[:], sgn[:, 1:2], None, ALU.mult, None, [ci, sgn], [ci])

            pw1 = sb("pw1", [128, 12, 32], F32)
            pw2 = sb("pw2", [128, 12, 32], F32)
            b.copy("dve", pw1[:, 0, :], abr[:], [abr], [pw1])
            b.ts("dve", pw2[:, 0, :], abi[:], sgn[:, 0:1], None, ALU.mult, None, [abi, sgn], [pw2])
            for lv in range(1, 12):
                b.tt("dve", tA[:], pw1[:, lv - 1, :], pw1[:, lv - 1, :], ALU.mult, [pw1], [tA])
                b.tt("dve", tB[:], pw2[:, lv - 1, :], pw2[:, lv - 1, :], ALU.mult, [pw2], [tB])
                b.tt("dve", pw1[:, lv, :], tA[:], tB[:], ALU.subtract, [tA, tB], [pw1])
                b.tt("dve", tC[:], pw1[:, lv - 1, :], pw2[:, lv - 1, :], ALU.mult, [pw1, pw2], [tC])
                b.ts("dve", pw2[:, lv, :], tC[:], 2.0, None, ALU.mult, None, [tC], [pw2])

            bri = sb("bri", [128, 32, 16], F32)
            bir = sb("bir", [128, 32, 16], F32)
            ccs = sb("ccs", [128, 32, 16], F32)
            bbpad = sb("bbpad", [128, 32, 128], BF16)
            ccpad = sb("ccpad", [128, 32, 128], BF16)
            lhsB = sb("lhsB", [128, 32, 128], BF16)
            dcol = sb("dcol", [128, 4], F32)
            glub = sb("glub", [128, 8], F32)
            wg = sb("wg", [128, 4, 1024], BF16)
            b.dma("sp", bri[0:64], I["s5_b_reT"][l], [], [bri])
            b.dma("sp", bri[64:128], I["s5_b_imT"][l], [], [bri])
            b.dma("sp", bir[0:64], I["s5_b_imT"][l], [], [bir])
            b.dma("sp", bir[64:128], I["s5_b_reT"][l], [], [bir])
            b.dma("sp", ccs[0:64], I["s5_c_reT"][l], [], [ccs])
            b.dma("sp", ccs[64:128], I["s5_c_imT"][l], [], [ccs])
            b.dma("sp", dcol[:], I["s5_dT"][l], [], [dcol])
            b.dma("sp", glub[:], I["s5_glu_bT"][l], [], [glub])
            b.dma("pool", wg[:], I["s5_glu_w"][l].rearrange("(j p) e -> p j e", p=128), [], [wg])
            b.memset("pool", bbpad[:], 0.0, [bbpad])
            b.memset("pool", ccpad[:], 0.0, [ccpad])
            for g in range(32):
                c0 = 16 * (g % 8)
                b.ts("dve", bri[:, g, :], bri[:, g, :], cr[:, g:g + 1], None, ALU.mult, None, [bri, cr], [bri])
                b.stt(bbpad[:, g, c0:c0 + 16], bir[:, g, :], ci[:, g:g + 1], bri[:, g, :], ALU.mult, ALU.add,
                      [bir, ci, bri], [bbpad])
            for k in range(8):
                b.ts("dve", ccpad[:, k::8, 16 * k:16 * k + 16], ccs[:, k::8, :], sgn[:, 0:1], None, ALU.mult, None,
                     [ccs, sgn], [ccpad])
            pT = [ps(f"pT{i}", [128, 8, 128], BF16) for i in range(2)]
            for q in range(4):
                p_ = pT[q % 2]
                for j in range(8):
                    b.tr(p_[:, j, :], bbpad[:, q * 8 + j, :], ident[:], [bbpad, ident], [p_])
                b.evac(lhsB[:, q * 8:(q + 1) * 8, :], p_[:], [p_], [lhsB])

            ut = [sb(f"ut{i}", [128, S], BF16) for i in range(2)]
            Hs = [sb(f"H{i}", [128, S], BF16) for i in range(8)]
            Mm = sb("Mm", [128, 8, 12, 128], BF16)
            yT = sb("yT", [128, 4, S], BF16)
            pX = [ps(f"pX{i}", [128, 512], F32) for i in range(4)]
            ya = [sb(f"ya{i}", [128, 512], F32) for i in range(2)]
            yb = [sb(f"yb{i}", [128, 512], F32) for i in range(2)]
            mt1 = [sb(f"mt1{i}", [128, 128], F32) for i in range(3)]
            mt2 = [sb(f"mt2{i}", [128, 128], F32) for i in range(3)]
            nx = [0]

            def nextp():
                p_ = pX[nx[0] % 4]
                nx[0] += 1
                return p_

            for j in range(4):
                u_t = ut[j % 2]
                b.dma("sp", u_t[:], sc["us5T"].t[j * 128:(j + 1) * 128, :], [sc["us5T"]], [u_t])
                for gi in range(8):
                    g = j * 8 + gi
                    for lv in range(12):
                        m1, m2 = mt1[(gi * 12 + lv) % 3], mt2[(gi * 12 + lv) % 3]
                        b.act(m1[:], identf[:], AF.Identity, [identf, pw1], [m1], scale=pw1[:, lv, g:g + 1])
                        b.act(m2[:], swapf[:], AF.Identity, [swapf, pw2], [m2], scale=pw2[:, lv, g:g + 1])
                        b.tt("pool", Mm[:, gi, lv, :], m1[:], m2[:], ALU.add, [m1, m2], [Mm])
                    for tt_ in range(8):
                        p_ = nextp()
                        b.mm(p_[:], lhsB[:, g, :], u_t[:, tt_ * 512:(tt_ + 1) * 512], True, True, [lhsB, u_t], [p_])
                        b.evac(Hs[gi][:, tt_ * 512:(tt_ + 1) * 512], p_[:], [p_], [Hs[gi]])

                def level(lv, down):
                    s_ = 1 << lv
                    n2 = S // (2 * s_)
                    for gi in range(8):
                        Hv = Hs[gi][:].rearrange("p (k s) -> p k s", s=2 * s_)
                        if not down:
                            k0, k1, doff, soff, dk = 0, n2, 2 * s_ - 1, s_ - 1, 0
                        else:
                            k0, k1, doff, soff, dk = 0, n2 - 1, s_ - 1, 2 * s_ - 1, 1
                        kk = k0
                        while kk < k1:
                            ke = min(kk + 512, k1)
                            p_ = nextp()
                            dst = Hv[:, kk + dk:ke + dk, doff]
                            b.mm(p_[:, 0:ke - kk], Mm[:, gi, lv, :], Hv[:, kk:ke, soff], True, False, [Mm, Hs[gi]], [p_])
                            b.mm(p_[:, 0:ke - kk], ident[:], dst, False, True, [ident, Hs[gi]], [p_])
                            b.evac(dst, p_[:, 0:ke - kk], [p_], [Hs[gi]])
                            kk = ke

                for lv in range(12):
                    level(lv, False)
                for lv in range(10, -1, -1):
                    level(lv, True)

                for tt_ in range(8):
                    tsl = slice(tt_ * 512, (tt_ + 1) * 512)
                    p_ = nextp()
                    for gi in range(8):
                        b.mm(p_[:], ccpad[:, j * 8 + gi, :], Hs[gi][:, tsl], gi == 0, gi == 7, [ccpad, Hs[gi]], [p_])
                    a_, b_ = ya[tt_ % 2], yb[tt_ % 2]
                    b.stt(a_[:], u_t[:, tsl], dcol[:, j:j + 1], p_[:], ALU.mult, ALU.add, [u_t, dcol, p_], [a_])
                    b.tt("pool", b_[:], a_[:], a_[:], ALU.mult, [a_], [b_])
                    b.ts("pool", b_[:], b_[:], 0.044715, 1.0, ALU.mult, ALU.add, [b_], [b_])
                    b.tt("pool", b_[:], b_[:], a_[:], ALU.mult, [b_, a_], [b_])
                    b.act(b_[:], b_[:], AF.Sigmoid, [b_], [b_], scale=1.5957691216057308)
                    b.tt("pool", yT[:, j, tsl], a_[:], b_[:], ALU.mult, [a_, b_], [yT])

            zt = [sb(f"zt{i}", [128, 512], BF16) for i in range(2)]
            sg = [sb(f"sg{i}", [128, 512], F32) for i in range(2)]
            og = [sb(f"og{i}", [128, 512], BF16) for i in range(2)]
            n_ = 0
            for c in range(4):
                for tt_ in range(8):
                    tsl = slice(tt_ * 512, (tt_ + 1) * 512)
                    z_, s_g, o_ = zt[n_ % 2], sg[n_ % 2], og[n_ % 2]
                    n_ += 1
                    b.dma("sp", z_[:], sc["zs5T"].t[c * 128:(c + 1) * 128, tsl], [sc["zs5T"]], [z_])
                    pa_, pg_ = nextp(), nextp()
                    for jj in range(4):
                        b.mm(pg_[:], wg[:, jj, 512 + c * 128:512 + (c + 1) * 128], yT[:, jj, tsl], jj == 0, jj == 3,
                             [wg, yT], [pg_])
                    for jj in range(4):
                        b.mm(pa_[:], wg[:, jj, c * 128:(c + 1) * 128], yT[:, jj, tsl], jj == 0, jj == 3, [wg, yT], [pa_])
                    b.act(s_g[:], pg_[:], AF.Sigmoid, [pg_, glub], [s_g], bias=glub[:, 4 + c:5 + c])
                    b.stt(s_g[:], pa_[:], glub[:, c:c + 1], s_g[:], ALU.add, ALU.mult, [pa_, glub, s_g], [s_g])
                    b.tt("pool", o_[:], s_g[:], z_[:], ALU.mult, [s_g, z_], [o_])
                    b.dma("sp", sc["mixT"].t[1024 + c * 128:1024 + (c + 1) * 128, tsl], o_[:], [o_], [sc["mixT"]])
            P.flush()

    def phase_D(self, l):
        nc, P, b, I, sc = self.nc, self.P, self.b, self.I, self.sc
        with contextlib.ExitStack() as st:
            def sb(name, shape, dt):
                return T(st.enter_context(nc.sbuf_tensor(f"D{l}_{name}", shape, dt)), name)

            def ps(name, shape, dt):
                return T(st.enter_context(nc.psum_tensor(f"D{l}_{name}", shape, dt)), name, True)

            pw = sb("pw", [128, 4, 128], BF16)
            pscale = sb("pscale", [128, 4], F32)
            rc = sb("rc", [128, 4, 16], F32)
            b.dma("pool", pw[:], I["pool_w"][l].rearrange("k c d -> c k d"), [], [pw])
            b.dma("sp", pscale[:], I["pool_scaleT"][l], [], [pscale])
            b.dma("sp", rc[:], I["poolrc"][:, :, :], [], [rc])
            u = [sb(f"u{i}", [128, S], F32) for i in range(2)]
            sa = sb("sa", [128, S], F32)
            sbb = sb("sbb", [128, S], F32)
            pl = sb("pl", [128, S], BF16)
            zt = [sb(f"zt{i}", [128, 512], BF16) for i in range(2)]
            og = [sb(f"og{i}", [128, 512], BF16) for i in range(2)]
            pX = [ps(f"pX{i}", [128, 512], F32) for i in range(2)]
            n_ = 0
            for k in range(4):
                w = 2 << k
                u_ = u[k % 2]
                b.dma("sp", u_[:], sc["upoolT"].t[k * 128:(k + 1) * 128, :], [sc["upoolT"]], [u_])
                cur = u_
                bufs = [sa, sbb]
                sh = 1
                step = 0
                while sh < w:
                    nxt = bufs[step % 2]
                    eng = "dve" if step % 2 == 0 else "pool"
                    b.tt(eng, nxt[:, sh:S], cur[:, sh:S], cur[:, 0:S - sh], ALU.add, [cur], [nxt])
                    b.copy("pool" if eng == "dve" else "dve", nxt[:, 0:sh], cur[:, 0:sh], [cur], [nxt])
                    cur = nxt
                    sh *= 2
                    step += 1
                b.tt("dve", pl[:, 0:16], cur[:, 0:16], rc[:, k, :], ALU.mult, [cur, rc], [pl])
                b.tt("dve", pl[:, 0:16], pl[:, 0:16], u_[:, 0:16], ALU.subtract, [pl, u_], [pl])
                b.stt(pl[:, 16:S], cur[:, 16:S], 1.0 / w, u_[:, 16:S], ALU.mult, ALU.subtract, [cur, u_], [pl])
                for tt_ in range(8):
                    tsl = slice(tt_ * 512, (tt_ + 1) * 512)
                    z_, o_ = zt[n_ % 2], og[n_ % 2]
                    p_ = pX[n_ % 2]
                    n_ += 1
                    b.dma("sp", z_[:], sc["zpoolT"].t[k * 128:(k + 1) * 128, tsl], [sc["zpoolT"]], [z_])
                    b.mm(p_[:], pw[:, k, :], pl[:, tsl], True, True, [pw, pl], [p_])
                    b.stt(o_[:], p_[:], pscale[:, k:k + 1], z_[:], ALU.mult, ALU.mult, [p_, pscale, z_], [o_])
                    b.dma("sp", sc["mixT"].t[1536 + k * 128:1536 + (k + 1) * 128, tsl], o_[:], [o_], [sc["mixT"]])
            P.flush()

    def phase_E(self, l, xin, xout):
        nc, P, b, I, sc = self.nc, self.P, self.b, self.I, self.sc
        xin_ap = xin.t if isinstance(xin, T) else xin
        xin_res = [xin] if isinstance(xin, T) else []
        with contextlib.ExitStack() as st:
            def sb(name, shape, dt):
                return T(st.enter_context(nc.sbuf_tensor(f"E{l}_{name}", shape, dt)), name)

            def ps(name, shape, dt):
                return T(st.enter_context(nc.psum_tensor(f"E{l}_{name}", shape, dt)), name, True)

            wo = sb("wo", [128, 16, D], BF16)
            gb = sb("gb", [128, D], F32)
            mx = [sb(f"mx{i}", [128, 16, 128], BF16) for i in range(3)]
            xt = [sb(f"xt{i}", [128, D], F32) for i in range(3)]
            ot = [sb(f"ot{i}", [128, D], F32) for i in range(3)]
            sq = sb("sq", [128, D], BF16)
            st1 = [sb(f"st{i}", [128, 4], F32) for i in range(3)]
            pacc = [ps(f"pacc{i}", [128, 512], F32) for i in range(8)]
            for kq in range(4):
                b.dma("pool", wo[:, kq * 4:(kq + 1) * 4, :],
                      I["w_out"][l, kq * 512:(kq + 1) * 512, :].rearrange("(kc p) n -> p kc n", p=128),
                      [], [wo])
            b.dma("sp", gb[:], I["norm_post"][l:l + 1, :].to_broadcast([128, D]), [], [gb])
            def loads(t):
                tsl = slice(t * 128, (t + 1) * 128)
                b.dma("sp", mx[t % 3][:], sc["mixT"].t[:, tsl].rearrange("(kc p) n -> p kc n", p=128),
                      [sc["mixT"]], [mx[t % 3]])
                b.dma("sp", xt[t % 3][:], xin_ap[tsl, :], xin_res, [xt[t % 3]])

            loads(0)
            loads(1)
            for t in range(NT):
                tsl = slice(t * 128, (t + 1) * 128)
                m_t, x_t, o_t, s_t = mx[t % 3], xt[t % 3], ot[t % 3], st1[t % 3]
                if t + 2 < NT:
                    loads(t + 2)
                for n in range(4):
                    pa = pacc[(t % 2) * 4 + n]
                    for kc in range(16):
                        b.mm(pa[:], m_t[:, kc, :], wo[:, kc, n * 512:(n + 1) * 512], kc == 0, kc == 15,
                             [m_t, wo], [pa])
                    b.evac(o_t[:, n * 512:(n + 1) * 512], pa[:], [pa], [o_t])
                b.act(sq[:], o_t[:], AF.Square, [o_t], [sq, s_t], accum_out=s_t[:, 0:1])
                b.ts("dve", s_t[:, 1:2], s_t[:, 0:1], 1.0 / D, 1e-6, ALU.mult, ALU.add, [s_t], [s_t])
                b.act(s_t[:, 2:3], s_t[:, 1:2], AF.Sqrt, [s_t], [s_t])
                P.op("dve", lambda e, s_t=s_t: e.reciprocal(out=s_t[:, 3:4], in_=s_t[:, 2:3]), [s_t], [s_t])
                b.stt(o_t[:], o_t[:], s_t[:, 3:4], gb[:], ALU.mult, ALU.mult, [o_t, s_t, gb], [o_t])
                b.tt("pool", o_t[:], o_t[:], x_t[:], ALU.add, [o_t, x_t], [o_t])
                b.dma("sp", xout.t[tsl, :], o_t[:], [o_t], [xout])
            P.flush()


def _host_inputs(inputs):
    perm = _perm_cols()
    w_in = np.asarray(inputs["w_in"])
    wp = np.zeros((DEPTH, D, WCOLS), np.float32)
    ok = perm >= 0
    wp[:, :, ok] = w_in[:, :, perm[ok]]
    shared = dict(_consts())
    shared["w_in"] = wp
    shared["norm_pre"] = np.ascontiguousarray(inputs["norm_pre"], dtype=np.float32)
    shared["norm_post"] = np.ascontiguousarray(inputs["norm_post"], dtype=np.float32)
    shared["w_out"] = np.ascontiguousarray(inputs["w_out"], dtype=np.float32)
    for kv in ("k", "v"):
        shared[f"cmp_w1_{kv}"] = np.ascontiguousarray(inputs[f"cmp_w1_{kv}"], dtype=np.float32)
        shared[f"cmp_w2_{kv}"] = np.ascontiguousarray(inputs[f"cmp_w2_{kv}"], dtype=np.float32)
        shared[f"cmp_posT_{kv}"] = np.ascontiguousarray(
            np.asarray(inputs[f"cmp_pos_{kv}"], dtype=np.float32).transpose(0, 2, 1))
    f32 = lambda a: np.ascontiguousarray(np.asarray(a, dtype=np.float32))
    shared["s5_a_reT"] = f32(np.asarray(inputs["s5_a_re"]).transpose(0, 2, 1))
    shared["s5_a_imT"] = f32(np.asarray(inputs["s5_a_im"]).transpose(0, 2, 1))
    shared["s5_log_dt"] = f32(inputs["s5_log_dt"])
    shared["s5_b_reT"] = f32(np.asarray(inputs["s5_b_re"]).transpose(0, 2, 1, 3))
    shared["s5_b_imT"] = f32(np.asarray(inputs["s5_b_im"]).transpose(0, 2, 1, 3))
    shared["s5_c_reT"] = f32(np.asarray(inputs["s5_c_re"]).transpose(0, 3, 1, 2))
    shared["s5_c_imT"] = f32(np.asarray(inputs["s5_c_im"]).transpose(0, 3, 1, 2))
    shared["s5_dT"] = f32(np.asarray(inputs["s5_d"]).reshape(DEPTH, 4, 128).transpose(0, 2, 1))
    shared["s5_glu_bT"] = f32(np.asarray(inputs["s5_glu_b"]).reshape(DEPTH, 8, 128).transpose(0, 2, 1))
    shared["s5_glu_w"] = f32(inputs["s5_glu_w"])
    shared["pool_w"] = f32(inputs["pool_w"])
    shared["pool_scaleT"] = f32(np.asarray(inputs["pool_scale"]).reshape(DEPTH, 4, 128).transpose(0, 2, 1))
    return shared


def kernel(**inputs):
    x = np.asarray(inputs["x"], dtype=np.float32)
    shared = _host_inputs(inputs)
    k = Kern()
    nc = k.build()
    nb = x.shape[0]
    in_maps = []
    for bi in range(nb):
        m = dict(shared)
        m["x"] = np.ascontiguousarray(x[bi])
        in_maps.append(m)
    res = run_bass_kernel_spmd(nc, in_maps, core_ids=list(range(nb)))
    return np.stack([np.asarray(r["y"]) for r in res.results], 0).astype(np.float32)
```

```python
import contextlib
import math
import numpy as np
import ml_dtypes
import concourse.bass as bass
import concourse.mybir as mybir
from concourse.bass_utils import run_bass_kernel_spmd

F32 = mybir.dt.float32
BF16 = mybir.dt.bfloat16
AF = mybir.ActivationFunctionType
ALU = mybir.AluOpType
AX = mybir.AxisListType

EPOCH = 7990
NDMASEM = 16
NPOOLSEM = 8

S = 4096
D = 2048
NT = S // 128
NQ = S // 512
DEPTH = 2
INW = 5656
HD = 128
SCALE = HD ** -0.5
NEGB = -30000.0


class Res:
    __slots__ = ("name", "lw", "rd", "excl")

    def __init__(self, name="", excl=False):
        self.name = name
        self.lw = None
        self.rd = {}
        self.excl = excl


class T:
    def __init__(self, t, name="", excl=False):
        self.t = t
        self.r = Res(name, excl)

    def __getitem__(self, k):
        return self.t[k]


def _res(x):
    return x.r if isinstance(x, T) else x


class Prog:
    ENGS = ("pe", "act", "dve", "pool", "sp")

    def __init__(self, nc, stack):
        self.nc = nc
        self.stack = stack
        self.ops = {e: [] for e in self.ENGS}
        self.sems = {e: [stack.enter_context(nc.semaphore(f"s_{e}_0"))] for e in self.ENGS}
        self.cnt = {e: 0 for e in self.ENGS}
        self.seen = {e: {} for e in self.ENGS}
        self.dsems = {"sp": [stack.enter_context(nc.semaphore(f"s_dma_{i}")) for i in range(NDMASEM)],
                      "pool": [stack.enter_context(nc.semaphore(f"s_dmap_{i}")) for i in range(NPOOLSEM)]}
        self.ndma = {"sp": 0, "pool": 0}
        self.strict = True
        self.nops = 0

    def _deps(self, reads, writes):
        deps = []
        for r in reads:
            r = _res(r)
            if r.lw is not None:
                deps.append(r.lw)
        for w in writes:
            w = _res(w)
            if w.lw is not None:
                deps.append(w.lw)
            deps.extend(w.rd.values())
        return deps

    def _waits(self, eng, deps):
        waits = {}
        seen = self.seen[eng]
        for (sem, val, src) in deps:
            if src == eng and (eng in ("pe", "sp") or not self.strict):
                continue
            k = id(sem)
            if seen.get(k, 0) >= val:
                continue
            if k not in waits or waits[k][1] < val:
                waits[k] = (sem, val)
        for k, (sem, val) in waits.items():
            seen[k] = val
        return list(waits.values())

    def _record(self, ev, reads, writes):
        key = id(ev[0])
        for r in reads:
            _res(r).rd[key] = ev
        for w in writes:
            w = _res(w)
            w.lw = ev
            w.rd = {}

    @staticmethod
    def _split(reads, writes):
        rd, wr = [], list(writes)
        for r in reads:
            if _res(r).excl:
                wr.append(r)
            else:
                rd.append(r)
        return rd, wr

    def op(self, eng, fn, reads=(), writes=()):
        reads, writes = self._split(reads, writes)
        deps = self._deps(reads, writes)
        waits = self._waits(eng, deps)
        if self.cnt[eng] >= EPOCH:
            self.sems[eng].append(
                self.stack.enter_context(self.nc.semaphore(f"s_{eng}_{len(self.sems[eng])}")))
            self.cnt[eng] = 0
        sem = self.sems[eng][-1]
        self.cnt[eng] += 1
        ev = (sem, self.cnt[eng], eng)
        self.ops[eng].append((waits, fn, sem, 1))
        self._record(ev, reads, writes)
        self.nops += 1
        return ev

    def dma(self, q, out, in_, reads=(), writes=(), **kw):
        reads, writes = self._split(reads, writes)
        deps = self._deps(reads, writes)
        i = self.ndma[q]
        self.ndma[q] += 1
        nsem = len(self.dsems[q])
        sem = self.dsems[q][i % nsem]
        tgt = 16 * (i // nsem + 1)
        if i >= nsem:
            deps.append((sem, tgt - 16, None))
        waits = self._waits(q, deps)
        fn = lambda e, out=out, in_=in_, kw=kw: e.dma_start(out=out, in_=in_, **kw)
        self.ops[q].append((waits, fn, sem, 16))
        ev = (sem, tgt, None)
        self._record(ev, reads, writes)
        self.nops += 1
        return ev

    def flush(self):
        deps = []
        for q in ("sp", "pool"):
            nsem = len(self.dsems[q])
            for j, sem in enumerate(self.dsems[q]):
                n = (self.ndma[q] - j + nsem - 1) // nsem if self.ndma[q] > j else 0
                if n > 0:
                    deps.append((sem, 16 * n, None))
        waits = self._waits("sp", deps)
        self.ops["sp"].append((waits, None, None, 0))
        nc = self.nc
        ops = self.ops
        self.ops = {e: [] for e in self.ENGS}
        with nc.Block() as block:
            def run(e, lst):
                for (waits, fn, sem, inc) in lst:
                    for (s, v) in waits:
                        e.wait_ge(s, v)
                    if fn is not None:
                        fn(e).then_inc(sem, inc)

            @block.tensor
            def _(e):
                run(e, ops["pe"])

            @block.scalar
            def _(e):
                run(e, ops["act"])

            @block.vector
            def _(e):
                run(e, ops["dve"])

            @block.gpsimd
            def _(e):
                run(e, ops["pool"])

            @block.sync
            def _(e):
                run(e, ops["sp"])


class B:
    def __init__(self, nc, P):
        self.nc = nc
        self.P = P
        self.flip = 0

    def act(self, out, in_, func, reads, writes, **kw):
        self.P.op("act", lambda e: e.activation(out=out, in_=in_, func=func, **kw), reads, writes)

    def mm(self, out, lhsT, rhs, start, stop, reads, writes, **kw):
        self.P.op("pe", lambda e: e.matmul(out, lhsT=lhsT, rhs=rhs, start=start, stop=stop, **kw),
                  reads, writes)

    def memset(self, eng, ap, val, writes):
        self.P.op(eng, lambda e: e.memset(ap, val), [], writes)

    def tr(self, out, in_, ident, reads, writes):
        self.P.op("pe", lambda e: e.transpose(out=out, in_=in_, identity=ident), reads, writes)

    def tt(self, eng, out, in0, in1, op, reads, writes):
        self.P.op(eng, lambda e: e.tensor_tensor(out=out, in0=in0, in1=in1, op=op), reads, writes)

    def ts(self, eng, out, in0, s1, s2, op0, op1, reads, writes):
        if op1 is None:
            self.P.op(eng, lambda e: e.tensor_scalar(out=out, in0=in0, scalar1=s1, scalar2=None, op0=op0),
                      reads, writes)
        else:
            self.P.op(eng, lambda e: e.tensor_scalar(out=out, in0=in0, scalar1=s1, scalar2=s2,
                                                      op0=op0, op1=op1), reads, writes)

    def stt(self, out, in0, scalar, in1, op0, op1, reads, writes):
        self.P.op("dve", lambda e: e.scalar_tensor_tensor(out=out, in0=in0, scalar=scalar, in1=in1,
                                                           op0=op0, op1=op1), reads, writes)

    def copy(self, eng, out, in_, reads, writes):
        if eng == "act":
            self.act(out, in_, AF.Copy, reads, writes)
        else:
            self.P.op(eng, lambda e: e.tensor_copy(out=out, in_=in_), reads, writes)

    def evac(self, out, in_, reads, writes):
        self.flip ^= 1
        self.copy("act" if self.flip else "dve", out, in_, reads, writes)

    def dma(self, q, out, in_, reads, writes, **kw):
        self.P.dma(q, out, in_, reads, writes, **kw)


OFF = {"q": 0, "kc": 1024, "vc": 1280, "ksl": 1536, "vsl": 1792, "kwn": 2048, "vwn": 2304,
       "gl": 2560, "zatt": 2584, "us5": 3608, "zs5": 4120, "upool": 4632, "zpool": 5144}
NWG = 12
WCOLS = 11 * 512 + 32


def _perm_cols():
    fm = []
    for h in range(8):
        fm.append(OFF["q"] + 128 * h)
    for nm in ("kc", "vc", "ksl", "kwn"):
        fm += [OFF[nm], OFF[nm] + 128]
    for nm in ("us5", "zs5", "upool", "zpool"):
        fm += [OFF[nm] + 128 * i for i in range(4)]
    cols = []
    for c in fm:
        cols += list(range(c, c + 128))
    cols += list(range(OFF["vsl"], OFF["vsl"] + 256)) + list(range(OFF["vwn"], OFF["vwn"] + 256))
    cols += list(range(OFF["zatt"], OFF["zatt"] + 1024))
    cols += list(range(OFF["gl"], OFF["gl"] + 24)) + [-1] * 8
    return np.array(cols)


def _consts():
    c = {}
    c["ident"] = np.eye(128, dtype=np.float32).astype(ml_dtypes.bfloat16)
    sw = np.zeros((128, 128), np.float32)
    for m in range(128):
        sw[(m + 64) % 128, m] = 1.0
    c["swapi"] = sw.astype(ml_dtypes.bfloat16)
    half = 64
    inv = 10000.0 ** (-np.arange(half, dtype=np.float32) / half)
    ang = np.arange(S, dtype=np.float32)[:, None] * inv[None, :]
    cos = np.cos(ang).astype(np.float32).T
    sin = np.sin(ang).astype(np.float32).T
    c["ropec"] = np.ascontiguousarray(np.concatenate([cos, cos], 0))
    c["ropes"] = np.ascontiguousarray(np.concatenate([-sin, sin], 0))
    bf = ml_dtypes.bfloat16
    p = np.arange(128)
    c["tribc"] = np.where(p[:, None] > p[None, :], NEGB, 0.0).astype(bf)
    c["tribw"] = np.where(p[:, None] <= p[None, :], NEGB, 0.0).astype(bf)
    u = np.arange(2560)
    c["tbc"] = np.where(16 * p[:, None] + 31 > u[None, :], NEGB, 0.0).astype(bf)
    c["blkexp"] = (np.arange(S)[None, :] // 64 == np.arange(64)[:, None]).astype(np.float32).astype(bf)
    n = np.arange(256)
    cs = n * 16
    js = np.arange(64) * 64
    ov = ((cs[:, None] < js[None, :] + 64) & (cs[:, None] + 32 > js[None, :])).astype(np.float32)
    ov = np.concatenate([ov, np.ones((256, 1), np.float32)], 1)
    ov[255] = 0.0
    c["ovl1"] = np.ascontiguousarray(ov.reshape(2, 128, 65).transpose(1, 0, 2)).astype(bf)
    q = np.arange(S)
    cur = q // 64
    j = np.arange(64)
    forced = (j[None, :] == cur[:, None]) | (j[None, :] == 0)
    causal = j[None, :] <= cur[:, None]
    sg = np.ones((128, 2), np.float32)
    sg[64:, 0] = -1.0
    sg[:64, 1] = -1.0
    c["sgn"] = sg
    t16 = np.arange(16)
    rc = np.stack([1.0 / np.minimum(t16 + 1, w) for w in (2, 4, 8, 16)], 0).astype(np.float32)
    c["poolrc"] = np.ascontiguousarray(np.broadcast_to(rc[None], (128, 4, 16))).astype(np.float32)
    tm = lambda a: np.ascontiguousarray(a.reshape(NQ, 4, 128, 64).transpose(0, 2, 1, 3))
    c["impA"] = tm((causal & ~forced).astype(np.float32))
    c["impB"] = tm(np.where(forced, 1e4, np.where(causal, 0.0, -1.0)).astype(np.float32))
    return c


class Kern:
    def __init__(self, debug=(), nlayers=DEPTH, phases=None, bstop=99):
        self.bstop = bstop
        self.debug = set(debug)
        self.nlayers = nlayers
        self.phases = phases
        self.nc = bass.Bass("TRN2", target_bir_lowering=False)
        self.gst = contextlib.ExitStack()
        self.P = None

    def din(self, name, shape, dt=F32):
        return self.nc.dram_tensor(name, list(shape), dt, kind="ExternalInput").ap()

    def dscr(self, name, shape, dt):
        kind = "ExternalOutput" if name in self.debug else "Internal"
        t = self.nc.dram_tensor(name, list(shape), dt, kind=kind).ap()
        return T(t, name)

    def build(self):
        nc = self.nc
        with self.gst as gst:
            self.P = Prog(nc, gst)
            self.b = B(nc, self.P)
            I = self.I = {}
            I["x"] = self.din("x", [S, D])
            I["norm_pre"] = self.din("norm_pre", [DEPTH, D])
            I["norm_post"] = self.din("norm_post", [DEPTH, D])
            I["w_in"] = self.din("w_in", [DEPTH, D, WCOLS])
            I["w_out"] = self.din("w_out", [DEPTH, D, D])
            I["ident"] = self.din("ident", [128, 128], BF16)
            I["swapi"] = self.din("swapi", [128, 128], BF16)
            I["ropec"] = self.din("ropec", [128, S])
            I["ropes"] = self.din("ropes", [128, S])
            for nm in ("tribc", "tribw"):
                I[nm] = self.din(nm, [128, 128], BF16)
            I["tbc"] = self.din("tbc", [128, 2560], BF16)
            I["blkexp"] = self.din("blkexp", [64, S], BF16)
            I["ovl1"] = self.din("ovl1", [128, 2, 65], BF16)
            I["impA"] = self.din("impA", [NQ, 128, 4, 64])
            I["impB"] = self.din("impB", [NQ, 128, 4, 64])
            for kv in ("k", "v"):
                I[f"cmp_w1_{kv}"] = self.din(f"cmp_w1_{kv}", [DEPTH, 128, 32, 256])
                I[f"cmp_w2_{kv}"] = self.din(f"cmp_w2_{kv}", [DEPTH, 256, 128])
                I[f"cmp_posT_{kv}"] = self.din(f"cmp_posT_{kv}", [DEPTH, 128, 32])
            I["sgn"] = self.din("sgn", [128, 2])
            I["s5_a_reT"] = self.din("s5_a_reT", [DEPTH, 64, 32])
            I["s5_a_imT"] = self.din("s5_a_imT", [DEPTH, 64, 32])
            I["s5_log_dt"] = self.din("s5_log_dt", [DEPTH, 32])
            for nm in ("s5_b_reT", "s5_b_imT", "s5_c_reT", "s5_c_imT"):
                I[nm] = self.din(nm, [DEPTH, 64, 32, 16])
            I["s5_dT"] = self.din("s5_dT", [DEPTH, 128, 4])
            I["s5_glu_bT"] = self.din("s5_glu_bT", [DEPTH, 128, 8])
            I["s5_glu_w"] = self.din("s5_glu_w", [DEPTH, 512, 1024])
            I["pool_w"] = self.din("pool_w", [DEPTH, 4, 128, 128])
            I["pool_scaleT"] = self.din("pool_scaleT", [DEPTH, 128, 4])
            I["poolrc"] = self.din("poolrc", [128, 4, 16])
            self.y = T(nc.dram_tensor("y", [S, D], F32, kind="ExternalOutput").ap(), "y")
            sc = self.sc = {}
            sc["qT"] = self.dscr("qT", [8, 128, S], BF16)
            sc["qrT"] = self.dscr("qrT", [8, 128, S], BF16)
            sc["kcT"] = self.dscr("kcT", [2, 128, S], BF16)
            sc["vcT"] = self.dscr("vcT", [2, 128, S], BF16)
            sc["kselT"] = self.dscr("kselT", [2, 128, S], BF16)
            sc["kwinT"] = self.dscr("kwinT", [2, 128, S], BF16)
            sc["us5T"] = self.dscr("us5T", [512, S], BF16)
            sc["zs5T"] = self.dscr("zs5T", [512, S], BF16)
            sc["upoolT"] = self.dscr("upoolT", [512, S], F32)
            sc["zpoolT"] = self.dscr("zpoolT", [512, S], BF16)
            sc["v"] = self.dscr("v", [4, 128, NT, 128], BF16)
            sc["zatt"] = self.dscr("zatt", [S, 1024], BF16)
            sc["gl"] = self.dscr("gl", [S, 32], F32)
            sc["mixT"] = self.dscr("mixT", [NT, 128, 16, 128], BF16)
            sc["x1"] = self.dscr("x1", [S, D], F32)
            for l in range(self.nlayers):
                xin = I["x"] if l == 0 else sc["x1"]
                xout = self.y if l == self.nlayers - 1 else sc["x1"]
                if self.phases is None or "A" in self.phases:
                    self.phase_A(l, xin)
                if self.phases is None or "B" in self.phases:
                    self.phase_B(l)
                if self.phases is None or "C" in self.phases:
                    self.phase_C(l)
                if self.phases is None or "D" in self.phases:
                    self.phase_D(l)
                if self.phases is None or "E" in self.phases:
                    self.phase_E(l, xin, xout)
        return nc

    def phase_A(self, l, xin):
        nc, P, b, I, sc = self.nc, self.P, self.b, self.I, self.sc
        xin_ap = xin.t if isinstance(xin, T) else xin
        xin_res = [xin] if isinstance(xin, T) else []
        with contextlib.ExitStack() as st:
            def sb(name, shape, dt):
                return T(st.enter_context(nc.sbuf_tensor(f"A{l}_{name}", shape, dt)), name)

            def ps(name, shape, dt):
                return T(st.enter_context(nc.psum_tensor(f"A{l}_{name}", shape, dt)), name, True)

            hT = sb("hT", [128, 16, S], BF16)
            ident = sb("ident", [128, 128], BF16)
            swapi = sb("swapi", [128, 128], BF16)
            st_outer = st
            st = st_outer.enter_context(contextlib.ExitStack())
            gb = sb("gb", [128, D], F32)
            xt = [sb(f"xt{i}", [128, D], F32) for i in range(3)]
            sq = sb("sq", [128, D], BF16)
            hb = [sb(f"hb{i}", [128, D], BF16) for i in range(3)]
            st1 = [sb(f"st{i}", [128, 4], F32) for i in range(3)]
            ptr = [ps(f"ptr{i}", [128, 8, 128], BF16) for i in range(4)]

            b.dma("sp", ident[:], I["ident"][:, :], [], [ident])
            b.dma("sp", swapi[:], I["swapi"][:, :], [], [swapi])
            b.dma("sp", gb[:], I["norm_pre"][l:l + 1, :].to_broadcast([128, D]), [], [gb])

            for t in range(NT):
                x_t, h_t, s_t = xt[t % 3], hb[t % 3], st1[t % 3]
                b.dma("sp", x_t[:], xin_ap[t * 128:(t + 1) * 128, :], xin_res, [x_t])
                b.act(sq[:], x_t[:], AF.Square, [x_t], [sq, s_t], accum_out=s_t[:, 0:1])
                b.ts("dve", s_t[:, 1:2], s_t[:, 0:1], 1.0 / D, 1e-6, ALU.mult, ALU.add, [s_t], [s_t])
                b.act(s_t[:, 2:3], s_t[:, 1:2], AF.Sqrt, [s_t], [s_t])
                P.op("dve", lambda e, s_t=s_t: e.reciprocal(out=s_t[:, 3:4], in_=s_t[:, 2:3]), [s_t], [s_t])
                b.stt(h_t[:], x_t[:], s_t[:, 3:4], gb[:], ALU.mult, ALU.mult, [x_t, s_t, gb], [h_t])
                for half in range(2):
                    p_t = ptr[(2 * t + half) % 4]
                    for j in range(8):
                        kc = half * 8 + j
                        b.tr(p_t[:, j, :], h_t[:, kc * 128:(kc + 1) * 128], ident[:], [h_t, ident], [p_t])
                    b.evac(hT[:, half * 8:(half + 1) * 8, t * 128:(t + 1) * 128], p_t[:], [p_t], [hT])

            P.flush()
            st.close()
            st = st_outer
            wb = [sb(f"wb{i}", [128, 16, 512], BF16) for i in range(2)]
            rope = [sb(f"rope{i}", [128, 2, 512], F32) for i in range(2)]
            pacc = [ps(f"pacc{i}", [128, 512], F32) for i in range(3)]
            prot = [ps(f"prot{i}", [128, 512], F32) for i in range(2)]
            qun = [sb(f"qun{i}", [128, 512], BF16) for i in range(3)]
            t1 = [sb(f"t1{i}", [128, 512], F32) for i in range(2)]
            t2 = [sb(f"t2{i}", [128, 512], F32) for i in range(2)]
            qro = [sb(f"qro{i}", [128, 512], BF16) for i in range(2)]
            o32 = [sb(f"o32{i}", [128, 512], F32) for i in range(2)]
            ogl = [sb(f"ogl{i}", [128, 32], F32) for i in range(2)]
            win = I["w_in"]
            nacc = [0]
            nrot = [0]

            def load_w(gi):
                w_t = wb[gi % 2]
                ncol = 512 if gi < 11 else 32
                src = win[l, :, gi * 512:gi * 512 + ncol].rearrange("(kc p) n -> p kc n", p=128)
                b.dma("pool", w_t[:, :, 0:ncol], src, [], [w_t])

            load_w(0)
            for gi in range(NWG):
                if gi + 1 < NWG:
                    load_w(gi + 1)
                w_t = wb[gi % 2]
                if gi < 8:
                    do_rope = gi in (0, 1, 3)
                    for tq in range(NQ):
                        tsl = slice(tq * 512, (tq + 1) * 512)
                        if do_rope:
                            rp = rope[tq % 2]
                            b.dma("sp", rp[:, 0, :], I["ropec"][:, tsl], [], [rp])
                            b.dma("sp", rp[:, 1, :], I["ropes"][:, tsl], [], [rp])
                        for c in range(4):
                            pa = pacc[nacc[0] % 3]
                            nacc[0] += 1
                            for kc in range(16):
                                b.mm(pa[:], w_t[:, kc, c * 128:(c + 1) * 128], hT[:, kc, tsl],
                                     kc == 0, kc == 15, [w_t, hT], [pa])
                            ch = gi * 4 + c
                            if gi in (0, 1):
                                qu = qun[nacc[0] % 3]
                                b.act(qu[:], pa[:], AF.Copy, [pa], [qu])
                                b.dma("sp", sc["qT"][ch, :, tsl], qu[:], [qu], [sc["qT"]])
                                self._rope(b, qu, rp, swapi, prot, t1, t2, qro, nrot,
                                           sc["qrT"], sc["qrT"][ch, :, tsl])
                            elif gi == 2:
                                qu = qun[nacc[0] % 3]
                                b.act(qu[:], pa[:], AF.Copy, [pa], [qu])
                                dst = sc["kcT"] if c < 2 else sc["vcT"]
                                b.dma("sp", dst[c % 2, :, tsl], qu[:], [qu], [dst])
                            elif gi == 3:
                                qu = qun[nacc[0] % 3]
                                b.act(qu[:], pa[:], AF.Copy, [pa], [qu])
                                dst = sc["kselT"] if c < 2 else sc["kwinT"]
                                self._rope(b, qu, rp, swapi, prot, t1, t2, qro, nrot, dst, dst[c % 2, :, tsl])
                            elif gi in (4, 5, 7):
                                qu = qun[nacc[0] % 3]
                                dst = {4: sc["us5T"], 5: sc["zs5T"], 7: sc["zpoolT"]}[gi]
                                b.act(qu[:], pa[:], AF.Copy if gi == 4 else AF.Silu, [pa], [qu])
                                b.dma("sp", dst[c * 128:(c + 1) * 128, tsl], qu[:], [qu], [dst])
                            else:
                                o = o32[nacc[0] % 2]
                                b.act(o[:], pa[:], AF.Copy, [pa], [o])
                                b.dma("sp", sc["upoolT"][c * 128:(c + 1) * 128, tsl], o[:], [o], [sc["upoolT"]])
                else:
                    ncol = 512 if gi < 11 else 32
                    for t in range(NT):
                        tsl = slice(t * 128, (t + 1) * 128)
                        pa = pacc[nacc[0] % 3]
                        nacc[0] += 1
                        for kc in range(16):
                            b.mm(pa[:, 0:ncol], hT[:, kc, tsl], w_t[:, kc, 0:ncol], kc == 0, kc == 15,
                                 [w_t, hT], [pa])
                        if gi == 8:
                            qu = qun[nacc[0] % 3]
                            b.act(qu[:], pa[:], AF.Copy, [pa], [qu])
                            for c4 in range(4):
                                b.dma("sp", sc["v"].t[c4, :, t, :], qu[:, c4 * 128:(c4 + 1) * 128], [qu], [sc["v"]])
                        elif gi in (9, 10):
                            qu = qun[nacc[0] % 3]
                            b.act(qu[:], pa[:], AF.Silu, [pa], [qu])
                            b.dma("sp", sc["zatt"][tsl, (gi - 9) * 512:(gi - 8) * 512], qu[:], [qu], [sc["zatt"]])
                        else:
                            o = ogl[nacc[0] % 2]
                            b.act(o[:], pa[:, 0:32], AF.Sigmoid, [pa], [o])
                            b.dma("sp", sc["gl"][tsl, :], o[:], [o], [sc["gl"]])
            P.flush()

    def _rope(self, b, qu, rp, swapi, prot, t1, t2, qro, nrot, dst_res, dst_ap):
        i = nrot[0]
        nrot[0] += 1
        pr, a1, a2, qo = prot[i % 2], t1[i % 2], t2[i % 2], qro[i % 2]
        b.mm(pr[:], swapi[:], qu[:], True, True, [swapi, qu], [pr])
        b.tt("dve", a1[:], pr[:], rp[:, 1, :], ALU.mult, [pr, rp], [a1])
        b.tt("pool", a2[:], qu[:], rp[:, 0, :], ALU.mult, [qu, rp], [a2])
        b.tt("dve", qo[:], a1[:], a2[:], ALU.add, [a1, a2], [qo])
        b.dma("sp", dst_ap, qo[:], [qo], [dst_res])


    def phase_B(self, l):
        nc, P, b, I, sc = self.nc, self.P, self.b, self.I, self.sc
        with contextlib.ExitStack() as st:
            def sb(name, shape, dt):
                return T(st.enter_context(nc.sbuf_tensor(f"B{l}_{name}", shape, dt)), name)

            def ps(name, shape, dt):
                return T(st.enter_context(nc.psum_tensor(f"B{l}_{name}", shape, dt)), name, True)

            ident = sb("ident", [128, 128], BF16)
            tribc = sb("tribc", [128, 128], BF16)
            tribw = sb("tribw", [128, 128], BF16)
            tbc = sb("tbc", [128, 2560], BF16)
            blkexp = sb("blkexp", [128, S], BF16)
            zl = sb("zl", [128, 128], BF16)
            zr = sb("zr", [128, 512], BF16)
            for (t_, nm) in ((ident, "ident"), (tribc, "tribc"), (tribw, "tribw"), (tbc, "tbc")):
                b.dma("sp", t_[:], I[nm][:, :], [], [t_])
            b.memset("pool", blkexp[:], 0.0, [blkexp])
            b.dma("sp", blkexp[0:64, :], I["blkexp"][:, :], [], [blkexp])
            b.memset("pool", zl[:], 0.0, [zl])
            b.memset("pool", zr[:], 0.0, [zr])

            NS_ = 3
            pS = [ps(f"pS{i}", [128, 512], F32) for i in range(NS_)]
            pO = [[ps(f"pO{a}{j}", [128, 2, 256], F32) for j in range(2)] for a in range(2)]
            pT = [ps(f"pT{i}", [128, 1024], BF16) for i in range(1)]

            qT = sb("qT", [128, 4, S], BF16)
            qrT = sb("qrT", [128, 4, S], BF16)
            kselT = sb("kselT", [128, S], BF16)
            kwinT = sb("kwinT", [128, S], BF16)
            vsel = sb("vsel", [128, NT, 129], BF16)
            vwin = sb("vwin", [128, NT, 129], BF16)
            kin = sb("kin", [128, S], BF16)
            vin = sb("vin", [128, S], BF16)
            w1k = sb("w1k", [128, 32, 256], BF16)
            w1v = sb("w1v", [128, 32, 256], BF16)
            w2k = sb("w2k", [128, 2, 128], BF16)
            w2v = sb("w2v", [128, 2, 128], BF16)
            posk = sb("posk", [128, 32], BF16)
            posv = sb("posv", [128, 32], BF16)
            hk = sb("hk", [128, 2, 256], BF16)
            hv = sb("hv", [128, 2, 256], BF16)
            cbias = sb("cbias", [128, 4], F32)
            kcmp = sb("kcmp", [128, 256], BF16)
            vcx = sb("vcx", [128, 2, 193], BF16)
            NP_ = 5
            pt = [sb(f"pt{i}", [128, 512], BF16) for i in range(NP_)]
            zat = [sb(f"zat{i}", [128, 4, 512], BF16) for i in range(2)]
            glt = [sb(f"glt{i}", [128, 4, 32], F32) for i in range(2)]
            iA = [sb(f"iA{i}", [128, 4, 64], F32) for i in range(2)]
            iB = [sb(f"iB{i}", [128, 4, 64], F32) for i in range(2)]
            accO = [sb(f"accO{i}", [128, 4, 4, 128], F32) for i in range(2)]
            impacc = [sb(f"impacc{i}", [128, 4, 64], F32) for i in range(2)]
            selbT = [sb(f"selbT{i}", [128, 512], BF16) for i in range(2)]
            imp2 = sb("imp2", [128, 64], F32)
            impw = sb("impw", [128, 64], F32)
            m8 = sb("m8", [128, 16], F32)
            s01 = sb("s01", [128, 64], F32)
            sb128 = sb("sb128", [128, 128], BF16)
            dn = [sb(f"dn{i}", [128, 8], F32) for i in range(4)]
            attb = [sb(f"attb{i}", [128, 512], BF16) for i in range(2)]
            attT = [sb(f"attT{i}", [128, 4, 128], BF16) for i in range(2)]

            b.memset("pool", sb128[:], 0.0, [sb128])
            b.memset("pool", hk[:], 0.0, [hk])
            b.memset("pool", hv[:], 0.0, [hv])
            b.memset("pool", kcmp[:], 0.0, [kcmp])
            b.memset("pool", vsel[:, :, 128:129], 1.0, [vsel])
            b.memset("pool", vwin[:, :, 128:129], 1.0, [vwin])
            b.dma("sp", vcx[:, :, 128:193], I["ovl1"][:, :, :], [], [vcx])

            cnt = {"S": 0, "O": 0, "P": 0, "T": 0, "dn": 0}

            for g in range(2):
                b.dma("sp", qT[:], sc["qT"].t[g * 4:(g + 1) * 4].rearrange("r p n -> p r n"), [sc["qT"]], [qT])
                b.dma("sp", qrT[:], sc["qrT"].t[g * 4:(g + 1) * 4].rearrange("r p n -> p r n"), [sc["qrT"]], [qrT])
                b.dma("sp", kselT[:], sc["kselT"].t[g], [sc["kselT"]], [kselT])
                b.dma("sp", kwinT[:], sc["kwinT"].t[g], [sc["kwinT"]], [kwinT])
                b.dma("sp", kin[:], sc["kcT"].t[g], [sc["kcT"]], [kin])
                b.dma("sp", vin[:], sc["vcT"].t[g], [sc["vcT"]], [vin])
                b.dma("sp", vsel[:, :, 0:128], sc["v"].t[g], [sc["v"]], [vsel])
                b.dma("sp", vwin[:, :, 0:128], sc["v"].t[2 + g], [sc["v"]], [vwin])
                if g == 0:
                    b.dma("pool", w1k[:], I["cmp_w1_k"][l], [], [w1k], max_dma_last_dim=4096)
                    b.dma("pool", w1v[:], I["cmp_w1_v"][l], [], [w1v], max_dma_last_dim=4096)
                    b.dma("pool", w2k[:], I["cmp_w2_k"][l].rearrange("(c p) d -> p c d", p=128), [], [w2k])
                    b.dma("pool", w2v[:], I["cmp_w2_v"][l].rearrange("(c p) d -> p c d", p=128), [], [w2v])
                    b.dma("pool", posk[:], I["cmp_posT_k"][l], [], [posk])
                    b.dma("pool", posv[:], I["cmp_posT_v"][l], [], [posv])
                    for kv, (w1, pos) in enumerate(((w1k, posk), (w1v, posv))):
                        for hc in range(2):
                            pa = pS[cnt["S"] % NS_]
                            cnt["S"] += 1
                            for li in range(32):
                                b.mm(pa[:, 0:1], w1[:, li, hc * 128:(hc + 1) * 128], pos[:, li:li + 1],
                                     li == 0, li == 31, [w1, pos], [pa])
                            b.copy("dve", cbias[:, kv * 2 + hc:kv * 2 + hc + 1], pa[:, 0:1], [pa], [cbias])

                if self.bstop <= 1:
                    break
                for kv, (src, w1, w2, hbuf) in enumerate(((kin, w1k, w2k, hk), (vin, w1v, w2v, hv))):
                    srcv = src[:].rearrange("p (n s) -> p n s", s=16)
                    for hc in range(2):
                        pa = pS[cnt["S"] % NS_]
                        cnt["S"] += 1
                        for li in range(32):
                            rhs = srcv[:, 0:255, li] if li < 16 else srcv[:, 1:256, li - 16]
                            b.mm(pa[:, 0:255], w1[:, li, hc * 128:(hc + 1) * 128], rhs, li == 0, li == 31,
                                 [w1, src], [pa])
                        b.act(hbuf[:, hc, 0:255], pa[:, 0:255], AF.Silu, [pa, cbias], [hbuf],
                              bias=cbias[:, kv * 2 + hc:kv * 2 + hc + 1])
                    if kv == 0:
                        pa = pS[cnt["S"] % NS_]
                        cnt["S"] += 1
                        for hc in range(2):
                            b.mm(pa[:, 0:255], w2[:, hc, :], hbuf[:, hc, 0:255], hc == 0, hc == 1, [w2, hbuf], [pa])
                        b.copy("dve", kcmp[:, 0:255], pa[:, 0:255], [pa], [kcmp])
                    else:
                        for a in range(2):
                            pa = pS[cnt["S"] % NS_]
                            cnt["S"] += 1
                            for hc in range(2):
                                b.mm(pa[:, 0:128], hbuf[:, hc, a * 128:(a + 1) * 128], w2[:, hc, :], hc == 0, hc == 1,
                                     [w2, hbuf], [pa])
                            b.copy("dve", vcx[:, a, 0:128], pa[:, 0:128], [pa], [vcx])

                if self.bstop <= 2:
                    break
                for i in range(NQ):
                    if self.bstop <= 7 and i >= 1:
                        break
                    q0 = i * 512
                    za, gt, A_, B_ = zat[i % 2], glt[i % 2], iA[i % 2], iB[i % 2]
                    ao, ia, sbt = accO[i % 2], impacc[i % 2], selbT[i % 2]

                    def tile_loads(ii):
                        qq = ii * 512
                        b.dma("sp", zat[ii % 2][:],
                              sc["zatt"].t[qq:qq + 512, g * 512:(g + 1) * 512].rearrange("(b p) c -> p b c", p=128),
                              [sc["zatt"]], [zat[ii % 2]])
                        b.dma("sp", glt[ii % 2][:], sc["gl"].t[qq:qq + 512, :].rearrange("(b p) c -> p b c", p=128),
                              [sc["gl"]], [glt[ii % 2]])
                        b.dma("sp", iA[ii % 2][:], I["impA"][ii], [], [iA[ii % 2]])
                        b.dma("sp", iB[ii % 2][:], I["impB"][ii], [], [iB[ii % 2]])

                    if i == 0:
                        tile_loads(0)

                    def norm_scales(Oset, width, gcol):
                        d_ = dn[cnt["dn"] % 4]
                        cnt["dn"] += 1
                        for j in range(2):
                            b.ts("dve", d_[:, 2 * j:2 * j + 2], Oset[j][:, :, width - 1], 1e-30, None, ALU.max, None,
                                 [Oset[j]], [d_])
                        P.op("dve", lambda e, d_=d_: e.reciprocal(out=d_[:, 0:4], in_=d_[:, 0:4]), [d_], [d_])
                        b.tt("dve", d_[:, 4:8], d_[:, 0:4], gt[:, :, gcol], ALU.mult, [d_, gt], [d_])
                        return d_

                    if self.bstop <= 2.5:
                        break
                    for r in range(4):
                        Oset = pO[cnt["O"] % 2]
                        cnt["O"] += 1
                        ets = []
                        for a in range(2):
                            u0 = q0 - 2048 * a
                            if u0 + 511 < 31:
                                continue
                            pa = pS[cnt["S"] % NS_]
                            cnt["S"] += 1
                            partial = u0 < 2063
                            b.mm(pa[:], kcmp[:, a * 128:(a + 1) * 128], qT[:, r, q0:q0 + 512], True, not partial,
                                 [kcmp, qT], [pa])
                            if partial:
                                b.mm(pa[:], ident[:], tbc[:, u0:u0 + 512], False, True, [ident, tbc], [pa])
                            e_ = pt[cnt["P"] % NP_]
                            cnt["P"] += 1
                            b.act(e_[:], pa[:], AF.Exp, [pa], [e_], scale=SCALE)
                            ets.append((a, e_))
                        if self.bstop <= 2.6:
                            continue
                        for bb in range(4):
                            for k_, (a, e_) in enumerate(ets):
                                b.mm(Oset[bb // 2][:, bb % 2, 0:193], e_[:, bb * 128:(bb + 1) * 128], vcx[:, a, :],
                                     k_ == 0, k_ == len(ets) - 1, [e_, vcx], [Oset[bb // 2]], skip_group_check=True)
                        if self.bstop <= 2.7:
                            continue
                        d_ = norm_scales(Oset, 193, (g * 4 + r) * 3 + 0)
                        if self.bstop <= 2.8:
                            continue
                        for bb in range(4):
                            Ob = Oset[bb // 2]
                            if self.bstop == 2.95:
                                pass
                            elif r == 0:
                                b.ts("dve", ia[:, bb, :], Ob[:, bb % 2, 128:192], d_[:, bb:bb + 1], None, ALU.mult, None,
                                     [Ob, d_], [ia])
                            else:
                                b.stt(ia[:, bb, :], Ob[:, bb % 2, 128:192], d_[:, bb:bb + 1], ia[:, bb, :],
                                      ALU.mult, ALU.add, [Ob, d_, ia], [ia])
                            if self.bstop == 2.9:
                                continue
                            b.act(ao[:, bb, r, :], Ob[:, bb % 2, 0:128], AF.Identity, [Ob, d_], [ao],
                                  scale=d_[:, 4 + bb:5 + bb])

                    if i + 1 < NQ:
                        tile_loads(i + 1)
                    if self.bstop <= 3:
                        break
                    for bb in range(4):
                        b.tt("dve", imp2[:], ia[:, bb, :], A_[:, bb, :], ALU.mult, [ia, A_], [imp2])
                        b.tt("dve", imp2[:], imp2[:], B_[:, bb, :], ALU.add, [imp2, B_], [imp2])
                        P.op("dve", lambda e: e.max(out=m8[:, 0:8], in_=imp2[:]), [imp2], [m8])
                        P.op("dve", lambda e: e.match_replace(out=impw[:], in_to_replace=m8[:, 0:8], in_values=imp2[:],
                                                              imm_value=-2.0), [imp2, m8], [impw])
                        P.op("dve", lambda e: e.max(out=m8[:, 8:16], in_=impw[:]), [impw], [m8])
                        b.ts("dve", s01[:], imp2[:], m8[:, 15:16], None, ALU.is_ge, None, [imp2, m8], [s01])
                        b.ts("dve", sb128[:, 0:64], s01[:], -NEGB, NEGB, ALU.mult, ALU.add, [s01], [sb128])
                        ptr_ = pT[0]
                        cnt["T"] += 1
                        b.tr(ptr_[:, 0:128], sb128[:], ident[:], [sb128, ident], [ptr_])
                        b.copy("dve", sbt[:, bb * 128:(bb + 1) * 128], ptr_[:, 0:128], [ptr_], [sbt])

                    if self.bstop <= 4:
                        break
                    def branch(r, kts, rng, masks, kT, vext, sel, gcol):
                        Oset = pO[cnt["O"] % 2]
                        cnt["O"] += 1
                        for j in range(2):
                            b.mm(Oset[j][:].rearrange("p a c -> p (a c)"), zl[:], zr[:], True, False, [zl, zr], [Oset[j]],
                                 skip_group_check=True)
                        last_kt = {}
                        for kt in kts:
                            lo, hi = rng(kt)
                            for bb in range(lo, hi + 1):
                                last_kt[bb] = kt

                        def qk(kt):
                            lo, hi = rng(kt)
                            c0, c1 = lo * 128, (hi + 1) * 128
                            pa = pS[cnt["S"] % NS_]
                            cnt["S"] += 1
                            mms = [(pa[:, c0:c1], kT[:, kt * 128:(kt + 1) * 128], qrT[:, r, q0 + c0:q0 + c1], [kT, qrT])]
                            if sel:
                                mms.append((pa[:, c0:c1], blkexp[:, kt * 128:(kt + 1) * 128], sbt[:, c0:c1], [blkexp, sbt]))
                            for (bb, tri) in masks(kt):
                                mms.append((pa[:, bb * 128:(bb + 1) * 128], ident[:], tri[:], [ident, tri]))
                            for n_, (o_, l_, r_, rd) in enumerate(mms):
                                b.mm(o_, l_, r_, n_ == 0, n_ == len(mms) - 1, rd, [pa])
                            return pa, lo, hi

                        pend = [qk(kts[0])]
                        if len(kts) > 1:
                            pend.append(qk(kts[1]))
                        for n_, kt in enumerate(kts):
                            pa, lo, hi = pend.pop(0)
                            if n_ + 2 < len(kts):
                                pend.append(qk(kts[n_ + 2]))
                            c0, c1 = lo * 128, (hi + 1) * 128
                            p_ = pt[cnt["P"] % NP_]
                            cnt["P"] += 1
                            b.act(p_[:, c0:c1], pa[:, c0:c1], AF.Exp, [pa], [p_], scale=SCALE)
                            for bb in range(lo, hi + 1):
                                b.mm(Oset[bb // 2][:, bb % 2, 0:129], p_[:, bb * 128:(bb + 1) * 128], vext[:, kt, :],
                                     False, last_kt[bb] == kt, [p_, vext], [Oset[bb // 2]], skip_group_check=True)
                        d_ = norm_scales(Oset, 129, gcol)
                        for bb in range(4):
                            Ob = Oset[bb // 2]
                            b.stt(ao[:, bb, r, :], Ob[:, bb % 2, 0:128], d_[:, 4 + bb:5 + bb], ao[:, bb, r, :],
                                  ALU.mult, ALU.add, [Ob, d_, ao], [ao])

                    for r in range(4):
                        kts = list(range(max(0, 4 * i - 4), 4 * i + 4))
                        rng = lambda kt: (max(0, kt - 4 * i), min(3, kt - 4 * i + 4))

                        def masks(kt):
                            m = []
                            if 0 <= kt - 4 * i <= 3:
                                m.append((kt - 4 * i, tribc))
                            if 0 <= kt + 4 - 4 * i <= 3:
                                m.append((kt + 4 - 4 * i, tribw))
                            return m
                        branch(r, kts, rng, masks, kwinT, vwin, False, (g * 4 + r) * 3 + 2)
                    if self.bstop <= 5:
                        break
                    for r in range(4):
                        kts = list(range(0, 4 * i + 4))
                        rng = lambda kt: (max(0, kt - 4 * i), 3)
                        masks = lambda kt: [(kt - 4 * i, tribc)] if kt >= 4 * i else []
                        branch(r, kts, rng, masks, kselT, vsel, True, (g * 4 + r) * 3 + 1)

                    if self.bstop <= 6:
                        break
                    for bb in range(4):
                        ab, aT = attb[bb % 2], attT[bb % 2]
                        b.tt("pool", ab[:], ao[:, bb, :, :].rearrange("p r d -> p (r d)"), za[:, bb, :], ALU.mult,
                             [ao, za], [ab])
                        ptr_ = pT[0]
                        cnt["T"] += 1
                        for r in range(4):
                            b.tr(ptr_[:, r * 128:(r + 1) * 128], ab[:, r * 128:(r + 1) * 128], ident[:], [ab, ident], [ptr_])
                        b.evac(aT[:].rearrange("p r q -> p (r q)"), ptr_[:, 0:512], [ptr_], [aT])
                        c0 = q0 + bb * 128
                        b.dma("sp", sc["mixT"].t[c0 // 128, :, g * 4:(g + 1) * 4, :], aT[:], [aT], [sc["mixT"]])
                if self.bstop <= 8:
                    break
            P.flush()


    def phase_C(self, l):
        nc, P, b, I, sc = self.nc, self.P, self.b, self.I, self.sc
        TWO_PI = 2.0 * math.pi
        with contextlib.ExitStack() as st:
            def sb(name, shape, dt):
                return T(st.enter_context(nc.sbuf_tensor(f"C{l}_{name}", shape, dt)), name)

            def ps(name, shape, dt):
                return T(st.enter_context(nc.psum_tensor(f"C{l}_{name}", shape, dt)), name, True)

            ident = sb("ident", [128, 128], BF16)
            identf = sb("identf", [128, 128], F32)
            swapf = sb("swapf", [128, 128], F32)
            sgn = sb("sgn", [128, 2], F32)
            b.dma("sp", ident[:], I["ident"][:, :], [], [ident])
            b.dma("pool", identf[:], I["ident"][:, :], [], [identf])
            b.dma("pool", swapf[:], I["swapi"][:, :], [], [swapf])
            b.dma("sp", sgn[:], I["sgn"][:, :], [], [sgn])

            ar = sb("ar", [128, 32], F32)
            ai = sb("ai", [128, 32], F32)
            dt_ = sb("dt", [128, 32], F32)
            tA = sb("tA", [128, 32], F32)
            tB = sb("tB", [128, 32], F32)
            tC = sb("tC", [128, 32], F32)
            th = sb("th", [128, 32], F32)
            mag = sb("mag", [128, 32], F32)
            abr = sb("abr", [128, 32], F32)
            abi = sb("abi", [128, 32], F32)
            cr = sb("cr", [128, 32], F32)
            ci = sb("ci", [128, 32], F32)
            for h in range(2):
                b.dma("sp", ar[h * 64:(h + 1) * 64, :], I["s5_a_reT"][l], [], [ar])
                b.dma("sp", ai[h * 64:(h + 1) * 64, :], I["s5_a_imT"][l], [], [ai])
            b.dma("sp", dt_[:], I["s5_log_dt"][l:l + 1, :].to_broadcast([128, 32]), [], [dt_])
            b.act(dt_[:], dt_[:], AF.Exp, [dt_], [dt_])
            b.tt("dve", tA[:], ar[:], dt_[:], ALU.mult, [ar, dt_], [tA])
            b.act(mag[:], tA[:], AF.Exp, [tA], [mag])
            b.tt("dve", th[:], ai[:], dt_[:], ALU.mult, [ai, dt_], [th])

            def sin_of(dst, src, shift):
                b.ts("dve", tB[:], src[:], shift, None, ALU.add, None, [src], [tB])
                b.copy("dve", tC[:], tB[:], [tB], [tC])
                for m in range(1, 9):
                    b.ts("dve", tA[:], tB[:], (2 * m - 1) * math.pi, TWO_PI, ALU.is_gt, ALU.mult, [tB], [tA])
                    b.tt("dve", tC[:], tC[:], tA[:], ALU.subtract, [tC, tA], [tC])
                b.act(dst[:], tC[:], AF.Sin, [tC], [dst])

            sin_of(abi, th, 0.0)
            sin_of(abr, th, 0.5 * math.pi)
            b.tt("dve", abr[:], abr[:], mag[:], ALU.mult, [abr, mag], [abr])
            b.tt("dve", abi[:], abi[:], mag[:], ALU.mult, [abi, mag], [abi])
            b.tt("dve", tA[:], ar[:], ar[:], ALU.mult, [ar], [tA])
            b.tt("dve", tB[:], ai[:], ai[:], ALU.mult, [ai], [tB])
            b.tt("dve", tA[:], tA[:], tB[:], ALU.add, [tA, tB], [tA])
            P.op("dve", lambda e: e.reciprocal(out=tA[:], in_=tA[:]), [tA], [tA])
            b.ts("dve", tB[:], abr[:], -1.0, None, ALU.add, None, [abr], [tB])
            b.tt("dve", cr[:], tB[:], ar[:], ALU.mult, [tB, ar], [cr])
            b.tt("dve", tC[:], abi[:], ai[:], ALU.mult, [abi, ai], [tC])
            b.tt("dve", cr[:], cr[:], tC[:], ALU.add, [cr, tC], [cr])
            b.tt("dve", cr[:], cr[:], tA[:], ALU.mult, [cr, tA], [cr])
            b.tt("dve", ci[:], abi[:], ar[:], ALU.mult, [abi, ar], [ci])
            b.tt("dve", tC[:], tB[:], ai[:], ALU.mult, [tB, ai], [tC])
            b.tt("dve", ci[:], ci[:], tC[:], ALU.subtract, [ci, tC], [ci])
            b.tt("dve", ci[:], ci[:], tA[:], ALU.mult, [ci, tA], [ci])
            b.ts("dve", ci[:], ci[:], sgn[:, 1:2], None, ALU.mult, None, [ci, sgn], [ci])

            pw1 = sb("pw1", [128, 12, 32], F32)
            pw2 = sb("pw2", [128, 12, 32], F32)
            b.copy("dve", pw1[:, 0, :], abr[:], [abr], [pw1])
            b.ts("dve", pw2[:, 0, :], abi[:], sgn[:, 0:1], None, ALU.mult, None, [abi, sgn], [pw2])
            for lv in range(1, 12):
                b.tt("dve", tA[:], pw1[:, lv - 1, :], pw1[:, lv - 1, :], ALU.mult, [pw1], [tA])
                b.tt("dve", tB[:], pw2[:, lv - 1, :], pw2[:, lv - 1, :], ALU.mult, [pw2], [tB])
                b.tt("dve", pw1[:, lv, :], tA[:], tB[:], ALU.subtract, [tA, tB], [pw1])
                b.tt("dve", tC[:], pw1[:, lv - 1, :], pw2[:, lv - 1, :], ALU.mult, [pw1, pw2], [tC])
                b.ts("dve", pw2[:, lv, :], tC[:], 2.0, None, ALU.mult, None, [tC], [pw2])

            bri = sb("bri", [128, 32, 16], F32)
            bir = sb("bir", [128, 32, 16], F32)
            ccs = sb("ccs", [128, 32, 16], F32)
            bbpad = sb("bbpad", [128, 32, 128], BF16)
            ccpad = sb("ccpad", [128, 32, 128], BF16)
            lhsB = sb("lhsB", [128, 32, 128], BF16)
            dcol = sb("dcol", [128, 4], F32)
            glub = sb("glub", [128, 8], F32)
            wg = sb("wg", [128, 4, 1024], BF16)
            b.dma("sp", bri[0:64], I["s5_b_reT"][l], [], [bri])
            b.dma("sp", bri[64:128], I["s5_b_imT"][l], [], [bri])
            b.dma("sp", bir[0:64], I["s5_b_imT"][l], [], [bir])
            b.dma("sp", bir[64:128], I["s5_b_reT"][l], [], [bir])
            b.dma("sp", ccs[0:64], I["s5_c_reT"][l], [], [ccs])
            b.dma("sp", ccs[64:128], I["s5_c_imT"][l], [], [ccs])
            b.dma("sp", dcol[:], I["s5_dT"][l], [], [dcol])
            b.dma("sp", glub[:], I["s5_glu_bT"][l], [], [glub])
            b.dma("pool", wg[:], I["s5_glu_w"][l].rearrange("(j p) e -> p j e", p=128), [], [wg])
            b.memset("pool", bbpad[:], 0.0, [bbpad])
            b.memset("pool", ccpad[:], 0.0, [ccpad])
            for g in range(32):
                c0 = 16 * (g % 8)
                b.ts("dve", bri[:, g, :], bri[:, g, :], cr[:, g:g + 1], None, ALU.mult, None, [bri, cr], [bri])
                b.stt(bbpad[:, g, c0:c0 + 16], bir[:, g, :], ci[:, g:g + 1], bri[:, g, :], ALU.mult, ALU.add,
                      [bir, ci, bri], [bbpad])
            for k in range(8):
                b.ts("dve", ccpad[:, k::8, 16 * k:16 * k + 16], ccs[:, k::8, :], sgn[:, 0:1], None, ALU.mult, None,
                     [ccs, sgn], [ccpad])
            pT = [ps(f"pT{i}", [128, 8, 128], BF16) for i in range(2)]
            for q in range(4):
                p_ = pT[q % 2]
                for j in range(8):
                    b.tr(p_[:, j, :], bbpad[:, q * 8 + j, :], ident[:], [bbpad, ident], [p_])
                b.evac(lhsB[:, q * 8:(q + 1) * 8, :], p_[:], [p_], [lhsB])

            ut = [sb(f"ut{i}", [128, S], BF16) for i in range(1)]
            Hs = [sb(f"H{i}", [128, S], BF16) for i in range(8)]
            Mm = sb("Mm", [128, 8, 12, 128], BF16)
            yT = sb("yT", [128, 4, S], BF16)
            pX = [ps(f"pX{i}", [128, 512], F32) for i in range(4)]
            ya = [sb(f"ya{i}", [128, 512], F32) for i in range(2)]
            yb = [sb(f"yb{i}", [128, 512], F32) for i in range(2)]
            mt1 = [sb(f"mt1{i}", [128, 12, 128], BF16) for i in range(2)]
            mt2 = [sb(f"mt2{i}", [128, 12, 128], BF16) for i in range(2)]
            nx = [0]
            nsc = [0]

            def nextp():
                p_ = pX[nx[0] % 4]
                nx[0] += 1
                return p_

            for j in range(4):
                u_t = ut[0]
                b.dma("sp", u_t[:], sc["us5T"].t[j * 128:(j + 1) * 128, :], [sc["us5T"]], [u_t])
                for gi in range(8):
                    g = j * 8 + gi
                    m1, m2 = mt1[gi % 2], mt2[gi % 2]
                    idB = identf[:].rearrange("p (o c) -> p o c", o=1).to_broadcast([128, 12, 128])
                    swB = swapf[:].rearrange("p (o c) -> p o c", o=1).to_broadcast([128, 12, 128])
                    b.tt("pool", m1[:], idB, pw1[:, :, g:g + 1].to_broadcast([128, 12, 128]), ALU.mult,
                         [identf, pw1], [m1])
                    b.tt("dve", m2[:], swB, pw2[:, :, g:g + 1].to_broadcast([128, 12, 128]), ALU.mult,
                         [swapf, pw2], [m2])
                    b.tt("pool", Mm[:, gi, :, :], m1[:], m2[:], ALU.add, [m1, m2], [Mm])
                    for tt_ in range(8):
                        p_ = nextp()
                        b.mm(p_[:], lhsB[:, g, :], u_t[:, tt_ * 512:(tt_ + 1) * 512], True, True, [lhsB, u_t], [p_])
                        b.evac(Hs[gi][:, tt_ * 512:(tt_ + 1) * 512], p_[:], [p_], [Hs[gi]])

                def level(lv, down):
                    s_ = 1 << lv
                    n2 = S // (2 * s_)
                    for gi in range(8):
                        Hv = Hs[gi][:].rearrange("p (k s) -> p k s", s=2 * s_)
                        if not down:
                            k0, k1, doff, soff, dk = 0, n2, 2 * s_ - 1, s_ - 1, 0
                        else:
                            k0, k1, doff, soff, dk = 0, n2 - 1, s_ - 1, 2 * s_ - 1, 1
                        kk = k0
                        while kk < k1:
                            ke = min(kk + 512, k1)
                            p_ = nextp()
                            dst = Hv[:, kk + dk:ke + dk, doff]
                            nsc[0] += 1
                            if nsc[0] % 2 == 0:
                                b.mm(p_[:, 0:ke - kk], Mm[:, gi, lv, :], Hv[:, kk:ke, soff], True, True, [Mm, Hs[gi]], [p_])
                                b.tt("dve", dst, p_[:, 0:ke - kk], dst, ALU.add, [p_, Hs[gi]], [Hs[gi]])
                            else:
                                b.mm(p_[:, 0:ke - kk], Mm[:, gi, lv, :], Hv[:, kk:ke, soff], True, False, [Mm, Hs[gi]], [p_])
                                b.mm(p_[:, 0:ke - kk], ident[:], dst, False, True, [ident, Hs[gi]], [p_])
                                b.copy("act", dst, p_[:, 0:ke - kk], [p_], [Hs[gi]])
                            kk = ke

                for lv in range(12):
                    level(lv, False)
                for lv in range(10, -1, -1):
                    level(lv, True)

                for tt_ in range(8):
                    tsl = slice(tt_ * 512, (tt_ + 1) * 512)
                    p_ = nextp()
                    for gi in range(8):
                        b.mm(p_[:], ccpad[:, j * 8 + gi, :], Hs[gi][:, tsl], gi == 0, gi == 7, [ccpad, Hs[gi]], [p_])
                    a_, b_ = ya[tt_ % 2], yb[tt_ % 2]
                    b.stt(a_[:], u_t[:, tsl], dcol[:, j:j + 1], p_[:], ALU.mult, ALU.add, [u_t, dcol, p_], [a_])
                    b.tt("pool", b_[:], a_[:], a_[:], ALU.mult, [a_], [b_])
                    b.ts("pool", b_[:], b_[:], 0.044715, 1.0, ALU.mult, ALU.add, [b_], [b_])
                    b.tt("pool", b_[:], b_[:], a_[:], ALU.mult, [b_, a_], [b_])
                    b.act(b_[:], b_[:], AF.Sigmoid, [b_], [b_], scale=1.5957691216057308)
                    b.tt("pool", yT[:, j, tsl], a_[:], b_[:], ALU.mult, [a_, b_], [yT])

            zt = [sb(f"zt{i}", [128, 512], BF16) for i in range(2)]
            sg = [sb(f"sg{i}", [128, 512], F32) for i in range(2)]
            og = [sb(f"og{i}", [128, 512], BF16) for i in range(2)]
            n_ = 0
            for c in range(4):
                for tt_ in range(8):
                    tsl = slice(tt_ * 512, (tt_ + 1) * 512)
                    z_, s_g, o_ = zt[n_ % 2], sg[n_ % 2], og[n_ % 2]
                    n_ += 1
                    b.dma("sp", z_[:], sc["zs5T"].t[c * 128:(c + 1) * 128, tsl], [sc["zs5T"]], [z_])
                    pa_, pg_ = nextp(), nextp()
                    for jj in range(4):
                        b.mm(pg_[:], wg[:, jj, 512 + c * 128:512 + (c + 1) * 128], yT[:, jj, tsl], jj == 0, jj == 3,
                             [wg, yT], [pg_])
                    for jj in range(4):
                        b.mm(pa_[:], wg[:, jj, c * 128:(c + 1) * 128], yT[:, jj, tsl], jj == 0, jj == 3, [wg, yT], [pa_])
                    b.act(s_g[:], pg_[:], AF.Sigmoid, [pg_, glub], [s_g], bias=glub[:, 4 + c:5 + c])
                    b.stt(s_g[:], pa_[:], glub[:, c:c + 1], s_g[:], ALU.add, ALU.mult, [pa_, glub, s_g], [s_g])
                    b.tt("pool", o_[:], s_g[:], z_[:], ALU.mult, [s_g, z_], [o_])
                    b.dma("sp", sc["mixT"].t[4 * tt_:4 * tt_ + 4, :, 8 + c, :].rearrange("a p q -> p a q"),
                          o_[:].rearrange("p (a q) -> p a q", a=4), [o_], [sc["mixT"]])
            P.flush()

    def phase_D(self, l):
        nc, P, b, I, sc = self.nc, self.P, self.b, self.I, self.sc
        with contextlib.ExitStack() as st:
            def sb(name, shape, dt):
                return T(st.enter_context(nc.sbuf_tensor(f"D{l}_{name}", shape, dt)), name)

            def ps(name, shape, dt):
                return T(st.enter_context(nc.psum_tensor(f"D{l}_{name}", shape, dt)), name, True)

            pw = sb("pw", [128, 4, 128], BF16)
            pscale = sb("pscale", [128, 4], F32)
            rc = sb("rc", [128, 4, 16], F32)
            b.dma("pool", pw[:], I["pool_w"][l].rearrange("k c d -> c k d"), [], [pw])
            b.dma("sp", pscale[:], I["pool_scaleT"][l], [], [pscale])
            b.dma("sp", rc[:], I["poolrc"][:, :, :], [], [rc])
            u = [sb(f"u{i}", [128, S], F32) for i in range(2)]
            sa = sb("sa", [128, S], F32)
            sbb = sb("sbb", [128, S], F32)
            pl = sb("pl", [128, S], BF16)
            zt = [sb(f"zt{i}", [128, 512], BF16) for i in range(2)]
            og = [sb(f"og{i}", [128, 512], BF16) for i in range(2)]
            pX = [ps(f"pX{i}", [128, 512], F32) for i in range(2)]
            n_ = 0
            for k in range(4):
                w = 2 << k
                u_ = u[k % 2]
                b.dma("sp", u_[:], sc["upoolT"].t[k * 128:(k + 1) * 128, :], [sc["upoolT"]], [u_])
                cur = u_
                bufs = [sa, sbb]
                sh = 1
                step = 0
                while sh < w:
                    nxt = bufs[step % 2]
                    eng = "dve" if step % 2 == 0 else "pool"
                    b.tt(eng, nxt[:, sh:S], cur[:, sh:S], cur[:, 0:S - sh], ALU.add, [cur], [nxt])
                    b.copy("pool" if eng == "dve" else "dve", nxt[:, 0:sh], cur[:, 0:sh], [cur], [nxt])
                    cur = nxt
                    sh *= 2
                    step += 1
                b.tt("dve", pl[:, 0:16], cur[:, 0:16], rc[:, k, :], ALU.mult, [cur, rc], [pl])
                b.tt("dve", pl[:, 0:16], pl[:, 0:16], u_[:, 0:16], ALU.subtract, [pl, u_], [pl])
                b.stt(pl[:, 16:S], cur[:, 16:S], 1.0 / w, u_[:, 16:S], ALU.mult, ALU.subtract, [cur, u_], [pl])
                for tt_ in range(8):
                    tsl = slice(tt_ * 512, (tt_ + 1) * 512)
                    z_, o_ = zt[n_ % 2], og[n_ % 2]
                    p_ = pX[n_ % 2]
                    n_ += 1
                    b.dma("sp", z_[:], sc["zpoolT"].t[k * 128:(k + 1) * 128, tsl], [sc["zpoolT"]], [z_])
                    b.mm(p_[:], pw[:, k, :], pl[:, tsl], True, True, [pw, pl], [p_])
                    b.stt(o_[:], p_[:], pscale[:, k:k + 1], z_[:], ALU.mult, ALU.mult, [p_, pscale, z_], [o_])
                    b.dma("sp", sc["mixT"].t[4 * tt_:4 * tt_ + 4, :, 12 + k, :].rearrange("a p q -> p a q"),
                          o_[:].rearrange("p (a q) -> p a q", a=4), [o_], [sc["mixT"]])
            P.flush()

    def phase_E(self, l, xin, xout):
        nc, P, b, I, sc = self.nc, self.P, self.b, self.I, self.sc
        xin_ap = xin.t if isinstance(xin, T) else xin
        xin_res = [xin] if isinstance(xin, T) else []
        with contextlib.ExitStack() as st:
            def sb(name, shape, dt):
                return T(st.enter_context(nc.sbuf_tensor(f"E{l}_{name}", shape, dt)), name)

            def ps(name, shape, dt):
                return T(st.enter_context(nc.psum_tensor(f"E{l}_{name}", shape, dt)), name, True)

            wo = sb("wo", [128, 16, D], BF16)
            gb = sb("gb", [128, D], F32)
            mx = [sb(f"mx{i}", [128, 16, 128], BF16) for i in range(3)]
            xt = [sb(f"xt{i}", [128, D], F32) for i in range(3)]
            ot = [sb(f"ot{i}", [128, D], F32) for i in range(3)]
            sq = sb("sq", [128, D], BF16)
            st1 = [sb(f"st{i}", [128, 4], F32) for i in range(3)]
            pacc = [ps(f"pacc{i}", [128, 512], F32) for i in range(8)]
            for kq in range(4):
                b.dma("pool", wo[:, kq * 4:(kq + 1) * 4, :],
                      I["w_out"][l, kq * 512:(kq + 1) * 512, :].rearrange("(kc p) n -> p kc n", p=128),
                      [], [wo])
            b.dma("sp", gb[:], I["norm_post"][l:l + 1, :].to_broadcast([128, D]), [], [gb])
            def loads(t):
                tsl = slice(t * 128, (t + 1) * 128)
                b.dma("sp", mx[t % 3][:], sc["mixT"].t[t], [sc["mixT"]], [mx[t % 3]])
                b.dma("sp", xt[t % 3][:], xin_ap[tsl, :], xin_res, [xt[t % 3]])

            loads(0)
            loads(1)
            for t in range(NT):
                tsl = slice(t * 128, (t + 1) * 128)
                m_t, x_t, o_t, s_t = mx[t % 3], xt[t % 3], ot[t % 3], st1[t % 3]
                if t + 2 < NT:
                    loads(t + 2)
                for n in range(4):
                    pa = pacc[(t % 2) * 4 + n]
                    for kc in range(16):
                        b.mm(pa[:], m_t[:, kc, :], wo[:, kc, n * 512:(n + 1) * 512], kc == 0, kc == 15,
                             [m_t, wo], [pa])
                    b.evac(o_t[:, n * 512:(n + 1) * 512], pa[:], [pa], [o_t])
                b.act(sq[:], o_t[:], AF.Square, [o_t], [sq, s_t], accum_out=s_t[:, 0:1])
                b.ts("dve", s_t[:, 1:2], s_t[:, 0:1], 1.0 / D, 1e-6, ALU.mult, ALU.add, [s_t], [s_t])
                b.act(s_t[:, 2:3], s_t[:, 1:2], AF.Sqrt, [s_t], [s_t])
                P.op("dve", lambda e, s_t=s_t: e.reciprocal(out=s_t[:, 3:4], in_=s_t[:, 2:3]), [s_t], [s_t])
                b.stt(o_t[:], o_t[:], s_t[:, 3:4], gb[:], ALU.mult, ALU.mult, [o_t, s_t, gb], [o_t])
                b.tt("pool", o_t[:], o_t[:], x_t[:], ALU.add, [o_t, x_t], [o_t])
                b.dma("sp", xout.t[tsl, :], o_t[:], [o_t], [xout])
            P.flush()


def _host_inputs(inputs):
    perm = _perm_cols()
    w_in = np.asarray(inputs["w_in"])
    wp = np.zeros((DEPTH, D, WCOLS), np.float32)
    ok = perm >= 0
    wp[:, :, ok] = w_in[:, :, perm[ok]]
    shared = dict(_consts())
    shared["w_in"] = wp
    shared["norm_pre"] = np.ascontiguousarray(inputs["norm_pre"], dtype=np.float32)
    shared["norm_post"] = np.ascontiguousarray(inputs["norm_post"], dtype=np.float32)
    shared["w_out"] = np.ascontiguousarray(inputs["w_out"], dtype=np.float32)
    for kv in ("k", "v"):
        shared[f"cmp_w1_{kv}"] = np.ascontiguousarray(
            np.asarray(inputs[f"cmp_w1_{kv}"], dtype=np.float32).transpose(0, 2, 1, 3))
        shared[f"cmp_w2_{kv}"] = np.ascontiguousarray(inputs[f"cmp_w2_{kv}"], dtype=np.float32)
        shared[f"cmp_posT_{kv}"] = np.ascontiguousarray(
            np.asarray(inputs[f"cmp_pos_{kv}"], dtype=np.float32).transpose(0, 2, 1))
    f32 = lambda a: np.ascontiguousarray(np.asarray(a, dtype=np.float32))
    shared["s5_a_reT"] = f32(np.asarray(inputs["s5_a_re"]).transpose(0, 2, 1))
    shared["s5_a_imT"] = f32(np.asarray(inputs["s5_a_im"]).transpose(0, 2, 1))
    shared["s5_log_dt"] = f32(inputs["s5_log_dt"])
    shared["s5_b_reT"] = f32(np.asarray(inputs["s5_b_re"]).transpose(0, 2, 1, 3))
    shared["s5_b_imT"] = f32(np.asarray(inputs["s5_b_im"]).transpose(0, 2, 1, 3))
    shared["s5_c_reT"] = f32(np.asarray(inputs["s5_c_re"]).transpose(0, 3, 1, 2))
    shared["s5_c_imT"] = f32(np.asarray(inputs["s5_c_im"]).transpose(0, 3, 1, 2))
    shared["s5_dT"] = f32(np.asarray(inputs["s5_d"]).reshape(DEPTH, 4, 128).transpose(0, 2, 1))
    shared["s5_glu_bT"] = f32(np.asarray(inputs["s5_glu_b"]).reshape(DEPTH, 8, 128).transpose(0, 2, 1))
    shared["s5_glu_w"] = f32(inputs["s5_glu_w"])
    shared["pool_w"] = f32(inputs["pool_w"])
    shared["pool_scaleT"] = f32(np.asarray(inputs["pool_scale"]).reshape(DEPTH, 4, 128).transpose(0, 2, 1))
    return shared


def kernel(**inputs):
    x = np.asarray(inputs["x"], dtype=np.float32)
    shared = _host_inputs(inputs)
    k = Kern()
    nc = k.build()
    nb = x.shape[0]
    in_maps = []
    for bi in range(nb):
        m = dict(shared)
        m["x"] = np.ascontiguousarray(x[bi])
        in_maps.append(m)
    res = run_bass_kernel_spmd(nc, in_maps, core_ids=list(range(nb)))
    return np.stack([np.asarray(r["y"]) for r in res.results], 0).astype(np.float32)
```

```python
import contextlib
import math
import numpy as np
import ml_dtypes
import concourse.bass as bass
import concourse.mybir as mybir
from concourse.bass_utils import run_bass_kernel_spmd

F32 = mybir.dt.float32
BF16 = mybir.dt.bfloat16
AF = mybir.ActivationFunctionType
ALU = mybir.AluOpType
AX = mybir.AxisListType

EPOCH = 7990
NDMASEM = 16
NPOOLSEM = 8

S = 4096
D = 2048
NT = S // 128
NQ = S // 512
DEPTH = 2
INW = 5656
HD = 128
SCALE = HD ** -0.5
NEGB = -30000.0


class Res:
    __slots__ = ("name", "lw", "rd", "excl")

    def __init__(self, name="", excl=False):
        self.name = name
        self.lw = None
        self.rd = {}
        self.excl = excl


class T:
    def __init__(self, t, name="", excl=False):
        self.t = t
        self.r = Res(name, excl)

    def __getitem__(self, k):
        return self.t[k]


def _res(x):
    return x.r if isinstance(x, T) else x


class Prog:
    ENGS = ("pe", "act", "dve", "pool", "sp")

    def __init__(self, nc, stack):
        self.nc = nc
        self.stack = stack
        self.ops = {e: [] for e in self.ENGS}
        self.sems = {e: [stack.enter_context(nc.semaphore(f"s_{e}_0"))] for e in self.ENGS}
        self.cnt = {e: 0 for e in self.ENGS}
        self.seen = {e: {} for e in self.ENGS}
        self.dsems = {"sp": [stack.enter_context(nc.semaphore(f"s_dma_{i}")) for i in range(NDMASEM)],
                      "pool": [stack.enter_context(nc.semaphore(f"s_dmap_{i}")) for i in range(NPOOLSEM)]}
        self.ndma = {"sp": 0, "pool": 0}
        self.strict = True
        self.nops = 0

    def _deps(self, reads, writes):
        deps = []
        for r in reads:
            r = _res(r)
            if r.lw is not None:
                deps.append(r.lw)
        for w in writes:
            w = _res(w)
            if w.lw is not None:
                deps.append(w.lw)
            deps.extend(w.rd.values())
        return deps

    def _waits(self, eng, deps):
        waits = {}
        seen = self.seen[eng]
        for (sem, val, src) in deps:
            if src == eng and (eng in ("pe", "sp") or not self.strict):
                continue
            k = id(sem)
            if seen.get(k, 0) >= val:
                continue
            if k not in waits or waits[k][1] < val:
                waits[k] = (sem, val)
        for k, (sem, val) in waits.items():
            seen[k] = val
        return list(waits.values())

    def _record(self, ev, reads, writes):
        key = id(ev[0])
        for r in reads:
            _res(r).rd[key] = ev
        for w in writes:
            w = _res(w)
            w.lw = ev
            w.rd = {}

    @staticmethod
    def _split(reads, writes):
        rd, wr = [], list(writes)
        for r in reads:
            if _res(r).excl:
                wr.append(r)
            else:
                rd.append(r)
        return rd, wr

    def op(self, eng, fn, reads=(), writes=()):
        reads, writes = self._split(reads, writes)
        deps = self._deps(reads, writes)
        waits = self._waits(eng, deps)
        if self.cnt[eng] >= EPOCH:
            self.sems[eng].append(
                self.stack.enter_context(self.nc.semaphore(f"s_{eng}_{len(self.sems[eng])}")))
            self.cnt[eng] = 0
        sem = self.sems[eng][-1]
        self.cnt[eng] += 1
        ev = (sem, self.cnt[eng], eng)
        self.ops[eng].append((waits, fn, sem, 1))
        self._record(ev, reads, writes)
        self.nops += 1
        return ev

    def dma(self, q, out, in_, reads=(), writes=(), **kw):
        reads, writes = self._split(reads, writes)
        deps = self._deps(reads, writes)
        i = self.ndma[q]
        self.ndma[q] += 1
        nsem = len(self.dsems[q])
        sem = self.dsems[q][i % nsem]
        tgt = 16 * (i // nsem + 1)
        if i >= nsem:
            deps.append((sem, tgt - 16, None))
        waits = self._waits(q, deps)
        fn = lambda e, out=out, in_=in_, kw=kw: e.dma_start(out=out, in_=in_, **kw)
        self.ops[q].append((waits, fn, sem, 16))
        ev = (sem, tgt, None)
        self._record(ev, reads, writes)
        self.nops += 1
        return ev

    def flush(self):
        deps = []
        for q in ("sp", "pool"):
            nsem = len(self.dsems[q])
            for j, sem in enumerate(self.dsems[q]):
                n = (self.ndma[q] - j + nsem - 1) // nsem if self.ndma[q] > j else 0
                if n > 0:
                    deps.append((sem, 16 * n, None))
        waits = self._waits("sp", deps)
        self.ops["sp"].append((waits, None, None, 0))
        nc = self.nc
        ops = self.ops
        self.ops = {e: [] for e in self.ENGS}
        with nc.Block() as block:
            def run(e, lst):
                for (waits, fn, sem, inc) in lst:
                    for (s, v) in waits:
                        e.wait_ge(s, v)
                    if fn is not None:
                        fn(e).then_inc(sem, inc)

            @block.tensor
            def _(e):
                run(e, ops["pe"])

            @block.scalar
            def _(e):
                run(e, ops["act"])

            @block.vector
            def _(e):
                run(e, ops["dve"])

            @block.gpsimd
            def _(e):
                run(e, ops["pool"])

            @block.sync
            def _(e):
                run(e, ops["sp"])


class B:
    def __init__(self, nc, P):
        self.nc = nc
        self.P = P
        self.flip = 0

    def act(self, out, in_, func, reads, writes, **kw):
        self.P.op("act", lambda e: e.activation(out=out, in_=in_, func=func, **kw), reads, writes)

    def mm(self, out, lhsT, rhs, start, stop, reads, writes, **kw):
        self.P.op("pe", lambda e: e.matmul(out, lhsT=lhsT, rhs=rhs, start=start, stop=stop, **kw),
                  reads, writes)

    def memset(self, eng, ap, val, writes):
        self.P.op(eng, lambda e: e.memset(ap, val), [], writes)

    def tr(self, out, in_, ident, reads, writes):
        self.P.op("pe", lambda e: e.transpose(out=out, in_=in_, identity=ident), reads, writes)

    def tt(self, eng, out, in0, in1, op, reads, writes):
        self.P.op(eng, lambda e: e.tensor_tensor(out=out, in0=in0, in1=in1, op=op), reads, writes)

    def ts(self, eng, out, in0, s1, s2, op0, op1, reads, writes):
        if op1 is None:
            self.P.op(eng, lambda e: e.tensor_scalar(out=out, in0=in0, scalar1=s1, scalar2=None, op0=op0),
                      reads, writes)
        else:
            self.P.op(eng, lambda e: e.tensor_scalar(out=out, in0=in0, scalar1=s1, scalar2=s2,
                                                      op0=op0, op1=op1), reads, writes)

    def stt(self, out, in0, scalar, in1, op0, op1, reads, writes):
        self.P.op("dve", lambda e: e.scalar_tensor_tensor(out=out, in0=in0, scalar=scalar, in1=in1,
                                                           op0=op0, op1=op1), reads, writes)

    def copy(self, eng, out, in_, reads, writes):
        if eng == "act":
            self.act(out, in_, AF.Copy, reads, writes)
        else:
            self.P.op(eng, lambda e: e.tensor_copy(out=out, in_=in_), reads, writes)

    def evac(self, out, in_, reads, writes):
        self.flip ^= 1
        self.copy("act" if self.flip else "dve", out, in_, reads, writes)

    def dma(self, q, out, in_, reads, writes, **kw):
        self.P.dma(q, out, in_, reads, writes, **kw)


OFF = {"q": 0, "kc": 1024, "vc": 1280, "ksl": 1536, "vsl": 1792, "kwn": 2048, "vwn": 2304,
       "gl": 2560, "zatt": 2584, "us5": 3608, "zs5": 4120, "upool": 4632, "zpool": 5144}
NWG = 12
WCOLS = 11 * 512 + 32


def _perm_cols():
    fm = []
    for h in range(8):
        fm.append(OFF["q"] + 128 * h)
    for nm in ("kc", "vc", "ksl", "kwn"):
        fm += [OFF[nm], OFF[nm] + 128]
    for nm in ("us5", "zs5", "upool", "zpool"):
        fm += [OFF[nm] + 128 * i for i in range(4)]
    cols = []
    for c in fm:
        cols += list(range(c, c + 128))
    cols += list(range(OFF["vsl"], OFF["vsl"] + 256)) + list(range(OFF["vwn"], OFF["vwn"] + 256))
    cols += list(range(OFF["zatt"], OFF["zatt"] + 1024))
    cols += list(range(OFF["gl"], OFF["gl"] + 24)) + [-1] * 8
    return np.array(cols)


def _consts():
    c = {}
    c["ident"] = np.eye(128, dtype=np.float32).astype(ml_dtypes.bfloat16)
    sw = np.zeros((128, 128), np.float32)
    for m in range(128):
        sw[(m + 64) % 128, m] = 1.0
    c["swapi"] = sw.astype(ml_dtypes.bfloat16)
    half = 64
    inv = 10000.0 ** (-np.arange(half, dtype=np.float32) / half)
    ang = np.arange(S, dtype=np.float32)[:, None] * inv[None, :]
    cos = np.cos(ang).astype(np.float32).T
    sin = np.sin(ang).astype(np.float32).T
    c["ropec"] = np.ascontiguousarray(np.concatenate([cos, cos], 0))
    c["ropes"] = np.ascontiguousarray(np.concatenate([-sin, sin], 0))
    bf = ml_dtypes.bfloat16
    p = np.arange(128)
    c["tribc"] = np.where(p[:, None] > p[None, :], NEGB, 0.0).astype(bf)
    c["tribw"] = np.where(p[:, None] <= p[None, :], NEGB, 0.0).astype(bf)
    u = np.arange(2560)
    c["tbc"] = np.where(16 * p[:, None] + 31 > u[None, :], NEGB, 0.0).astype(bf)
    c["blkexp"] = (np.arange(S)[None, :] // 64 == np.arange(64)[:, None]).astype(np.float32).astype(bf)
    n = np.arange(256)
    cs = n * 16
    js = np.arange(64) * 64
    ov = ((cs[:, None] < js[None, :] + 64) & (cs[:, None] + 32 > js[None, :])).astype(np.float32)
    ov = np.concatenate([ov, np.ones((256, 1), np.float32)], 1)
    ov[255] = 0.0
    c["ovl1"] = np.ascontiguousarray(ov.reshape(2, 128, 65).transpose(1, 0, 2)).astype(bf)
    q = np.arange(S)
    cur = q // 64
    j = np.arange(64)
    forced = (j[None, :] == cur[:, None]) | (j[None, :] == 0)
    causal = j[None, :] <= cur[:, None]
    sg = np.ones((128, 2), np.float32)
    sg[64:, 0] = -1.0
    sg[:64, 1] = -1.0
    c["sgn"] = sg
    t16 = np.arange(16)
    rc = np.stack([1.0 / np.minimum(t16 + 1, w) for w in (2, 4, 8, 16)], 0).astype(np.float32)
    c["poolrc"] = np.ascontiguousarray(np.broadcast_to(rc[None], (128, 4, 16))).astype(np.float32)
    tm = lambda a: np.ascontiguousarray(a.reshape(NQ, 4, 128, 64).transpose(0, 2, 1, 3))
    c["impA"] = tm((causal & ~forced).astype(np.float32))
    c["impB"] = tm(np.where(forced, 1e4, np.where(causal, 0.0, -1.0)).astype(np.float32))
    return c


class Kern:
    def __init__(self, debug=(), nlayers=DEPTH, phases=None, bstop=99):
        self.bstop = bstop
        self.debug = set(debug)
        self.nlayers = nlayers
        self.phases = phases
        self.nc = bass.Bass("TRN2", target_bir_lowering=False)
        self.gst = contextlib.ExitStack()
        self.P = None

    def din(self, name, shape, dt=F32):
        return self.nc.dram_tensor(name, list(shape), dt, kind="ExternalInput").ap()

    def dscr(self, name, shape, dt):
        kind = "ExternalOutput" if name in self.debug else "Internal"
        t = self.nc.dram_tensor(name, list(shape), dt, kind=kind).ap()
        return T(t, name)

    def build(self):
        nc = self.nc
        with self.gst as gst:
            self.P = Prog(nc, gst)
            self.b = B(nc, self.P)
            I = self.I = {}
            I["x"] = self.din("x", [S, D])
            I["norm_pre"] = self.din("norm_pre", [DEPTH, D])
            I["norm_post"] = self.din("norm_post", [DEPTH, D])
            I["w_in"] = self.din("w_in", [DEPTH, D, WCOLS])
            I["w_out"] = self.din("w_out", [DEPTH, D, D])
            I["ident"] = self.din("ident", [128, 128], BF16)
            I["swapi"] = self.din("swapi", [128, 128], BF16)
            I["ropec"] = self.din("ropec", [128, S])
            I["ropes"] = self.din("ropes", [128, S])
            for nm in ("tribc", "tribw"):
                I[nm] = self.din(nm, [128, 128], BF16)
            I["tbc"] = self.din("tbc", [128, 2560], BF16)
            I["blkexp"] = self.din("blkexp", [64, S], BF16)
            I["ovl1"] = self.din("ovl1", [128, 2, 65], BF16)
            I["impA"] = self.din("impA", [NQ, 128, 4, 64])
            I["impB"] = self.din("impB", [NQ, 128, 4, 64])
            for kv in ("k", "v"):
                I[f"cmp_w1_{kv}"] = self.din(f"cmp_w1_{kv}", [DEPTH, 128, 32, 256])
                I[f"cmp_w2_{kv}"] = self.din(f"cmp_w2_{kv}", [DEPTH, 256, 128])
                I[f"cmp_posT_{kv}"] = self.din(f"cmp_posT_{kv}", [DEPTH, 128, 32])
            I["sgn"] = self.din("sgn", [128, 2])
            I["s5_a_reT"] = self.din("s5_a_reT", [DEPTH, 64, 32])
            I["s5_a_imT"] = self.din("s5_a_imT", [DEPTH, 64, 32])
            I["s5_log_dt"] = self.din("s5_log_dt", [DEPTH, 32])
            for nm in ("s5_b_reT", "s5_b_imT", "s5_c_reT", "s5_c_imT"):
                I[nm] = self.din(nm, [DEPTH, 64, 32, 16])
            I["s5_dT"] = self.din("s5_dT", [DEPTH, 128, 4])
            I["s5_glu_bT"] = self.din("s5_glu_bT", [DEPTH, 128, 8])
            I["s5_glu_w"] = self.din("s5_glu_w", [DEPTH, 512, 1024])
            I["pool_w"] = self.din("pool_w", [DEPTH, 4, 128, 128])
            I["pool_scaleT"] = self.din("pool_scaleT", [DEPTH, 128, 4])
            I["poolrc"] = self.din("poolrc", [128, 4, 16])
            self.y = T(nc.dram_tensor("y", [S, D], F32, kind="ExternalOutput").ap(), "y")
            sc = self.sc = {}
            sc["qT"] = self.dscr("qT", [8, 128, S], BF16)
            sc["qrT"] = self.dscr("qrT", [8, 128, S], BF16)
            sc["kcT"] = self.dscr("kcT", [2, 128, S], BF16)
            sc["vcT"] = self.dscr("vcT", [2, 128, S], BF16)
            sc["kselT"] = self.dscr("kselT", [2, 128, S], BF16)
            sc["kwinT"] = self.dscr("kwinT", [2, 128, S], BF16)
            sc["us5T"] = self.dscr("us5T", [512, S], BF16)
            sc["zs5T"] = self.dscr("zs5T", [512, S], BF16)
            sc["upoolT"] = self.dscr("upoolT", [512, S], F32)
            sc["zpoolT"] = self.dscr("zpoolT", [512, S], BF16)
            sc["v"] = self.dscr("v", [4, 128, NT, 128], BF16)
            sc["zatt"] = self.dscr("zatt", [S, 1024], BF16)
            sc["gl"] = self.dscr("gl", [S, 32], F32)
            sc["mixT"] = self.dscr("mixT", [NT, 128, 16, 128], BF16)
            sc["x1"] = self.dscr("x1", [S, D], F32)
            for l in range(self.nlayers):
                xin = I["x"] if l == 0 else sc["x1"]
                xout = self.y if l == self.nlayers - 1 else sc["x1"]
                if self.phases is None or "A" in self.phases:
                    self.phase_A(l, xin)
                if self.phases is None or "B" in self.phases:
                    self.phase_B(l)
                if self.phases is None or "C" in self.phases:
                    self.phase_C(l)
                if self.phases is None or "D" in self.phases:
                    self.phase_D(l)
                if self.phases is None or "E" in self.phases:
                    self.phase_E(l, xin, xout)
        return nc

    def phase_A(self, l, xin):
        nc, P, b, I, sc = self.nc, self.P, self.b, self.I, self.sc
        xin_ap = xin.t if isinstance(xin, T) else xin
        xin_res = [xin] if isinstance(xin, T) else []
        with contextlib.ExitStack() as st:
            def sb(name, shape, dt):
                return T(st.enter_context(nc.sbuf_tensor(f"A{l}_{name}", shape, dt)), name)

            def ps(name, shape, dt):
                return T(st.enter_context(nc.psum_tensor(f"A{l}_{name}", shape, dt)), name, True)

            hT = sb("hT", [128, 16, S], BF16)
            ident = sb("ident", [128, 128], BF16)
            swapi = sb("swapi", [128, 128], BF16)
            st_outer = st
            st = st_outer.enter_context(contextlib.ExitStack())
            gb = sb("gb", [128, D], F32)
            xt = [sb(f"xt{i}", [128, D], F32) for i in range(3)]
            sq = sb("sq", [128, D], BF16)
            hb = [sb(f"hb{i}", [128, D], BF16) for i in range(3)]
            st1 = [sb(f"st{i}", [128, 4], F32) for i in range(3)]
            ptr = [ps(f"ptr{i}", [128, 8, 128], BF16) for i in range(4)]

            b.dma("sp", ident[:], I["ident"][:, :], [], [ident])
            b.dma("sp", swapi[:], I["swapi"][:, :], [], [swapi])
            b.dma("sp", gb[:], I["norm_pre"][l:l + 1, :].to_broadcast([128, D]), [], [gb])

            for t in range(NT):
                x_t, h_t, s_t = xt[t % 3], hb[t % 3], st1[t % 3]
                b.dma("sp", x_t[:], xin_ap[t * 128:(t + 1) * 128, :], xin_res, [x_t])
                b.act(sq[:], x_t[:], AF.Square, [x_t], [sq, s_t], accum_out=s_t[:, 0:1])
                b.ts("dve", s_t[:, 1:2], s_t[:, 0:1], 1.0 / D, 1e-6, ALU.mult, ALU.add, [s_t], [s_t])
                b.act(s_t[:, 2:3], s_t[:, 1:2], AF.Sqrt, [s_t], [s_t])
                P.op("dve", lambda e, s_t=s_t: e.reciprocal(out=s_t[:, 3:4], in_=s_t[:, 2:3]), [s_t], [s_t])
                b.stt(h_t[:], x_t[:], s_t[:, 3:4], gb[:], ALU.mult, ALU.mult, [x_t, s_t, gb], [h_t])
                for half in range(2):
                    p_t = ptr[(2 * t + half) % 4]
                    for j in range(8):
                        kc = half * 8 + j
                        b.tr(p_t[:, j, :], h_t[:, kc * 128:(kc + 1) * 128], ident[:], [h_t, ident], [p_t])
                    b.evac(hT[:, half * 8:(half + 1) * 8, t * 128:(t + 1) * 128], p_t[:], [p_t], [hT])

            P.flush()
            st.close()
            st = st_outer
            wb = [sb(f"wb{i}", [128, 16, 512], BF16) for i in range(2)]
            rope = [sb(f"rope{i}", [128, 2, 512], F32) for i in range(2)]
            NPA = 5
            NQU = 6
            pacc = [ps(f"pacc{i}", [128, 512], F32) for i in range(NPA)]
            prot = [ps(f"prot{i}", [128, 512], F32) for i in range(2)]
            qun = [sb(f"qun{i}", [128, 512], BF16) for i in range(NQU)]
            t1 = [sb(f"t1{i}", [128, 512], F32) for i in range(2)]
            t2 = [sb(f"t2{i}", [128, 512], F32) for i in range(2)]
            qro = [sb(f"qro{i}", [128, 512], BF16) for i in range(2)]
            o32 = [sb(f"o32{i}", [128, 512], F32) for i in range(2)]
            ogl = [sb(f"ogl{i}", [128, 32], F32) for i in range(2)]
            win = I["w_in"]
            nacc = [0]
            nrot = [0]

            def load_w(gi):
                w_t = wb[gi % 2]
                ncol = 512 if gi < 11 else 32
                src = win[l, :, gi * 512:gi * 512 + ncol].rearrange("(kc p) n -> p kc n", p=128)
                b.dma("pool", w_t[:, :, 0:ncol], src, [], [w_t])

            load_w(0)
            for gi in range(NWG):
                if gi + 1 < NWG:
                    load_w(gi + 1)
                w_t = wb[gi % 2]
                if gi < 8:
                    do_rope = gi in (0, 1, 3)
                    for tq in range(NQ):
                        tsl = slice(tq * 512, (tq + 1) * 512)
                        if do_rope:
                            rp = rope[tq % 2]
                            b.dma("sp", rp[:, 0, :], I["ropec"][:, tsl], [], [rp])
                            b.dma("sp", rp[:, 1, :], I["ropes"][:, tsl], [], [rp])
                        for c in range(4):
                            pa = pacc[nacc[0] % NPA]
                            nacc[0] += 1
                            for kc in range(16):
                                b.mm(pa[:], w_t[:, kc, c * 128:(c + 1) * 128], hT[:, kc, tsl],
                                     kc == 0, kc == 15, [w_t, hT], [pa])
                            ch = gi * 4 + c
                            if gi in (0, 1):
                                qu = qun[nacc[0] % NQU]
                                b.act(qu[:], pa[:], AF.Copy, [pa], [qu])
                                b.dma("sp", sc["qT"][ch, :, tsl], qu[:], [qu], [sc["qT"]])
                                self._rope(b, qu, rp, swapi, prot, t1, t2, qro, nrot,
                                           sc["qrT"], sc["qrT"][ch, :, tsl])
                            elif gi == 2:
                                qu = qun[nacc[0] % NQU]
                                b.act(qu[:], pa[:], AF.Copy, [pa], [qu])
                                dst = sc["kcT"] if c < 2 else sc["vcT"]
                                b.dma("sp", dst[c % 2, :, tsl], qu[:], [qu], [dst])
                            elif gi == 3:
                                qu = qun[nacc[0] % NQU]
                                b.act(qu[:], pa[:], AF.Copy, [pa], [qu])
                                dst = sc["kselT"] if c < 2 else sc["kwinT"]
                                self._rope(b, qu, rp, swapi, prot, t1, t2, qro, nrot, dst, dst[c % 2, :, tsl])
                            elif gi in (4, 5, 7):
                                qu = qun[nacc[0] % NQU]
                                dst = {4: sc["us5T"], 5: sc["zs5T"], 7: sc["zpoolT"]}[gi]
                                b.act(qu[:], pa[:], AF.Copy if gi == 4 else AF.Silu, [pa], [qu])
                                b.dma("sp", dst[c * 128:(c + 1) * 128, tsl], qu[:], [qu], [dst])
                            else:
                                o = o32[nacc[0] % 2]
                                b.act(o[:], pa[:], AF.Copy, [pa], [o])
                                b.dma("sp", sc["upoolT"][c * 128:(c + 1) * 128, tsl], o[:], [o], [sc["upoolT"]])
                else:
                    ncol = 512 if gi < 11 else 32
                    for t in range(NT):
                        tsl = slice(t * 128, (t + 1) * 128)
                        pa = pacc[nacc[0] % NPA]
                        nacc[0] += 1
                        for kc in range(16):
                            b.mm(pa[:, 0:ncol], hT[:, kc, tsl], w_t[:, kc, 0:ncol], kc == 0, kc == 15,
                                 [w_t, hT], [pa])
                        if gi == 8:
                            qu = qun[nacc[0] % NQU]
                            b.act(qu[:], pa[:], AF.Copy, [pa], [qu])
                            for c4 in range(4):
                                b.dma("sp", sc["v"].t[c4, :, t, :], qu[:, c4 * 128:(c4 + 1) * 128], [qu], [sc["v"]])
                        elif gi in (9, 10):
                            qu = qun[nacc[0] % NQU]
                            b.act(qu[:], pa[:], AF.Silu, [pa], [qu])
                            b.dma("sp", sc["zatt"][tsl, (gi - 9) * 512:(gi - 8) * 512], qu[:], [qu], [sc["zatt"]])
                        else:
                            o = ogl[nacc[0] % 2]
                            b.act(o[:], pa[:, 0:32], AF.Sigmoid, [pa], [o])
                            b.dma("sp", sc["gl"][tsl, :], o[:], [o], [sc["gl"]])
            P.flush()

    def _rope(self, b, qu, rp, swapi, prot, t1, t2, qro, nrot, dst_res, dst_ap):
        i = nrot[0]
        nrot[0] += 1
        pr, a1, a2, qo = prot[i % 2], t1[i % 2], t2[i % 2], qro[i % 2]
        b.mm(pr[:], swapi[:], qu[:], True, True, [swapi, qu], [pr])
        b.tt("dve", a1[:], pr[:], rp[:, 1, :], ALU.mult, [pr, rp], [a1])
        b.tt("pool", a2[:], qu[:], rp[:, 0, :], ALU.mult, [qu, rp], [a2])
        b.tt("dve", qo[:], a1[:], a2[:], ALU.add, [a1, a2], [qo])
        b.dma("sp", dst_ap, qo[:], [qo], [dst_res])


    def phase_B(self, l):
        nc, P, b, I, sc = self.nc, self.P, self.b, self.I, self.sc
        with contextlib.ExitStack() as st:
            def sb(name, shape, dt):
                return T(st.enter_context(nc.sbuf_tensor(f"B{l}_{name}", shape, dt)), name)

            def ps(name, shape, dt):
                return T(st.enter_context(nc.psum_tensor(f"B{l}_{name}", shape, dt)), name, True)

            ident = sb("ident", [128, 128], BF16)
            tribc = sb("tribc", [128, 128], BF16)
            tribw = sb("tribw", [128, 128], BF16)
            tbc = sb("tbc", [128, 2560], BF16)
            blkexp = sb("blkexp", [128, S], BF16)
            zl = sb("zl", [128, 128], BF16)
            zr = sb("zr", [128, 512], BF16)
            for (t_, nm) in ((ident, "ident"), (tribc, "tribc"), (tribw, "tribw"), (tbc, "tbc")):
                b.dma("sp", t_[:], I[nm][:, :], [], [t_])
            b.memset("pool", blkexp[:], 0.0, [blkexp])
            b.dma("sp", blkexp[0:64, :], I["blkexp"][:, :], [], [blkexp])
            b.memset("pool", zl[:], 0.0, [zl])
            b.memset("pool", zr[:], 0.0, [zr])

            NS_ = 3
            pS = [ps(f"pS{i}", [128, 512], F32) for i in range(NS_)]
            pO = [[ps(f"pO{a}{j}", [128, 2, 256], F32) for j in range(2)] for a in range(2)]
            pT = [ps(f"pT{i}", [128, 1024], BF16) for i in range(1)]

            qT = sb("qT", [128, 4, S], BF16)
            qrT = sb("qrT", [128, 4, S], BF16)
            kselT = sb("kselT", [128, S], BF16)
            kwinT = sb("kwinT", [128, S], BF16)
            vsel = sb("vsel", [128, NT, 129], BF16)
            vwin = sb("vwin", [128, NT, 129], BF16)
            kin = sb("kin", [128, S], BF16)
            vin = sb("vin", [128, S], BF16)
            w1k = sb("w1k", [128, 32, 256], BF16)
            w1v = sb("w1v", [128, 32, 256], BF16)
            w2k = sb("w2k", [128, 2, 128], BF16)
            w2v = sb("w2v", [128, 2, 128], BF16)
            posk = sb("posk", [128, 32], BF16)
            posv = sb("posv", [128, 32], BF16)
            hk = sb("hk", [128, 2, 256], BF16)
            hv = sb("hv", [128, 2, 256], BF16)
            cbias = sb("cbias", [128, 4], F32)
            kcmp = sb("kcmp", [128, 256], BF16)
            vcx = sb("vcx", [128, 2, 193], BF16)
            NP_ = 5
            pt = [sb(f"pt{i}", [128, 512], BF16) for i in range(NP_)]
            zat = [sb(f"zat{i}", [128, 4, 512], BF16) for i in range(2)]
            glt = [sb(f"glt{i}", [128, 4, 32], F32) for i in range(2)]
            iA = [sb(f"iA{i}", [128, 4, 64], F32) for i in range(2)]
            iB = [sb(f"iB{i}", [128, 4, 64], F32) for i in range(2)]
            accO = [sb(f"accO{i}", [128, 4, 4, 128], F32) for i in range(2)]
            impacc = [sb(f"impacc{i}", [128, 4, 64], F32) for i in range(2)]
            selbT = [sb(f"selbT{i}", [128, 512], BF16) for i in range(2)]
            imp2 = sb("imp2", [128, 64], F32)
            impw = sb("impw", [128, 64], F32)
            m8 = sb("m8", [128, 16], F32)
            s01 = sb("s01", [128, 64], F32)
            sb128 = sb("sb128", [128, 128], BF16)
            dn = [sb(f"dn{i}", [128, 8], F32) for i in range(4)]
            attb = [sb(f"attb{i}", [128, 512], BF16) for i in range(2)]
            attT = [sb(f"attT{i}", [128, 4, 128], BF16) for i in range(2)]

            b.memset("pool", sb128[:], 0.0, [sb128])
            b.memset("pool", hk[:], 0.0, [hk])
            b.memset("pool", hv[:], 0.0, [hv])
            b.memset("pool", kcmp[:], 0.0, [kcmp])
            b.memset("pool", vsel[:, :, 128:129], 1.0, [vsel])
            b.memset("pool", vwin[:, :, 128:129], 1.0, [vwin])
            b.dma("sp", vcx[:, :, 128:193], I["ovl1"][:, :, :], [], [vcx])

            cnt = {"S": 0, "O": 0, "P": 0, "T": 0, "dn": 0}

            for g in range(2):
                b.dma("sp", qT[:], sc["qT"].t[g * 4:(g + 1) * 4].rearrange("r p n -> p r n"), [sc["qT"]], [qT])
                b.dma("sp", qrT[:], sc["qrT"].t[g * 4:(g + 1) * 4].rearrange("r p n -> p r n"), [sc["qrT"]], [qrT])
                b.dma("sp", kselT[:], sc["kselT"].t[g], [sc["kselT"]], [kselT])
                b.dma("sp", kwinT[:], sc["kwinT"].t[g], [sc["kwinT"]], [kwinT])
                b.dma("sp", kin[:], sc["kcT"].t[g], [sc["kcT"]], [kin])
                b.dma("sp", vin[:], sc["vcT"].t[g], [sc["vcT"]], [vin])
                b.dma("sp", vsel[:, :, 0:128], sc["v"].t[g], [sc["v"]], [vsel])
                b.dma("sp", vwin[:, :, 0:128], sc["v"].t[2 + g], [sc["v"]], [vwin])
                if g == 0:
                    b.dma("pool", w1k[:], I["cmp_w1_k"][l], [], [w1k], max_dma_last_dim=4096)
                    b.dma("pool", w1v[:], I["cmp_w1_v"][l], [], [w1v], max_dma_last_dim=4096)
                    b.dma("pool", w2k[:], I["cmp_w2_k"][l].rearrange("(c p) d -> p c d", p=128), [], [w2k])
                    b.dma("pool", w2v[:], I["cmp_w2_v"][l].rearrange("(c p) d -> p c d", p=128), [], [w2v])
                    b.dma("pool", posk[:], I["cmp_posT_k"][l], [], [posk])
                    b.dma("pool", posv[:], I["cmp_posT_v"][l], [], [posv])
                    for kv, (w1, pos) in enumerate(((w1k, posk), (w1v, posv))):
                        for hc in range(2):
                            pa = pS[cnt["S"] % NS_]
                            cnt["S"] += 1
                            for li in range(32):
                                b.mm(pa[:, 0:1], w1[:, li, hc * 128:(hc + 1) * 128], pos[:, li:li + 1],
                                     li == 0, li == 31, [w1, pos], [pa])
                            b.copy("dve", cbias[:, kv * 2 + hc:kv * 2 + hc + 1], pa[:, 0:1], [pa], [cbias])

                if self.bstop <= 1:
                    break
                for kv, (src, w1, w2, hbuf) in enumerate(((kin, w1k, w2k, hk), (vin, w1v, w2v, hv))):
                    srcv = src[:].rearrange("p (n s) -> p n s", s=16)
                    for hc in range(2):
                        pa = pS[cnt["S"] % NS_]
                        cnt["S"] += 1
                        for li in range(32):
                            rhs = srcv[:, 0:255, li] if li < 16 else srcv[:, 1:256, li - 16]
                            b.mm(pa[:, 0:255], w1[:, li, hc * 128:(hc + 1) * 128], rhs, li == 0, li == 31,
                                 [w1, src], [pa])
                        b.act(hbuf[:, hc, 0:255], pa[:, 0:255], AF.Silu, [pa, cbias], [hbuf],
                              bias=cbias[:, kv * 2 + hc:kv * 2 + hc + 1])
                    if kv == 0:
                        pa = pS[cnt["S"] % NS_]
                        cnt["S"] += 1
                        for hc in range(2):
                            b.mm(pa[:, 0:255], w2[:, hc, :], hbuf[:, hc, 0:255], hc == 0, hc == 1, [w2, hbuf], [pa])
                        b.copy("dve", kcmp[:, 0:255], pa[:, 0:255], [pa], [kcmp])
                    else:
                        for a in range(2):
                            pa = pS[cnt["S"] % NS_]
                            cnt["S"] += 1
                            for hc in range(2):
                                b.mm(pa[:, 0:128], hbuf[:, hc, a * 128:(a + 1) * 128], w2[:, hc, :], hc == 0, hc == 1,
                                     [w2, hbuf], [pa])
                            b.copy("dve", vcx[:, a, 0:128], pa[:, 0:128], [pa], [vcx])

                if self.bstop <= 2:
                    break
                for i in range(NQ):
                    if self.bstop <= 7 and i >= 1:
                        break
                    q0 = i * 512
                    za, gt, A_, B_ = zat[i % 2], glt[i % 2], iA[i % 2], iB[i % 2]
                    ao, ia, sbt = accO[i % 2], impacc[i % 2], selbT[i % 2]

                    def tile_loads(ii):
                        qq = ii * 512
                        b.dma("sp", zat[ii % 2][:],
                              sc["zatt"].t[qq:qq + 512, g * 512:(g + 1) * 512].rearrange("(b p) c -> p b c", p=128),
                              [sc["zatt"]], [zat[ii % 2]])
                        b.dma("sp", glt[ii % 2][:], sc["gl"].t[qq:qq + 512, :].rearrange("(b p) c -> p b c", p=128),
                              [sc["gl"]], [glt[ii % 2]])
                        b.dma("sp", iA[ii % 2][:], I["impA"][ii], [], [iA[ii % 2]])
                        b.dma("sp", iB[ii % 2][:], I["impB"][ii], [], [iB[ii % 2]])

                    if i == 0:
                        tile_loads(0)

                    def norm_scales(Oset, width, gcol):
                        d_ = dn[cnt["dn"] % 4]
                        cnt["dn"] += 1
                        for j in range(2):
                            b.ts("dve", d_[:, 2 * j:2 * j + 2], Oset[j][:, :, width - 1], 1e-30, None, ALU.max, None,
                                 [Oset[j]], [d_])
                        P.op("dve", lambda e, d_=d_: e.reciprocal(out=d_[:, 0:4], in_=d_[:, 0:4]), [d_], [d_])
                        b.tt("dve", d_[:, 4:8], d_[:, 0:4], gt[:, :, gcol], ALU.mult, [d_, gt], [d_])
                        return d_

                    if self.bstop <= 2.5:
                        break
                    for r in range(4):
                        Oset = pO[cnt["O"] % 2]
                        cnt["O"] += 1
                        ets = []
                        for a in range(2):
                            u0 = q0 - 2048 * a
                            if u0 + 511 < 31:
                                continue
                            pa = pS[cnt["S"] % NS_]
                            cnt["S"] += 1
                            partial = u0 < 2063
                            b.mm(pa[:], kcmp[:, a * 128:(a + 1) * 128], qT[:, r, q0:q0 + 512], True, not partial,
                                 [kcmp, qT], [pa])
                            if partial:
                                b.mm(pa[:], ident[:], tbc[:, u0:u0 + 512], False, True, [ident, tbc], [pa])
                            e_ = pt[cnt["P"] % NP_]
                            cnt["P"] += 1
                            b.act(e_[:], pa[:], AF.Exp, [pa], [e_], scale=SCALE)
                            ets.append((a, e_))
                        if self.bstop <= 2.6:
                            continue
                        for bb in range(4):
                            for k_, (a, e_) in enumerate(ets):
                                b.mm(Oset[bb // 2][:, bb % 2, 0:193], e_[:, bb * 128:(bb + 1) * 128], vcx[:, a, :],
                                     k_ == 0, k_ == len(ets) - 1, [e_, vcx], [Oset[bb // 2]], skip_group_check=True)
                        if self.bstop <= 2.7:
                            continue
                        d_ = norm_scales(Oset, 193, (g * 4 + r) * 3 + 0)
                        if self.bstop <= 2.8:
                            continue
                        for bb in range(4):
                            Ob = Oset[bb // 2]
                            if self.bstop == 2.95:
                                pass
                            elif r == 0:
                                b.ts("dve", ia[:, bb, :], Ob[:, bb % 2, 128:192], d_[:, bb:bb + 1], None, ALU.mult, None,
                                     [Ob, d_], [ia])
                            else:
                                b.stt(ia[:, bb, :], Ob[:, bb % 2, 128:192], d_[:, bb:bb + 1], ia[:, bb, :],
                                      ALU.mult, ALU.add, [Ob, d_, ia], [ia])
                            if self.bstop == 2.9:
                                continue
                            b.act(ao[:, bb, r, :], Ob[:, bb % 2, 0:128], AF.Identity, [Ob, d_], [ao],
                                  scale=d_[:, 4 + bb:5 + bb])

                    if i + 1 < NQ:
                        tile_loads(i + 1)
                    if self.bstop <= 3:
                        break
                    for bb in range(4):
                        b.tt("dve", imp2[:], ia[:, bb, :], A_[:, bb, :], ALU.mult, [ia, A_], [imp2])
                        b.tt("dve", imp2[:], imp2[:], B_[:, bb, :], ALU.add, [imp2, B_], [imp2])
                        P.op("dve", lambda e: e.max(out=m8[:, 0:8], in_=imp2[:]), [imp2], [m8])
                        P.op("dve", lambda e: e.match_replace(out=impw[:], in_to_replace=m8[:, 0:8], in_values=imp2[:],
                                                              imm_value=-2.0), [imp2, m8], [impw])
                        P.op("dve", lambda e: e.max(out=m8[:, 8:16], in_=impw[:]), [impw], [m8])
                        b.ts("dve", s01[:], imp2[:], m8[:, 15:16], None, ALU.is_ge, None, [imp2, m8], [s01])
                        b.ts("dve", sb128[:, 0:64], s01[:], -NEGB, NEGB, ALU.mult, ALU.add, [s01], [sb128])
                        ptr_ = pT[0]
                        cnt["T"] += 1
                        b.tr(ptr_[:, 0:128], sb128[:], ident[:], [sb128, ident], [ptr_])
                        b.copy("dve", sbt[:, bb * 128:(bb + 1) * 128], ptr_[:, 0:128], [ptr_], [sbt])

                    if self.bstop <= 4:
                        break
                    def branch(r, kts, rng, masks, kT, vext, sel, gcol):
                        Oset = pO[cnt["O"] % 2]
                        cnt["O"] += 1
                        for j in range(2):
                            b.mm(Oset[j][:].rearrange("p a c -> p (a c)"), zl[:], zr[:], True, False, [zl, zr], [Oset[j]],
                                 skip_group_check=True)
                        last_kt = {}
                        for kt in kts:
                            lo, hi = rng(kt)
                            for bb in range(lo, hi + 1):
                                last_kt[bb] = kt

                        def qk(kt):
                            lo, hi = rng(kt)
                            c0, c1 = lo * 128, (hi + 1) * 128
                            pa = pS[cnt["S"] % NS_]
                            cnt["S"] += 1
                            mms = [(pa[:, c0:c1], kT[:, kt * 128:(kt + 1) * 128], qrT[:, r, q0 + c0:q0 + c1], [kT, qrT])]
                            if sel:
                                mms.append((pa[:, c0:c1], blkexp[:, kt * 128:(kt + 1) * 128], sbt[:, c0:c1], [blkexp, sbt]))
                            for (bb, tri) in masks(kt):
                                mms.append((pa[:, bb * 128:(bb + 1) * 128], ident[:], tri[:], [ident, tri]))
                            for n_, (o_, l_, r_, rd) in enumerate(mms):
                                b.mm(o_, l_, r_, n_ == 0, n_ == len(mms) - 1, rd, [pa])
                            return pa, lo, hi

                        pend = [qk(kts[0])]
                        if len(kts) > 1:
                            pend.append(qk(kts[1]))
                        for n_, kt in enumerate(kts):
                            pa, lo, hi = pend.pop(0)
                            if n_ + 2 < len(kts):
                                pend.append(qk(kts[n_ + 2]))
                            c0, c1 = lo * 128, (hi + 1) * 128
                            p_ = pt[cnt["P"] % NP_]
                            cnt["P"] += 1
                            b.act(p_[:, c0:c1], pa[:, c0:c1], AF.Exp, [pa], [p_], scale=SCALE)
                            for bb in range(lo, hi + 1):
                                b.mm(Oset[bb // 2][:, bb % 2, 0:129], p_[:, bb * 128:(bb + 1) * 128], vext[:, kt, :],
                                     False, last_kt[bb] == kt, [p_, vext], [Oset[bb // 2]], skip_group_check=True)
                        d_ = norm_scales(Oset, 129, gcol)
                        for bb in range(4):
                            Ob = Oset[bb // 2]
                            b.stt(ao[:, bb, r, :], Ob[:, bb % 2, 0:128], d_[:, 4 + bb:5 + bb], ao[:, bb, r, :],
                                  ALU.mult, ALU.add, [Ob, d_, ao], [ao])

                    for r in range(4):
                        kts = list(range(max(0, 4 * i - 4), 4 * i + 4))
                        rng = lambda kt: (max(0, kt - 4 * i), min(3, kt - 4 * i + 4))

                        def masks(kt):
                            m = []
                            if 0 <= kt - 4 * i <= 3:
                                m.append((kt - 4 * i, tribc))
                            if 0 <= kt + 4 - 4 * i <= 3:
                                m.append((kt + 4 - 4 * i, tribw))
                            return m
                        branch(r, kts, rng, masks, kwinT, vwin, False, (g * 4 + r) * 3 + 2)
                    if self.bstop <= 5:
                        break
                    for r in range(4):
                        kts = list(range(0, 4 * i + 4))
                        rng = lambda kt: (max(0, kt - 4 * i), 3)
                        masks = lambda kt: [(kt - 4 * i, tribc)] if kt >= 4 * i else []
                        branch(r, kts, rng, masks, kselT, vsel, True, (g * 4 + r) * 3 + 1)

                    if self.bstop <= 6:
                        break
                    for bb in range(4):
                        ab, aT = attb[bb % 2], attT[bb % 2]
                        b.tt("pool", ab[:], ao[:, bb, :, :].rearrange("p r d -> p (r d)"), za[:, bb, :], ALU.mult,
                             [ao, za], [ab])
                        ptr_ = pT[0]
                        cnt["T"] += 1
                        for r in range(4):
                            b.tr(ptr_[:, r * 128:(r + 1) * 128], ab[:, r * 128:(r + 1) * 128], ident[:], [ab, ident], [ptr_])
                        b.evac(aT[:].rearrange("p r q -> p (r q)"), ptr_[:, 0:512], [ptr_], [aT])
                        c0 = q0 + bb * 128
                        b.dma("sp", sc["mixT"].t[c0 // 128, :, g * 4:(g + 1) * 4, :], aT[:], [aT], [sc["mixT"]])
                if self.bstop <= 8:
                    break
            P.flush()


    def phase_C(self, l):
        nc, P, b, I, sc = self.nc, self.P, self.b, self.I, self.sc
        TWO_PI = 2.0 * math.pi
        with contextlib.ExitStack() as st:
            def sb(name, shape, dt):
                return T(st.enter_context(nc.sbuf_tensor(f"C{l}_{name}", shape, dt)), name)

            def ps(name, shape, dt):
                return T(st.enter_context(nc.psum_tensor(f"C{l}_{name}", shape, dt)), name, True)

            ident = sb("ident", [128, 128], BF16)
            identf = sb("identf", [128, 128], F32)
            swapf = sb("swapf", [128, 128], F32)
            sgn = sb("sgn", [128, 2], F32)
            b.dma("sp", ident[:], I["ident"][:, :], [], [ident])
            b.dma("pool", identf[:], I["ident"][:, :], [], [identf])
            b.dma("pool", swapf[:], I["swapi"][:, :], [], [swapf])
            b.dma("sp", sgn[:], I["sgn"][:, :], [], [sgn])

            ar = sb("ar", [128, 32], F32)
            ai = sb("ai", [128, 32], F32)
            dt_ = sb("dt", [128, 32], F32)
            tA = sb("tA", [128, 32], F32)
            tB = sb("tB", [128, 32], F32)
            tC = sb("tC", [128, 32], F32)
            th = sb("th", [128, 32], F32)
            mag = sb("mag", [128, 32], F32)
            abr = sb("abr", [128, 32], F32)
            abi = sb("abi", [128, 32], F32)
            cr = sb("cr", [128, 32], F32)
            ci = sb("ci", [128, 32], F32)
            for h in range(2):
                b.dma("sp", ar[h * 64:(h + 1) * 64, :], I["s5_a_reT"][l], [], [ar])
                b.dma("sp", ai[h * 64:(h + 1) * 64, :], I["s5_a_imT"][l], [], [ai])
            b.dma("sp", dt_[:], I["s5_log_dt"][l:l + 1, :].to_broadcast([128, 32]), [], [dt_])
            b.act(dt_[:], dt_[:], AF.Exp, [dt_], [dt_])
            b.tt("dve", tA[:], ar[:], dt_[:], ALU.mult, [ar, dt_], [tA])
            b.act(mag[:], tA[:], AF.Exp, [tA], [mag])
            b.tt("dve", th[:], ai[:], dt_[:], ALU.mult, [ai, dt_], [th])

            def sin_of(dst, src, shift):
                b.ts("dve", tB[:], src[:], shift, None, ALU.add, None, [src], [tB])
                b.copy("dve", tC[:], tB[:], [tB], [tC])
                for m in range(1, 9):
                    b.ts("dve", tA[:], tB[:], (2 * m - 1) * math.pi, TWO_PI, ALU.is_gt, ALU.mult, [tB], [tA])
                    b.tt("dve", tC[:], tC[:], tA[:], ALU.subtract, [tC, tA], [tC])
                b.act(dst[:], tC[:], AF.Sin, [tC], [dst])

            sin_of(abi, th, 0.0)
            sin_of(abr, th, 0.5 * math.pi)
            b.tt("dve", abr[:], abr[:], mag[:], ALU.mult, [abr, mag], [abr])
            b.tt("dve", abi[:], abi[:], mag[:], ALU.mult, [abi, mag], [abi])
            b.tt("dve", tA[:], ar[:], ar[:], ALU.mult, [ar], [tA])
            b.tt("dve", tB[:], ai[:], ai[:], ALU.mult, [ai], [tB])
            b.tt("dve", tA[:], tA[:], tB[:], ALU.add, [tA, tB], [tA])
            P.op("dve", lambda e: e.reciprocal(out=tA[:], in_=tA[:]), [tA], [tA])
            b.ts("dve", tB[:], abr[:], -1.0, None, ALU.add, None, [abr], [tB])
            b.tt("dve", cr[:], tB[:], ar[:], ALU.mult, [tB, ar], [cr])
            b.tt("dve", tC[:], abi[:], ai[:], ALU.mult, [abi, ai], [tC])
            b.tt("dve", cr[:], cr[:], tC[:], ALU.add, [cr, tC], [cr])
            b.tt("dve", cr[:], cr[:], tA[:], ALU.mult, [cr, tA], [cr])
            b.tt("dve", ci[:], abi[:], ar[:], ALU.mult, [abi, ar], [ci])
            b.tt("dve", tC[:], tB[:], ai[:], ALU.mult, [tB, ai], [tC])
            b.tt("dve", ci[:], ci[:], tC[:], ALU.subtract, [ci, tC], [ci])
            b.tt("dve", ci[:], ci[:], tA[:], ALU.mult, [ci, tA], [ci])
            b.ts("dve", ci[:], ci[:], sgn[:, 1:2], None, ALU.mult, None, [ci, sgn], [ci])

            pw1 = sb("pw1", [128, 12, 32], F32)
            pw2 = sb("pw2", [128, 12, 32], F32)
            b.copy("dve", pw1[:, 0, :], abr[:], [abr], [pw1])
            b.ts("dve", pw2[:, 0, :], abi[:], sgn[:, 0:1], None, ALU.mult, None, [abi, sgn], [pw2])
            for lv in range(1, 12):
                b.tt("dve", tA[:], pw1[:, lv - 1, :], pw1[:, lv - 1, :], ALU.mult, [pw1], [tA])
                b.tt("dve", tB[:], pw2[:, lv - 1, :], pw2[:, lv - 1, :], ALU.mult, [pw2], [tB])
                b.tt("dve", pw1[:, lv, :], tA[:], tB[:], ALU.subtract, [tA, tB], [pw1])
                b.tt("dve", tC[:], pw1[:, lv - 1, :], pw2[:, lv - 1, :], ALU.mult, [pw1, pw2], [tC])
                b.ts("dve", pw2[:, lv, :], tC[:], 2.0, None, ALU.mult, None, [tC], [pw2])

            bri = sb("bri", [128, 32, 16], F32)
            bir = sb("bir", [128, 32, 16], F32)
            ccs = sb("ccs", [128, 32, 16], F32)
            bbpad = sb("bbpad", [128, 32, 128], BF16)
            ccpad = sb("ccpad", [128, 32, 128], BF16)
            lhsB = sb("lhsB", [128, 32, 128], BF16)
            dcol = sb("dcol", [128, 4], F32)
            glub = sb("glub", [128, 8], F32)
            wg = sb("wg", [128, 4, 1024], BF16)
            b.dma("sp", bri[0:64], I["s5_b_reT"][l], [], [bri])
            b.dma("sp", bri[64:128], I["s5_b_imT"][l], [], [bri])
            b.dma("sp", bir[0:64], I["s5_b_imT"][l], [], [bir])
            b.dma("sp", bir[64:128], I["s5_b_reT"][l], [], [bir])
            b.dma("sp", ccs[0:64], I["s5_c_reT"][l], [], [ccs])
            b.dma("sp", ccs[64:128], I["s5_c_imT"][l], [], [ccs])
            b.dma("sp", dcol[:], I["s5_dT"][l], [], [dcol])
            b.dma("sp", glub[:], I["s5_glu_bT"][l], [], [glub])
            b.dma("pool", wg[:], I["s5_glu_w"][l].rearrange("(j p) e -> p j e", p=128), [], [wg])
            b.memset("pool", bbpad[:], 0.0, [bbpad])
            b.memset("pool", ccpad[:], 0.0, [ccpad])
            for g in range(32):
                c0 = 16 * (g % 8)
                b.ts("dve", bri[:, g, :], bri[:, g, :], cr[:, g:g + 1], None, ALU.mult, None, [bri, cr], [bri])
                b.stt(bbpad[:, g, c0:c0 + 16], bir[:, g, :], ci[:, g:g + 1], bri[:, g, :], ALU.mult, ALU.add,
                      [bir, ci, bri], [bbpad])
            for k in range(8):
                b.ts("dve", ccpad[:, k::8, 16 * k:16 * k + 16], ccs[:, k::8, :], sgn[:, 0:1], None, ALU.mult, None,
                     [ccs, sgn], [ccpad])
            pT = [ps(f"pT{i}", [128, 8, 128], BF16) for i in range(2)]
            for q in range(4):
                p_ = pT[q % 2]
                for j in range(8):
                    b.tr(p_[:, j, :], bbpad[:, q * 8 + j, :], ident[:], [bbpad, ident], [p_])
                b.evac(lhsB[:, q * 8:(q + 1) * 8, :], p_[:], [p_], [lhsB])

            ut = [sb(f"ut{i}", [128, S], BF16) for i in range(1)]
            Hs = [sb(f"H{i}", [128, S], BF16) for i in range(8)]
            Mm = sb("Mm", [128, 8, 12, 128], BF16)
            yT = sb("yT", [128, 4, S], BF16)
            pX = [ps(f"pX{i}", [128, 512], F32) for i in range(4)]
            ya = [sb(f"ya{i}", [128, 512], F32) for i in range(2)]
            yb = [sb(f"yb{i}", [128, 512], F32) for i in range(2)]
            mt1 = [sb(f"mt1{i}", [128, 12, 128], BF16) for i in range(2)]
            mt2 = [sb(f"mt2{i}", [128, 12, 128], BF16) for i in range(2)]
            nx = [0]
            nsc = [0]

            def nextp():
                p_ = pX[nx[0] % 4]
                nx[0] += 1
                return p_

            for j in range(4):
                u_t = ut[0]
                b.dma("sp", u_t[:], sc["us5T"].t[j * 128:(j + 1) * 128, :], [sc["us5T"]], [u_t])
                for gi in range(8):
                    g = j * 8 + gi
                    m1, m2 = mt1[gi % 2], mt2[gi % 2]
                    idB = identf[:].rearrange("p (o c) -> p o c", o=1).to_broadcast([128, 12, 128])
                    swB = swapf[:].rearrange("p (o c) -> p o c", o=1).to_broadcast([128, 12, 128])
                    b.tt("pool", m1[:], idB, pw1[:, :, g:g + 1].to_broadcast([128, 12, 128]), ALU.mult,
                         [identf, pw1], [m1])
                    b.tt("dve", m2[:], swB, pw2[:, :, g:g + 1].to_broadcast([128, 12, 128]), ALU.mult,
                         [swapf, pw2], [m2])
                    b.tt("pool", Mm[:, gi, :, :], m1[:], m2[:], ALU.add, [m1, m2], [Mm])
                    for tt_ in range(8):
                        p_ = nextp()
                        b.mm(p_[:], lhsB[:, g, :], u_t[:, tt_ * 512:(tt_ + 1) * 512], True, True, [lhsB, u_t], [p_])
                        b.evac(Hs[gi][:, tt_ * 512:(tt_ + 1) * 512], p_[:], [p_], [Hs[gi]])

                def level(lv, down):
                    s_ = 1 << lv
                    n2 = S // (2 * s_)
                    for gi in range(8):
                        Hv = Hs[gi][:].rearrange("p (k s) -> p k s", s=2 * s_)
                        if not down:
                            k0, k1, doff, soff, dk = 0, n2, 2 * s_ - 1, s_ - 1, 0
                        else:
                            k0, k1, doff, soff, dk = 0, n2 - 1, s_ - 1, 2 * s_ - 1, 1
                        kk = k0
                        while kk < k1:
                            ke = min(kk + 512, k1)
                            p_ = nextp()
                            dst = Hv[:, kk + dk:ke + dk, doff]
                            nsc[0] += 1
                            if nsc[0] % 2 == 0:
                                b.mm(p_[:, 0:ke - kk], Mm[:, gi, lv, :], Hv[:, kk:ke, soff], True, True, [Mm, Hs[gi]], [p_])
                                b.tt("dve", dst, p_[:, 0:ke - kk], dst, ALU.add, [p_, Hs[gi]], [Hs[gi]])
                            else:
                                b.mm(p_[:, 0:ke - kk], Mm[:, gi, lv, :], Hv[:, kk:ke, soff], True, False, [Mm, Hs[gi]], [p_])
                                b.mm(p_[:, 0:ke - kk], ident[:], dst, False, True, [ident, Hs[gi]], [p_])
                                b.copy("act", dst, p_[:, 0:ke - kk], [p_], [Hs[gi]])
                            kk = ke

                for lv in range(12):
                    level(lv, False)
                for lv in range(10, -1, -1):
                    level(lv, True)

                for tt_ in range(8):
                    tsl = slice(tt_ * 512, (tt_ + 1) * 512)
                    p_ = nextp()
                    for gi in range(8):
                        b.mm(p_[:], ccpad[:, j * 8 + gi, :], Hs[gi][:, tsl], gi == 0, gi == 7, [ccpad, Hs[gi]], [p_])
                    a_, b_ = ya[tt_ % 2], yb[tt_ % 2]
                    b.stt(a_[:], u_t[:, tsl], dcol[:, j:j + 1], p_[:], ALU.mult, ALU.add, [u_t, dcol, p_], [a_])
                    b.tt("pool", b_[:], a_[:], a_[:], ALU.mult, [a_], [b_])
                    b.ts("pool", b_[:], b_[:], 0.044715, 1.0, ALU.mult, ALU.add, [b_], [b_])
                    b.tt("pool", b_[:], b_[:], a_[:], ALU.mult, [b_, a_], [b_])
                    b.act(b_[:], b_[:], AF.Sigmoid, [b_], [b_], scale=1.5957691216057308)
                    b.tt("pool", yT[:, j, tsl], a_[:], b_[:], ALU.mult, [a_, b_], [yT])

            zt = [sb(f"zt{i}", [128, 512], BF16) for i in range(2)]
            sg = [sb(f"sg{i}", [128, 512], F32) for i in range(2)]
            og = [sb(f"og{i}", [128, 512], BF16) for i in range(2)]
            n_ = 0
            for c in range(4):
                for tt_ in range(8):
                    tsl = slice(tt_ * 512, (tt_ + 1) * 512)
                    z_, s_g, o_ = zt[n_ % 2], sg[n_ % 2], og[n_ % 2]
                    n_ += 1
                    b.dma("sp", z_[:], sc["zs5T"].t[c * 128:(c + 1) * 128, tsl], [sc["zs5T"]], [z_])
                    pa_, pg_ = nextp(), nextp()
                    for jj in range(4):
                        b.mm(pg_[:], wg[:, jj, 512 + c * 128:512 + (c + 1) * 128], yT[:, jj, tsl], jj == 0, jj == 3,
                             [wg, yT], [pg_])
                    for jj in range(4):
                        b.mm(pa_[:], wg[:, jj, c * 128:(c + 1) * 128], yT[:, jj, tsl], jj == 0, jj == 3, [wg, yT], [pa_])
                    b.act(s_g[:], pg_[:], AF.Sigmoid, [pg_, glub], [s_g], bias=glub[:, 4 + c:5 + c])
                    b.stt(s_g[:], pa_[:], glub[:, c:c + 1], s_g[:], ALU.add, ALU.mult, [pa_, glub, s_g], [s_g])
                    b.tt("pool", o_[:], s_g[:], z_[:], ALU.mult, [s_g, z_], [o_])
                    b.dma("sp", sc["mixT"].t[4 * tt_:4 * tt_ + 4, :, 8 + c, :].rearrange("a p q -> p a q"),
                          o_[:].rearrange("p (a q) -> p a q", a=4), [o_], [sc["mixT"]])
            P.flush()

    def phase_D(self, l):
        nc, P, b, I, sc = self.nc, self.P, self.b, self.I, self.sc
        with contextlib.ExitStack() as st:
            def sb(name, shape, dt):
                return T(st.enter_context(nc.sbuf_tensor(f"D{l}_{name}", shape, dt)), name)

            def ps(name, shape, dt):
                return T(st.enter_context(nc.psum_tensor(f"D{l}_{name}", shape, dt)), name, True)

            pw = sb("pw", [128, 4, 128], BF16)
            pscale = sb("pscale", [128, 4], F32)
            rc = sb("rc", [128, 4, 16], F32)
            b.dma("pool", pw[:], I["pool_w"][l].rearrange("k c d -> c k d"), [], [pw])
            b.dma("sp", pscale[:], I["pool_scaleT"][l], [], [pscale])
            b.dma("sp", rc[:], I["poolrc"][:, :, :], [], [rc])
            u = [sb(f"u{i}", [128, S], F32) for i in range(2)]
            sa = sb("sa", [128, S], F32)
            sbb = sb("sbb", [128, S], F32)
            pl = sb("pl", [128, S], BF16)
            sc2 = sb("sc2", [128, S // 2], F32)
            sa_lo, sa_hi, sb_lo, sb_hi, pl_lo, pl_hi = (Res() for _ in range(6))
            zt = [sb(f"zt{i}", [128, 512], BF16) for i in range(2)]
            og = [sb(f"og{i}", [128, 512], BF16) for i in range(2)]
            pX = [ps(f"pX{i}", [128, 512], F32) for i in range(2)]
            n_ = 0
            for k in range(4):
                w = 2 << k
                u_ = u[k % 2]
                b.dma("sp", u_[:], sc["upoolT"].t[k * 128:(k + 1) * 128, :], [sc["upoolT"]], [u_])
                H_ = S // 2
                cur, cur_r = u_, (u_, u_)
                bufs = [(sa, (sa_lo, sa_hi)), (sbb, (sb_lo, sb_hi))]
                sh = 1
                step = 0
                while sh < w:
                    nxt, nxt_r = bufs[step % 2]
                    b.copy("dve", nxt[:, 0:sh], cur[:, 0:sh], [cur_r[0]], [nxt_r[0]])
                    b.tt("dve", nxt[:, sh:H_], cur[:, sh:H_], cur[:, 0:H_ - sh], ALU.add, [cur_r[0]], [nxt_r[0]])
                    b.tt("pool", nxt[:, H_:S], cur[:, H_:S], cur[:, H_ - sh:S - sh], ALU.add, [cur_r[0], cur_r[1]], [nxt_r[1]])
                    cur, cur_r = nxt, nxt_r
                    sh *= 2
                    step += 1
                b.tt("dve", pl[:, 0:16], cur[:, 0:16], rc[:, k, :], ALU.mult, [cur_r[0], rc], [pl_lo])
                b.tt("dve", pl[:, 0:16], pl[:, 0:16], u_[:, 0:16], ALU.subtract, [pl_lo, u_], [pl_lo])
                b.stt(pl[:, 16:H_], cur[:, 16:H_], 1.0 / w, u_[:, 16:H_], ALU.mult, ALU.subtract, [cur_r[0], u_], [pl_lo])
                b.ts("pool", sc2[:], cur[:, H_:S], 1.0 / w, 0.0, ALU.mult, ALU.add, [cur_r[1]], [sc2])
                b.tt("pool", pl[:, H_:S], sc2[:], u_[:, H_:S], ALU.subtract, [sc2, u_], [pl_hi])
                for tt_ in range(8):
                    tsl = slice(tt_ * 512, (tt_ + 1) * 512)
                    z_, o_ = zt[n_ % 2], og[n_ % 2]
                    p_ = pX[n_ % 2]
                    n_ += 1
                    b.dma("sp", z_[:], sc["zpoolT"].t[k * 128:(k + 1) * 128, tsl], [sc["zpoolT"]], [z_])
                    b.mm(p_[:], pw[:, k, :], pl[:, tsl], True, True, [pw, pl_lo if tt_ < 4 else pl_hi], [p_])
                    b.stt(o_[:], p_[:], pscale[:, k:k + 1], z_[:], ALU.mult, ALU.mult, [p_, pscale, z_], [o_])
                    b.dma("sp", sc["mixT"].t[4 * tt_:4 * tt_ + 4, :, 12 + k, :].rearrange("a p q -> p a q"),
                          o_[:].rearrange("p (a q) -> p a q", a=4), [o_], [sc["mixT"]])
            P.flush()

    def phase_E(self, l, xin, xout):
        nc, P, b, I, sc = self.nc, self.P, self.b, self.I, self.sc
        xin_ap = xin.t if isinstance(xin, T) else xin
        xin_res = [xin] if isinstance(xin, T) else []
        with contextlib.ExitStack() as st:
            def sb(name, shape, dt):
                return T(st.enter_context(nc.sbuf_tensor(f"E{l}_{name}", shape, dt)), name)

            def ps(name, shape, dt):
                return T(st.enter_context(nc.psum_tensor(f"E{l}_{name}", shape, dt)), name, True)

            wo = sb("wo", [128, 16, D], BF16)
            gb = sb("gb", [128, D], F32)
            mx = [sb(f"mx{i}", [128, 16, 128], BF16) for i in range(3)]
            xt = [sb(f"xt{i}", [128, D], F32) for i in range(3)]
            ot = [sb(f"ot{i}", [128, D], F32) for i in range(3)]
            sq = sb("sq", [128, D], BF16)
            st1 = [sb(f"st{i}", [128, 4], F32) for i in range(3)]
            pacc = [ps(f"pacc{i}", [128, 512], F32) for i in range(8)]
            for kq in range(4):
                b.dma("pool", wo[:, kq * 4:(kq + 1) * 4, :],
                      I["w_out"][l, kq * 512:(kq + 1) * 512, :].rearrange("(kc p) n -> p kc n", p=128),
                      [], [wo])
            b.dma("sp", gb[:], I["norm_post"][l:l + 1, :].to_broadcast([128, D]), [], [gb])
            def loads(t):
                tsl = slice(t * 128, (t + 1) * 128)
                b.dma("sp", mx[t % 3][:], sc["mixT"].t[t], [sc["mixT"]], [mx[t % 3]])
                b.dma("sp", xt[t % 3][:], xin_ap[tsl, :], xin_res, [xt[t % 3]])

            loads(0)
            loads(1)
            for t in range(NT):
                tsl = slice(t * 128, (t + 1) * 128)
                m_t, x_t, o_t, s_t = mx[t % 3], xt[t % 3], ot[t % 3], st1[t % 3]
                if t + 2 < NT:
                    loads(t + 2)
                for n in range(4):
                    pa = pacc[(t % 2) * 4 + n]
                    for kc in range(16):
                        b.mm(pa[:], m_t[:, kc, :], wo[:, kc, n * 512:(n + 1) * 512], kc == 0, kc == 15,
                             [m_t, wo], [pa])
                    b.evac(o_t[:, n * 512:(n + 1) * 512], pa[:], [pa], [o_t])
                b.act(sq[:], o_t[:], AF.Square, [o_t], [sq, s_t], accum_out=s_t[:, 0:1])
                b.ts("dve", s_t[:, 1:2], s_t[:, 0:1], 1.0 / D, 1e-6, ALU.mult, ALU.add, [s_t], [s_t])
                b.act(s_t[:, 2:3], s_t[:, 1:2], AF.Sqrt, [s_t], [s_t])
                P.op("dve", lambda e, s_t=s_t: e.reciprocal(out=s_t[:, 3:4], in_=s_t[:, 2:3]), [s_t], [s_t])
                b.stt(o_t[:], o_t[:], s_t[:, 3:4], gb[:], ALU.mult, ALU.mult, [o_t, s_t, gb], [o_t])
                b.tt("pool", o_t[:], o_t[:], x_t[:], ALU.add, [o_t, x_t], [o_t])
                b.dma("sp", xout.t[tsl, :], o_t[:], [o_t], [xout])
            P.flush()


def _host_inputs(inputs):
    perm = _perm_cols()
    w_in = np.asarray(inputs["w_in"])
    wp = np.zeros((DEPTH, D, WCOLS), np.float32)
    ok = perm >= 0
    wp[:, :, ok] = w_in[:, :, perm[ok]]
    shared = dict(_consts())
    shared["w_in"] = wp
    shared["norm_pre"] = np.ascontiguousarray(inputs["norm_pre"], dtype=np.float32)
    shared["norm_post"] = np.ascontiguousarray(inputs["norm_post"], dtype=np.float32)
    shared["w_out"] = np.ascontiguousarray(inputs["w_out"], dtype=np.float32)
    for kv in ("k", "v"):
        shared[f"cmp_w1_{kv}"] = np.ascontiguousarray(
            np.asarray(inputs[f"cmp_w1_{kv}"], dtype=np.float32).transpose(0, 2, 1, 3))
        shared[f"cmp_w2_{kv}"] = np.ascontiguousarray(inputs[f"cmp_w2_{kv}"], dtype=np.float32)
        shared[f"cmp_posT_{kv}"] = np.ascontiguousarray(
            np.asarray(inputs[f"cmp_pos_{kv}"], dtype=np.float32).transpose(0, 2, 1))
    f32 = lambda a: np.ascontiguousarray(np.asarray(a, dtype=np.float32))
    shared["s5_a_reT"] = f32(np.asarray(inputs["s5_a_re"]).transpose(0, 2, 1))
    shared["s5_a_imT"] = f32(np.asarray(inputs["s5_a_im"]).transpose(0, 2, 1))
    shared["s5_log_dt"] = f32(inputs["s5_log_dt"])
    shared["s5_b_reT"] = f32(np.asarray(inputs["s5_b_re"]).transpose(0, 2, 1, 3))
    shared["s5_b_imT"] = f32(np.asarray(inputs["s5_b_im"]).transpose(0, 2, 1, 3))
    shared["s5_c_reT"] = f32(np.asarray(inputs["s5_c_re"]).transpose(0, 3, 1, 2))
    shared["s5_c_imT"] = f32(np.asarray(inputs["s5_c_im"]).transpose(0, 3, 1, 2))
    shared["s5_dT"] = f32(np.asarray(inputs["s5_d"]).reshape(DEPTH, 4, 128).transpose(0, 2, 1))
    shared["s5_glu_bT"] = f32(np.asarray(inputs["s5_glu_b"]).reshape(DEPTH, 8, 128).transpose(0, 2, 1))
    shared["s5_glu_w"] = f32(inputs["s5_glu_w"])
    shared["pool_w"] = f32(inputs["pool_w"])
    shared["pool_scaleT"] = f32(np.asarray(inputs["pool_scale"]).reshape(DEPTH, 4, 128).transpose(0, 2, 1))
    return shared


def kernel(**inputs):
    x = np.asarray(inputs["x"], dtype=np.float32)
    shared = _host_inputs(inputs)
    k = Kern()
    nc = k.build()
    nb = x.shape[0]
    in_maps = []
    for bi in range(nb):
        m = dict(shared)
        m["x"] = np.ascontiguousarray(x[bi])
        in_maps.append(m)
    res = run_bass_kernel_spmd(nc, in_maps, core_ids=list(range(nb)))
    return np.stack([np.asarray(r["y"]) for r in res.results], 0).astype(np.float32)
```

```python
import contextlib
import math
import numpy as np
import ml_dtypes
import concourse.bass as bass
import concourse.mybir as mybir
from concourse.bass_utils import run_bass_kernel_spmd

F32 = mybir.dt.float32
BF16 = mybir.dt.bfloat16
AF = mybir.ActivationFunctionType
ALU = mybir.AluOpType
AX = mybir.AxisListType

EPOCH = 7990
NDMASEM = 16
NPOOLSEM = 8

S = 4096
D = 2048
NT = S // 128
NQ = S // 512
DEPTH = 2
INW = 5656
HD = 128
SCALE = HD ** -0.5
NEGB = -30000.0


class Res:
    __slots__ = ("name", "lw", "rd", "excl")

    def __init__(self, name="", excl=False):
        self.name = name
        self.lw = None
        self.rd = {}
        self.excl = excl


class T:
    def __init__(self, t, name="", excl=False):
        self.t = t
        self.r = Res(name, excl)

    def __getitem__(self, k):
        return self.t[k]


def _res(x):
    return x.r if isinstance(x, T) else x


class Prog:
    ENGS = ("pe", "act", "dve", "pool", "sp")

    def __init__(self, nc, stack):
        self.nc = nc
        self.stack = stack
        self.ops = {e: [] for e in self.ENGS}
        self.sems = {e: [stack.enter_context(nc.semaphore(f"s_{e}_0"))] for e in self.ENGS}
        self.cnt = {e: 0 for e in self.ENGS}
        self.seen = {e: {} for e in self.ENGS}
        self.dsems = {"sp": [stack.enter_context(nc.semaphore(f"s_dma_{i}")) for i in range(NDMASEM)],
                      "pool": [stack.enter_context(nc.semaphore(f"s_dmap_{i}")) for i in range(NPOOLSEM)]}
        self.ndma = {"sp": 0, "pool": 0}
        self.strict = True
        self.nops = 0

    def _deps(self, reads, writes):
        deps = []
        for r in reads:
            r = _res(r)
            if r.lw is not None:
                deps.append(r.lw)
        for w in writes:
            w = _res(w)
            if w.lw is not None:
                deps.append(w.lw)
            deps.extend(w.rd.values())
        return deps

    def _waits(self, eng, deps):
        waits = {}
        seen = self.seen[eng]
        for (sem, val, src) in deps:
            if src == eng and (eng in ("pe", "sp") or not self.strict):
                continue
            k = id(sem)
            if seen.get(k, 0) >= val:
                continue
            if k not in waits or waits[k][1] < val:
                waits[k] = (sem, val)
        for k, (sem, val) in waits.items():
            seen[k] = val
        return list(waits.values())

    def _record(self, ev, reads, writes):
        key = id(ev[0])
        for r in reads:
            _res(r).rd[key] = ev
        for w in writes:
            w = _res(w)
            w.lw = ev
            w.rd = {}

    @staticmethod
    def _split(reads, writes):
        rd, wr = [], list(writes)
        for r in reads:
            if _res(r).excl:
                wr.append(r)
            else:
                rd.append(r)
        return rd, wr

    def op(self, eng, fn, reads=(), writes=()):
        reads, writes = self._split(reads, writes)
        deps = self._deps(reads, writes)
        waits = self._waits(eng, deps)
        if self.cnt[eng] >= EPOCH:
            self.sems[eng].append(
                self.stack.enter_context(self.nc.semaphore(f"s_{eng}_{len(self.sems[eng])}")))
            self.cnt[eng] = 0
        sem = self.sems[eng][-1]
        self.cnt[eng] += 1
        ev = (sem, self.cnt[eng], eng)
        self.ops[eng].append((waits, fn, sem, 1))
        self._record(ev, reads, writes)
        self.nops += 1
        return ev

    def dma(self, q, out, in_, reads=(), writes=(), **kw):
        reads, writes = self._split(reads, writes)
        deps = self._deps(reads, writes)
        i = self.ndma[q]
        self.ndma[q] += 1
        nsem = len(self.dsems[q])
        sem = self.dsems[q][i % nsem]
        tgt = 16 * (i // nsem + 1)
        if i >= nsem:
            deps.append((sem, tgt - 16, None))
        waits = self._waits(q, deps)
        fn = lambda e, out=out, in_=in_, kw=kw: e.dma_start(out=out, in_=in_, **kw)
        self.ops[q].append((waits, fn, sem, 16))
        ev = (sem, tgt, None)
        self._record(ev, reads, writes)
        self.nops += 1
        return ev

    def flush(self):
        deps = []
        for q in ("sp", "pool"):
            nsem = len(self.dsems[q])
            for j, sem in enumerate(self.dsems[q]):
                n = (self.ndma[q] - j + nsem - 1) // nsem if self.ndma[q] > j else 0
                if n > 0:
                    deps.append((sem, 16 * n, None))
        waits = self._waits("sp", deps)
        self.ops["sp"].append((waits, None, None, 0))
        nc = self.nc
        ops = self.ops
        self.ops = {e: [] for e in self.ENGS}
        with nc.Block() as block:
            def run(e, lst):
                for (waits, fn, sem, inc) in lst:
                    for (s, v) in waits:
                        e.wait_ge(s, v)
                    if fn is not None:
                        fn(e).then_inc(sem, inc)

            @block.tensor
            def _(e):
                run(e, ops["pe"])

            @block.scalar
            def _(e):
                run(e, ops["act"])

            @block.vector
            def _(e):
                run(e, ops["dve"])

            @block.gpsimd
            def _(e):
                run(e, ops["pool"])

            @block.sync
            def _(e):
                run(e, ops["sp"])


class B:
    def __init__(self, nc, P):
        self.nc = nc
        self.P = P
        self.flip = 0

    def act(self, out, in_, func, reads, writes, **kw):
        self.P.op("act", lambda e: e.activation(out=out, in_=in_, func=func, **kw), reads, writes)

    def mm(self, out, lhsT, rhs, start, stop, reads, writes, **kw):
        self.P.op("pe", lambda e: e.matmul(out, lhsT=lhsT, rhs=rhs, start=start, stop=stop, **kw),
                  reads, writes)

    def memset(self, eng, ap, val, writes):
        self.P.op(eng, lambda e: e.memset(ap, val), [], writes)

    def tr(self, out, in_, ident, reads, writes):
        self.P.op("pe", lambda e: e.transpose(out=out, in_=in_, identity=ident), reads, writes)

    def tt(self, eng, out, in0, in1, op, reads, writes):
        self.P.op(eng, lambda e: e.tensor_tensor(out=out, in0=in0, in1=in1, op=op), reads, writes)

    def ts(self, eng, out, in0, s1, s2, op0, op1, reads, writes):
        if op1 is None:
            self.P.op(eng, lambda e: e.tensor_scalar(out=out, in0=in0, scalar1=s1, scalar2=None, op0=op0),
                      reads, writes)
        else:
            self.P.op(eng, lambda e: e.tensor_scalar(out=out, in0=in0, scalar1=s1, scalar2=s2,
                                                      op0=op0, op1=op1), reads, writes)

    def stt(self, out, in0, scalar, in1, op0, op1, reads, writes):
        self.P.op("dve", lambda e: e.scalar_tensor_tensor(out=out, in0=in0, scalar=scalar, in1=in1,
                                                           op0=op0, op1=op1), reads, writes)

    def copy(self, eng, out, in_, reads, writes):
        if eng == "act":
            self.act(out, in_, AF.Copy, reads, writes)
        else:
            self.P.op(eng, lambda e: e.tensor_copy(out=out, in_=in_), reads, writes)

    def evac(self, out, in_, reads, writes):
        self.flip ^= 1
        self.copy("act" if self.flip else "dve", out, in_, reads, writes)

    def dma(self, q, out, in_, reads, writes, **kw):
        self.P.dma(q, out, in_, reads, writes, **kw)


OFF = {"q": 0, "kc": 1024, "vc": 1280, "ksl": 1536, "vsl": 1792, "kwn": 2048, "vwn": 2304,
       "gl": 2560, "zatt": 2584, "us5": 3608, "zs5": 4120, "upool": 4632, "zpool": 5144}
NWG = 12
WCOLS = 11 * 512 + 32


def _perm_cols():
    fm = []
    for h in range(8):
        fm.append(OFF["q"] + 128 * h)
    for nm in ("kc", "vc", "ksl", "kwn"):
        fm += [OFF[nm], OFF[nm] + 128]
    for nm in ("us5", "zs5", "upool", "zpool"):
        fm += [OFF[nm] + 128 * i for i in range(4)]
    cols = []
    for c in fm:
        cols += list(range(c, c + 128))
    cols += list(range(OFF["vsl"], OFF["vsl"] + 256)) + list(range(OFF["vwn"], OFF["vwn"] + 256))
    cols += list(range(OFF["zatt"], OFF["zatt"] + 1024))
    cols += list(range(OFF["gl"], OFF["gl"] + 24)) + [-1] * 8
    return np.array(cols)


def _consts():
    c = {}
    c["ident"] = np.eye(128, dtype=np.float32).astype(ml_dtypes.bfloat16)
    sw = np.zeros((128, 128), np.float32)
    for m in range(128):
        sw[(m + 64) % 128, m] = 1.0
    c["swapi"] = sw.astype(ml_dtypes.bfloat16)
    half = 64
    inv = 10000.0 ** (-np.arange(half, dtype=np.float32) / half)
    ang = np.arange(S, dtype=np.float32)[:, None] * inv[None, :]
    cos = np.cos(ang).astype(np.float32).T
    sin = np.sin(ang).astype(np.float32).T
    c["ropec"] = np.ascontiguousarray(np.concatenate([cos, cos], 0))
    c["ropes"] = np.ascontiguousarray(np.concatenate([-sin, sin], 0))
    bf = ml_dtypes.bfloat16
    p = np.arange(128)
    c["tribc"] = np.where(p[:, None] > p[None, :], NEGB, 0.0).astype(bf)
    c["tribw"] = np.where(p[:, None] <= p[None, :], NEGB, 0.0).astype(bf)
    u = np.arange(2560)
    c["tbc"] = np.where(16 * p[:, None] + 31 > u[None, :], NEGB, 0.0).astype(bf)
    c["blkexp"] = (np.arange(S)[None, :] // 64 == np.arange(64)[:, None]).astype(np.float32).astype(bf)
    n = np.arange(256)
    cs = n * 16
    js = np.arange(64) * 64
    ov = ((cs[:, None] < js[None, :] + 64) & (cs[:, None] + 32 > js[None, :])).astype(np.float32)
    ov = np.concatenate([ov, np.ones((256, 1), np.float32)], 1)
    ov[255] = 0.0
    c["ovl1"] = np.ascontiguousarray(ov.reshape(2, 128, 65).transpose(1, 0, 2)).astype(bf)
    q = np.arange(S)
    cur = q // 64
    j = np.arange(64)
    forced = (j[None, :] == cur[:, None]) | (j[None, :] == 0)
    causal = j[None, :] <= cur[:, None]
    sg = np.ones((128, 2), np.float32)
    sg[64:, 0] = -1.0
    sg[:64, 1] = -1.0
    c["sgn"] = sg
    t16 = np.arange(16)
    rc = np.stack([1.0 / np.minimum(t16 + 1, w) for w in (2, 4, 8, 16)], 0).astype(np.float32)
    c["poolrc"] = np.ascontiguousarray(np.broadcast_to(rc[None], (128, 4, 16))).astype(np.float32)
    tm = lambda a: np.ascontiguousarray(a.reshape(NQ, 4, 128, 64).transpose(0, 2, 1, 3))
    c["impA"] = tm((causal & ~forced).astype(np.float32))
    c["impB"] = tm(np.where(forced, 1e4, np.where(causal, 0.0, -1.0)).astype(np.float32))
    return c


class Kern:
    def __init__(self, debug=(), nlayers=DEPTH, phases=None, bstop=99):
        self.bstop = bstop
        self.debug = set(debug)
        self.nlayers = nlayers
        self.phases = phases
        self.nc = bass.Bass("TRN2", target_bir_lowering=False)
        self.gst = contextlib.ExitStack()
        self.P = None

    def din(self, name, shape, dt=F32):
        return self.nc.dram_tensor(name, list(shape), dt, kind="ExternalInput").ap()

    def dscr(self, name, shape, dt):
        kind = "ExternalOutput" if name in self.debug else "Internal"
        t = self.nc.dram_tensor(name, list(shape), dt, kind=kind).ap()
        return T(t, name)

    def build(self):
        nc = self.nc
        with self.gst as gst:
            self.P = Prog(nc, gst)
            self.b = B(nc, self.P)
            I = self.I = {}
            I["x"] = self.din("x", [S, D])
            I["norm_pre"] = self.din("norm_pre", [DEPTH, D])
            I["norm_post"] = self.din("norm_post", [DEPTH, D])
            I["w_in"] = self.din("w_in", [DEPTH, D, WCOLS])
            I["w_out"] = self.din("w_out", [DEPTH, D, D])
            I["ident"] = self.din("ident", [128, 128], BF16)
            I["swapi"] = self.din("swapi", [128, 128], BF16)
            I["ropec"] = self.din("ropec", [128, S])
            I["ropes"] = self.din("ropes", [128, S])
            for nm in ("tribc", "tribw"):
                I[nm] = self.din(nm, [128, 128], BF16)
            I["tbc"] = self.din("tbc", [128, 2560], BF16)
            I["blkexp"] = self.din("blkexp", [64, S], BF16)
            I["ovl1"] = self.din("ovl1", [128, 2, 65], BF16)
            I["impA"] = self.din("impA", [NQ, 128, 4, 64])
            I["impB"] = self.din("impB", [NQ, 128, 4, 64])
            for kv in ("k", "v"):
                I[f"cmp_w1_{kv}"] = self.din(f"cmp_w1_{kv}", [DEPTH, 128, 32, 256])
                I[f"cmp_w2_{kv}"] = self.din(f"cmp_w2_{kv}", [DEPTH, 256, 128])
                I[f"cmp_posT_{kv}"] = self.din(f"cmp_posT_{kv}", [DEPTH, 128, 32])
            I["sgn"] = self.din("sgn", [128, 2])
            I["s5_a_reT"] = self.din("s5_a_reT", [DEPTH, 64, 32])
            I["s5_a_imT"] = self.din("s5_a_imT", [DEPTH, 64, 32])
            I["s5_log_dt"] = self.din("s5_log_dt", [DEPTH, 32])
            for nm in ("s5_b_reT", "s5_b_imT", "s5_c_reT", "s5_c_imT"):
                I[nm] = self.din(nm, [DEPTH, 64, 32, 16])
            I["s5_dT"] = self.din("s5_dT", [DEPTH, 128, 4])
            I["s5_glu_bT"] = self.din("s5_glu_bT", [DEPTH, 128, 8])
            I["s5_glu_w"] = self.din("s5_glu_w", [DEPTH, 512, 1024])
            I["pool_w"] = self.din("pool_w", [DEPTH, 4, 128, 128])
            I["pool_scaleT"] = self.din("pool_scaleT", [DEPTH, 128, 4])
            I["poolrc"] = self.din("poolrc", [128, 4, 16])
            self.y = T(nc.dram_tensor("y", [S, D], F32, kind="ExternalOutput").ap(), "y")
            sc = self.sc = {}
            sc["qT"] = self.dscr("qT", [8, 128, S], BF16)
            sc["qrT"] = self.dscr("qrT", [8, 128, S], BF16)
            sc["kcT"] = self.dscr("kcT", [2, 128, S], BF16)
            sc["vcT"] = self.dscr("vcT", [2, 128, S], BF16)
            sc["kselT"] = self.dscr("kselT", [2, 128, S], BF16)
            sc["kwinT"] = self.dscr("kwinT", [2, 128, S], BF16)
            sc["us5T"] = self.dscr("us5T", [512, S], BF16)
            sc["zs5T"] = self.dscr("zs5T", [512, S], BF16)
            sc["upoolT"] = self.dscr("upoolT", [512, S], F32)
            sc["zpoolT"] = self.dscr("zpoolT", [512, S], BF16)
            sc["v"] = self.dscr("v", [4, 128, NT, 128], BF16)
            sc["zatt"] = self.dscr("zatt", [S, 1024], BF16)
            sc["gl"] = self.dscr("gl", [S, 32], F32)
            sc["mixT"] = self.dscr("mixT", [NT, 128, 16, 128], BF16)
            sc["x1"] = self.dscr("x1", [S, D], F32)
            for l in range(self.nlayers):
                xin = I["x"] if l == 0 else sc["x1"]
                xout = self.y if l == self.nlayers - 1 else sc["x1"]
                if self.phases is None or "A" in self.phases:
                    self.phase_A(l, xin)
                if self.phases is None or "B" in self.phases:
                    self.phase_B(l)
                if self.phases is None or "C" in self.phases:
                    self.phase_C(l)
                if self.phases is None or "D" in self.phases:
                    self.phase_D(l)
                if self.phases is None or "E" in self.phases:
                    self.phase_E(l, xin, xout)
        return nc

    def phase_A(self, l, xin):
        nc, P, b, I, sc = self.nc, self.P, self.b, self.I, self.sc
        xin_ap = xin.t if isinstance(xin, T) else xin
        xin_res = [xin] if isinstance(xin, T) else []
        with contextlib.ExitStack() as st:
            def sb(name, shape, dt):
                return T(st.enter_context(nc.sbuf_tensor(f"A{l}_{name}", shape, dt)), name)

            def ps(name, shape, dt):
                return T(st.enter_context(nc.psum_tensor(f"A{l}_{name}", shape, dt)), name, True)

            hT = sb("hT", [128, 16, S], BF16)
            ident = sb("ident", [128, 128], BF16)
            swapi = sb("swapi", [128, 128], BF16)
            st_outer = st
            st = st_outer.enter_context(contextlib.ExitStack())
            gb = sb("gb", [128, D], F32)
            xt = [sb(f"xt{i}", [128, D], F32) for i in range(3)]
            sq = sb("sq", [128, D], BF16)
            hb = [sb(f"hb{i}", [128, D], BF16) for i in range(3)]
            st1 = [sb(f"st{i}", [128, 4], F32) for i in range(3)]
            ptr = [ps(f"ptr{i}", [128, 8, 128], BF16) for i in range(4)]

            b.dma("sp", ident[:], I["ident"][:, :], [], [ident])
            b.dma("sp", swapi[:], I["swapi"][:, :], [], [swapi])
            b.dma("sp", gb[:], I["norm_pre"][l:l + 1, :].to_broadcast([128, D]), [], [gb])

            for t in range(NT):
                x_t, h_t, s_t = xt[t % 3], hb[t % 3], st1[t % 3]
                b.dma("sp", x_t[:], xin_ap[t * 128:(t + 1) * 128, :], xin_res, [x_t])
                b.act(sq[:], x_t[:], AF.Square, [x_t], [sq, s_t], accum_out=s_t[:, 0:1])
                b.ts("dve", s_t[:, 1:2], s_t[:, 0:1], 1.0 / D, 1e-6, ALU.mult, ALU.add, [s_t], [s_t])
                b.act(s_t[:, 2:3], s_t[:, 1:2], AF.Sqrt, [s_t], [s_t])
                P.op("dve", lambda e, s_t=s_t: e.reciprocal(out=s_t[:, 3:4], in_=s_t[:, 2:3]), [s_t], [s_t])
                b.stt(h_t[:], x_t[:], s_t[:, 3:4], gb[:], ALU.mult, ALU.mult, [x_t, s_t, gb], [h_t])
                for half in range(2):
                    p_t = ptr[(2 * t + half) % 4]
                    for j in range(8):
                        kc = half * 8 + j
                        b.tr(p_t[:, j, :], h_t[:, kc * 128:(kc + 1) * 128], ident[:], [h_t, ident], [p_t])
                    b.evac(hT[:, half * 8:(half + 1) * 8, t * 128:(t + 1) * 128], p_t[:], [p_t], [hT])

            P.flush()
            st.close()
            st = st_outer
            wb = [sb(f"wb{i}", [128, 16, 512], BF16) for i in range(2)]
            rope = [sb(f"rope{i}", [128, 2, 512], F32) for i in range(2)]
            NPA = 5
            NQU = 6
            pacc = [ps(f"pacc{i}", [128, 512], F32) for i in range(NPA)]
            prot = [ps(f"prot{i}", [128, 512], F32) for i in range(2)]
            qun = [sb(f"qun{i}", [128, 512], BF16) for i in range(NQU)]
            t1 = [sb(f"t1{i}", [128, 512], F32) for i in range(2)]
            t2 = [sb(f"t2{i}", [128, 512], F32) for i in range(2)]
            qro = [sb(f"qro{i}", [128, 512], BF16) for i in range(2)]
            o32 = [sb(f"o32{i}", [128, 512], F32) for i in range(2)]
            ogl = [sb(f"ogl{i}", [128, 32], F32) for i in range(2)]
            win = I["w_in"]
            nacc = [0]
            nrot = [0]

            def load_w(gi):
                w_t = wb[gi % 2]
                ncol = 512 if gi < 11 else 32
                src = win[l, :, gi * 512:gi * 512 + ncol].rearrange("(kc p) n -> p kc n", p=128)
                b.dma("pool", w_t[:, :, 0:ncol], src, [], [w_t])

            load_w(0)
            for gi in range(NWG):
                if gi + 1 < NWG:
                    load_w(gi + 1)
                w_t = wb[gi % 2]
                if gi < 8:
                    do_rope = gi in (0, 1, 3)
                    for tq in range(NQ):
                        tsl = slice(tq * 512, (tq + 1) * 512)
                        if do_rope:
                            rp = rope[tq % 2]
                            b.dma("sp", rp[:, 0, :], I["ropec"][:, tsl], [], [rp])
                            b.dma("sp", rp[:, 1, :], I["ropes"][:, tsl], [], [rp])
                        for c in range(4):
                            pa = pacc[nacc[0] % NPA]
                            nacc[0] += 1
                            for kc in range(16):
                                b.mm(pa[:], w_t[:, kc, c * 128:(c + 1) * 128], hT[:, kc, tsl],
                                     kc == 0, kc == 15, [w_t, hT], [pa])
                            ch = gi * 4 + c
                            if gi in (0, 1):
                                qu = qun[nacc[0] % NQU]
                                b.act(qu[:], pa[:], AF.Copy, [pa], [qu])
                                b.dma("sp", sc["qT"][ch, :, tsl], qu[:], [qu], [sc["qT"]])
                                self._rope(b, qu, rp, swapi, prot, t1, t2, qro, nrot,
                                           sc["qrT"], sc["qrT"][ch, :, tsl])
                            elif gi == 2:
                                qu = qun[nacc[0] % NQU]
                                b.act(qu[:], pa[:], AF.Copy, [pa], [qu])
                                dst = sc["kcT"] if c < 2 else sc["vcT"]
                                b.dma("sp", dst[c % 2, :, tsl], qu[:], [qu], [dst])
                            elif gi == 3:
                                qu = qun[nacc[0] % NQU]
                                b.act(qu[:], pa[:], AF.Copy, [pa], [qu])
                                dst = sc["kselT"] if c < 2 else sc["kwinT"]
                                self._rope(b, qu, rp, swapi, prot, t1, t2, qro, nrot, dst, dst[c % 2, :, tsl])
                            elif gi in (4, 5, 7):
                                qu = qun[nacc[0] % NQU]
                                dst = {4: sc["us5T"], 5: sc["zs5T"], 7: sc["zpoolT"]}[gi]
                                b.act(qu[:], pa[:], AF.Copy if gi == 4 else AF.Silu, [pa], [qu])
                                b.dma("sp", dst[c * 128:(c + 1) * 128, tsl], qu[:], [qu], [dst])
                            else:
                                o = o32[nacc[0] % 2]
                                b.act(o[:], pa[:], AF.Copy, [pa], [o])
                                b.dma("sp", sc["upoolT"][c * 128:(c + 1) * 128, tsl], o[:], [o], [sc["upoolT"]])
                else:
                    ncol = 512 if gi < 11 else 32
                    for t in range(NT):
                        tsl = slice(t * 128, (t + 1) * 128)
                        pa = pacc[nacc[0] % NPA]
                        nacc[0] += 1
                        for kc in range(16):
                            b.mm(pa[:, 0:ncol], hT[:, kc, tsl], w_t[:, kc, 0:ncol], kc == 0, kc == 15,
                                 [w_t, hT], [pa])
                        if gi == 8:
                            qu = qun[nacc[0] % NQU]
                            b.act(qu[:], pa[:], AF.Copy, [pa], [qu])
                            for c4 in range(4):
                                b.dma("sp", sc["v"].t[c4, :, t, :], qu[:, c4 * 128:(c4 + 1) * 128], [qu], [sc["v"]])
                        elif gi in (9, 10):
                            qu = qun[nacc[0] % NQU]
                            b.act(qu[:], pa[:], AF.Silu, [pa], [qu])
                            b.dma("sp", sc["zatt"][tsl, (gi - 9) * 512:(gi - 8) * 512], qu[:], [qu], [sc["zatt"]])
                        else:
                            o = ogl[nacc[0] % 2]
                            b.act(o[:], pa[:, 0:32], AF.Sigmoid, [pa], [o])
                            b.dma("sp", sc["gl"][tsl, :], o[:], [o], [sc["gl"]])
            P.flush()

    def _rope(self, b, qu, rp, swapi, prot, t1, t2, qro, nrot, dst_res, dst_ap):
        i = nrot[0]
        nrot[0] += 1
        pr, a1, a2, qo = prot[i % 2], t1[i % 2], t2[i % 2], qro[i % 2]
        b.mm(pr[:], swapi[:], qu[:], True, True, [swapi, qu], [pr])
        b.tt("dve", a1[:], pr[:], rp[:, 1, :], ALU.mult, [pr, rp], [a1])
        b.tt("pool", a2[:], qu[:], rp[:, 0, :], ALU.mult, [qu, rp], [a2])
        b.tt("dve", qo[:], a1[:], a2[:], ALU.add, [a1, a2], [qo])
        b.dma("sp", dst_ap, qo[:], [qo], [dst_res])


    def phase_B(self, l):
        nc, P, b, I, sc = self.nc, self.P, self.b, self.I, self.sc
        with contextlib.ExitStack() as st:
            def sb(name, shape, dt):
                return T(st.enter_context(nc.sbuf_tensor(f"B{l}_{name}", shape, dt)), name)

            def ps(name, shape, dt):
                return T(st.enter_context(nc.psum_tensor(f"B{l}_{name}", shape, dt)), name, True)

            ident = sb("ident", [128, 128], BF16)
            tribc = sb("tribc", [128, 128], BF16)
            tribw = sb("tribw", [128, 128], BF16)
            tbc = sb("tbc", [128, 2560], BF16)
            blkexp = sb("blkexp", [128, S], BF16)
            zl = sb("zl", [128, 128], BF16)
            zr = sb("zr", [128, 512], BF16)
            for (t_, nm) in ((ident, "ident"), (tribc, "tribc"), (tribw, "tribw"), (tbc, "tbc")):
                b.dma("sp", t_[:], I[nm][:, :], [], [t_])
            b.memset("pool", blkexp[:], 0.0, [blkexp])
            b.dma("sp", blkexp[0:64, :], I["blkexp"][:, :], [], [blkexp])
            b.memset("pool", zl[:], 0.0, [zl])
            b.memset("pool", zr[:], 0.0, [zr])

            NS_ = 3
            pS = [ps(f"pS{i}", [128, 512], F32) for i in range(NS_)]
            pO = [[ps(f"pO{a}{j}", [128, 2, 256], F32) for j in range(2)] for a in range(2)]
            pT = [ps(f"pT{i}", [128, 1024], BF16) for i in range(1)]

            qT = sb("qT", [128, 4, S], BF16)
            qrT = sb("qrT", [128, 4, S], BF16)
            kselT = sb("kselT", [128, S], BF16)
            kwinT = sb("kwinT", [128, S], BF16)
            vsel = sb("vsel", [128, NT, 129], BF16)
            vwin = sb("vwin", [128, NT, 129], BF16)
            kin = sb("kin", [128, S], BF16)
            vin = sb("vin", [128, S], BF16)
            w1k = sb("w1k", [128, 32, 256], BF16)
            w1v = sb("w1v", [128, 32, 256], BF16)
            w2k = sb("w2k", [128, 2, 128], BF16)
            w2v = sb("w2v", [128, 2, 128], BF16)
            posk = sb("posk", [128, 32], BF16)
            posv = sb("posv", [128, 32], BF16)
            hk = sb("hk", [128, 2, 256], BF16)
            hv = sb("hv", [128, 2, 256], BF16)
            cbias = sb("cbias", [128, 4], F32)
            kcmp = sb("kcmp", [128, 256], BF16)
            vcx = sb("vcx", [128, 2, 193], BF16)
            NP_ = 5
            pt = [sb(f"pt{i}", [128, 512], BF16) for i in range(NP_)]
            zat = [sb(f"zat{i}", [128, 4, 512], BF16) for i in range(2)]
            glt = [sb(f"glt{i}", [128, 4, 32], F32) for i in range(2)]
            iA = [sb(f"iA{i}", [128, 4, 64], F32) for i in range(2)]
            iB = [sb(f"iB{i}", [128, 4, 64], F32) for i in range(2)]
            accO = [sb(f"accO{i}", [128, 4, 4, 128], F32) for i in range(2)]
            impacc = [sb(f"impacc{i}", [128, 4, 64], F32) for i in range(2)]
            selbT = [sb(f"selbT{i}", [128, 512], BF16) for i in range(2)]
            imp2 = sb("imp2", [128, 64], F32)
            impw = sb("impw", [128, 64], F32)
            m8 = sb("m8", [128, 16], F32)
            s01 = sb("s01", [128, 64], F32)
            sb128 = sb("sb128", [128, 128], BF16)
            dn = [sb(f"dn{i}", [128, 8], F32) for i in range(4)]
            attb = [sb(f"attb{i}", [128, 512], BF16) for i in range(2)]
            attT = [sb(f"attT{i}", [128, 4, 128], BF16) for i in range(2)]

            b.memset("pool", sb128[:], 0.0, [sb128])
            b.memset("pool", hk[:], 0.0, [hk])
            b.memset("pool", hv[:], 0.0, [hv])
            b.memset("pool", kcmp[:], 0.0, [kcmp])
            b.memset("pool", vsel[:, :, 128:129], 1.0, [vsel])
            b.memset("pool", vwin[:, :, 128:129], 1.0, [vwin])
            b.dma("sp", vcx[:, :, 128:193], I["ovl1"][:, :, :], [], [vcx])

            cnt = {"S": 0, "O": 0, "P": 0, "T": 0, "dn": 0}

            for g in range(2):
                b.dma("sp", qT[:], sc["qT"].t[g * 4:(g + 1) * 4].rearrange("r p n -> p r n"), [sc["qT"]], [qT])
                b.dma("sp", qrT[:], sc["qrT"].t[g * 4:(g + 1) * 4].rearrange("r p n -> p r n"), [sc["qrT"]], [qrT])
                b.dma("sp", kselT[:], sc["kselT"].t[g], [sc["kselT"]], [kselT])
                b.dma("sp", kwinT[:], sc["kwinT"].t[g], [sc["kwinT"]], [kwinT])
                b.dma("sp", kin[:], sc["kcT"].t[g], [sc["kcT"]], [kin])
                b.dma("sp", vin[:], sc["vcT"].t[g], [sc["vcT"]], [vin])
                b.dma("sp", vsel[:, :, 0:128], sc["v"].t[g], [sc["v"]], [vsel])
                b.dma("sp", vwin[:, :, 0:128], sc["v"].t[2 + g], [sc["v"]], [vwin])
                if g == 0:
                    b.dma("pool", w1k[:], I["cmp_w1_k"][l], [], [w1k], max_dma_last_dim=4096)
                    b.dma("pool", w1v[:], I["cmp_w1_v"][l], [], [w1v], max_dma_last_dim=4096)
                    b.dma("pool", w2k[:], I["cmp_w2_k"][l].rearrange("(c p) d -> p c d", p=128), [], [w2k])
                    b.dma("pool", w2v[:], I["cmp_w2_v"][l].rearrange("(c p) d -> p c d", p=128), [], [w2v])
                    b.dma("pool", posk[:], I["cmp_posT_k"][l], [], [posk])
                    b.dma("pool", posv[:], I["cmp_posT_v"][l], [], [posv])
                    for kv, (w1, pos) in enumerate(((w1k, posk), (w1v, posv))):
                        for hc in range(2):
                            pa = pS[cnt["S"] % NS_]
                            cnt["S"] += 1
                            for li in range(32):
                                b.mm(pa[:, 0:1], w1[:, li, hc * 128:(hc + 1) * 128], pos[:, li:li + 1],
                                     li == 0, li == 31, [w1, pos], [pa])
                            b.copy("dve", cbias[:, kv * 2 + hc:kv * 2 + hc + 1], pa[:, 0:1], [pa], [cbias])

                if self.bstop <= 1:
                    break
                for kv, (src, w1, w2, hbuf) in enumerate(((kin, w1k, w2k, hk), (vin, w1v, w2v, hv))):
                    srcv = src[:].rearrange("p (n s) -> p n s", s=16)
                    for hc in range(2):
                        pa = pS[cnt["S"] % NS_]
                        cnt["S"] += 1
                        for li in range(32):
                            rhs = srcv[:, 0:255, li] if li < 16 else srcv[:, 1:256, li - 16]
                            b.mm(pa[:, 0:255], w1[:, li, hc * 128:(hc + 1) * 128], rhs, li == 0, li == 31,
                                 [w1, src], [pa])
                        b.act(hbuf[:, hc, 0:255], pa[:, 0:255], AF.Silu, [pa, cbias], [hbuf],
                              bias=cbias[:, kv * 2 + hc:kv * 2 + hc + 1])
                    if kv == 0:
                        pa = pS[cnt["S"] % NS_]
                        cnt["S"] += 1
                        for hc in range(2):
                            b.mm(pa[:, 0:255], w2[:, hc, :], hbuf[:, hc, 0:255], hc == 0, hc == 1, [w2, hbuf], [pa])
                        b.copy("dve", kcmp[:, 0:255], pa[:, 0:255], [pa], [kcmp])
                    else:
                        for a in range(2):
                            pa = pS[cnt["S"] % NS_]
                            cnt["S"] += 1
                            for hc in range(2):
                                b.mm(pa[:, 0:128], hbuf[:, hc, a * 128:(a + 1) * 128], w2[:, hc, :], hc == 0, hc == 1,
                                     [w2, hbuf], [pa])
                            b.copy("dve", vcx[:, a, 0:128], pa[:, 0:128], [pa], [vcx])

                if self.bstop <= 2:
                    break
                for i in range(NQ):
                    if self.bstop <= 7 and i >= 1:
                        break
                    q0 = i * 512
                    za, gt, A_, B_ = zat[i % 2], glt[i % 2], iA[i % 2], iB[i % 2]
                    ao, ia, sbt = accO[i % 2], impacc[i % 2], selbT[i % 2]

                    def tile_loads(ii):
                        qq = ii * 512
                        b.dma("sp", zat[ii % 2][:],
                              sc["zatt"].t[qq:qq + 512, g * 512:(g + 1) * 512].rearrange("(b p) c -> p b c", p=128),
                              [sc["zatt"]], [zat[ii % 2]])
                        b.dma("sp", glt[ii % 2][:], sc["gl"].t[qq:qq + 512, :].rearrange("(b p) c -> p b c", p=128),
                              [sc["gl"]], [glt[ii % 2]])
                        b.dma("sp", iA[ii % 2][:], I["impA"][ii], [], [iA[ii % 2]])
                        b.dma("sp", iB[ii % 2][:], I["impB"][ii], [], [iB[ii % 2]])

                    if i == 0:
                        tile_loads(0)

                    def norm_scales(Oset, width, gcol):
                        d_ = dn[cnt["dn"] % 4]
                        cnt["dn"] += 1
                        for j in range(2):
                            b.ts("dve", d_[:, 2 * j:2 * j + 2], Oset[j][:, :, width - 1], 1e-30, None, ALU.max, None,
                                 [Oset[j]], [d_])
                        P.op("dve", lambda e, d_=d_: e.reciprocal(out=d_[:, 0:4], in_=d_[:, 0:4]), [d_], [d_])
                        b.tt("dve", d_[:, 4:8], d_[:, 0:4], gt[:, :, gcol], ALU.mult, [d_, gt], [d_])
                        return d_

                    if self.bstop <= 2.5:
                        break
                    for r in range(4):
                        Oset = pO[cnt["O"] % 2]
                        cnt["O"] += 1
                        ets = []
                        for a in range(2):
                            u0 = q0 - 2048 * a
                            if u0 + 511 < 31:
                                continue
                            pa = pS[cnt["S"] % NS_]
                            cnt["S"] += 1
                            partial = u0 < 2063
                            b.mm(pa[:], kcmp[:, a * 128:(a + 1) * 128], qT[:, r, q0:q0 + 512], True, not partial,
                                 [kcmp, qT], [pa])
                            if partial:
                                b.mm(pa[:], ident[:], tbc[:, u0:u0 + 512], False, True, [ident, tbc], [pa])
                            e_ = pt[cnt["P"] % NP_]
                            cnt["P"] += 1
                            b.act(e_[:], pa[:], AF.Exp, [pa], [e_], scale=SCALE)
                            ets.append((a, e_))
                        if self.bstop <= 2.6:
                            continue
                        for bb in range(4):
                            for k_, (a, e_) in enumerate(ets):
                                b.mm(Oset[bb // 2][:, bb % 2, 0:193], e_[:, bb * 128:(bb + 1) * 128], vcx[:, a, :],
                                     k_ == 0, k_ == len(ets) - 1, [e_, vcx], [Oset[bb // 2]], skip_group_check=True)
                        if self.bstop <= 2.7:
                            continue
                        d_ = norm_scales(Oset, 193, (g * 4 + r) * 3 + 0)
                        if self.bstop <= 2.8:
                            continue
                        for bb in range(4):
                            Ob = Oset[bb // 2]
                            if self.bstop == 2.95:
                                pass
                            elif r == 0:
                                b.ts("dve", ia[:, bb, :], Ob[:, bb % 2, 128:192], d_[:, bb:bb + 1], None, ALU.mult, None,
                                     [Ob, d_], [ia])
                            else:
                                b.stt(ia[:, bb, :], Ob[:, bb % 2, 128:192], d_[:, bb:bb + 1], ia[:, bb, :],
                                      ALU.mult, ALU.add, [Ob, d_, ia], [ia])
                            if self.bstop == 2.9:
                                continue
                            b.act(ao[:, bb, r, :], Ob[:, bb % 2, 0:128], AF.Identity, [Ob, d_], [ao],
                                  scale=d_[:, 4 + bb:5 + bb])

                    if i + 1 < NQ:
                        tile_loads(i + 1)
                    if self.bstop <= 3:
                        break
                    for bb in range(4):
                        b.tt("dve", imp2[:], ia[:, bb, :], A_[:, bb, :], ALU.mult, [ia, A_], [imp2])
                        b.tt("dve", imp2[:], imp2[:], B_[:, bb, :], ALU.add, [imp2, B_], [imp2])
                        P.op("dve", lambda e: e.max(out=m8[:, 0:8], in_=imp2[:]), [imp2], [m8])
                        P.op("dve", lambda e: e.match_replace(out=impw[:], in_to_replace=m8[:, 0:8], in_values=imp2[:],
                                                              imm_value=-2.0), [imp2, m8], [impw])
                        P.op("dve", lambda e: e.max(out=m8[:, 8:16], in_=impw[:]), [impw], [m8])
                        b.ts("dve", s01[:], imp2[:], m8[:, 15:16], None, ALU.is_ge, None, [imp2, m8], [s01])
                        b.ts("dve", sb128[:, 0:64], s01[:], -NEGB, NEGB, ALU.mult, ALU.add, [s01], [sb128])
                        ptr_ = pT[0]
                        cnt["T"] += 1
                        b.tr(ptr_[:, 0:128], sb128[:], ident[:], [sb128, ident], [ptr_])
                        b.copy("dve", sbt[:, bb * 128:(bb + 1) * 128], ptr_[:, 0:128], [ptr_], [sbt])

                    if self.bstop <= 4:
                        break
                    def branch(r, kts, rng, masks, kT, vext, sel, gcol):
                        Oset = pO[cnt["O"] % 2]
                        cnt["O"] += 1
                        for j in range(2):
                            b.mm(Oset[j][:].rearrange("p a c -> p (a c)"), zl[:], zr[:], True, False, [zl, zr], [Oset[j]],
                                 skip_group_check=True)
                        last_kt = {}
                        for kt in kts:
                            lo, hi = rng(kt)
                            for bb in range(lo, hi + 1):
                                last_kt[bb] = kt

                        def qk(kt):
                            lo, hi = rng(kt)
                            c0, c1 = lo * 128, (hi + 1) * 128
                            pa = pS[cnt["S"] % NS_]
                            cnt["S"] += 1
                            mms = [(pa[:, c0:c1], kT[:, kt * 128:(kt + 1) * 128], qrT[:, r, q0 + c0:q0 + c1], [kT, qrT])]
                            if sel:
                                mms.append((pa[:, c0:c1], blkexp[:, kt * 128:(kt + 1) * 128], sbt[:, c0:c1], [blkexp, sbt]))
                            for (bb, tri) in masks(kt):
                                mms.append((pa[:, bb * 128:(bb + 1) * 128], ident[:], tri[:], [ident, tri]))
                            for n_, (o_, l_, r_, rd) in enumerate(mms):
                                b.mm(o_, l_, r_, n_ == 0, n_ == len(mms) - 1, rd, [pa])
                            return pa, lo, hi

                        pend = [qk(kts[0])]
                        if len(kts) > 1:
                            pend.append(qk(kts[1]))
                        for n_, kt in enumerate(kts):
                            pa, lo, hi = pend.pop(0)
                            if n_ + 2 < len(kts):
                                pend.append(qk(kts[n_ + 2]))
                            c0, c1 = lo * 128, (hi + 1) * 128
                            p_ = pt[cnt["P"] % NP_]
                            cnt["P"] += 1
                            b.act(p_[:, c0:c1], pa[:, c0:c1], AF.Exp, [pa], [p_], scale=SCALE)
                            for bb in range(lo, hi + 1):
                                b.mm(Oset[bb // 2][:, bb % 2, 0:129], p_[:, bb * 128:(bb + 1) * 128], vext[:, kt, :],
                                     False, last_kt[bb] == kt, [p_, vext], [Oset[bb // 2]], skip_group_check=True)
                        d_ = norm_scales(Oset, 129, gcol)
                        for bb in range(4):
                            Ob = Oset[bb // 2]
                            b.stt(ao[:, bb, r, :], Ob[:, bb % 2, 0:128], d_[:, 4 + bb:5 + bb], ao[:, bb, r, :],
                                  ALU.mult, ALU.add, [Ob, d_, ao], [ao])

                    for r in range(4):
                        kts = list(range(max(0, 4 * i - 4), 4 * i + 4))
                        rng = lambda kt: (max(0, kt - 4 * i), min(3, kt - 4 * i + 4))

                        def masks(kt):
                            m = []
                            if 0 <= kt - 4 * i <= 3:
                                m.append((kt - 4 * i, tribc))
                            if 0 <= kt + 4 - 4 * i <= 3:
                                m.append((kt + 4 - 4 * i, tribw))
                            return m
                        branch(r, kts, rng, masks, kwinT, vwin, False, (g * 4 + r) * 3 + 2)
                    if self.bstop <= 5:
                        break
                    for r in range(4):
                        kts = list(range(0, 4 * i + 4))
                        rng = lambda kt: (max(0, kt - 4 * i), 3)
                        masks = lambda kt: [(kt - 4 * i, tribc)] if kt >= 4 * i else []
                        branch(r, kts, rng, masks, kselT, vsel, True, (g * 4 + r) * 3 + 1)

                    if self.bstop <= 6:
                        break
                    for bb in range(4):
                        ab, aT = attb[bb % 2], attT[bb % 2]
                        b.tt("pool", ab[:], ao[:, bb, :, :].rearrange("p r d -> p (r d)"), za[:, bb, :], ALU.mult,
                             [ao, za], [ab])
                        ptr_ = pT[0]
                        cnt["T"] += 1
                        for r in range(4):
                            b.tr(ptr_[:, r * 128:(r + 1) * 128], ab[:, r * 128:(r + 1) * 128], ident[:], [ab, ident], [ptr_])
                        b.evac(aT[:].rearrange("p r q -> p (r q)"), ptr_[:, 0:512], [ptr_], [aT])
                        c0 = q0 + bb * 128
                        b.dma("sp", sc["mixT"].t[c0 // 128, :, g * 4:(g + 1) * 4, :], aT[:], [aT], [sc["mixT"]])
                if self.bstop <= 8:
                    break
            P.flush()


    def phase_C(self, l):
        nc, P, b, I, sc = self.nc, self.P, self.b, self.I, self.sc
        TWO_PI = 2.0 * math.pi
        with contextlib.ExitStack() as st:
            def sb(name, shape, dt):
                return T(st.enter_context(nc.sbuf_tensor(f"C{l}_{name}", shape, dt)), name)

            def ps(name, shape, dt):
                return T(st.enter_context(nc.psum_tensor(f"C{l}_{name}", shape, dt)), name, True)

            ident = sb("ident", [128, 128], BF16)
            identf = sb("identf", [128, 128], F32)
            swapf = sb("swapf", [128, 128], F32)
            sgn = sb("sgn", [128, 2], F32)
            b.dma("sp", ident[:], I["ident"][:, :], [], [ident])
            b.dma("pool", identf[:], I["ident"][:, :], [], [identf])
            b.dma("pool", swapf[:], I["swapi"][:, :], [], [swapf])
            b.dma("sp", sgn[:], I["sgn"][:, :], [], [sgn])

            pw1 = sb("pw1", [128, 12, 32], F32)
            pw2 = sb("pw2", [128, 12, 32], F32)
            ccpad = sb("ccpad", [128, 32, 128], BF16)
            lhsB = sb("lhsB", [128, 32, 128], BF16)
            dcol = sb("dcol", [128, 4], F32)
            glub = sb("glub", [128, 8], F32)
            yT = sb("yT", [128, 4, S], BF16)
            pX = [ps(f"pX{i}", [128, 512], F32) for i in range(4)]
            st_outer = st
            st = st_outer.enter_context(contextlib.ExitStack())
            ar = sb("ar", [128, 32], F32)
            ai = sb("ai", [128, 32], F32)
            dt_ = sb("dt", [128, 32], F32)
            tA = sb("tA", [128, 32], F32)
            tB = sb("tB", [128, 32], F32)
            tC = sb("tC", [128, 32], F32)
            th = sb("th", [128, 32], F32)
            mag = sb("mag", [128, 32], F32)
            abr = sb("abr", [128, 32], F32)
            abi = sb("abi", [128, 32], F32)
            cr = sb("cr", [128, 32], F32)
            ci = sb("ci", [128, 32], F32)
            for h in range(2):
                b.dma("sp", ar[h * 64:(h + 1) * 64, :], I["s5_a_reT"][l], [], [ar])
                b.dma("sp", ai[h * 64:(h + 1) * 64, :], I["s5_a_imT"][l], [], [ai])
            b.dma("sp", dt_[:], I["s5_log_dt"][l:l + 1, :].to_broadcast([128, 32]), [], [dt_])
            b.act(dt_[:], dt_[:], AF.Exp, [dt_], [dt_])
            b.tt("dve", tA[:], ar[:], dt_[:], ALU.mult, [ar, dt_], [tA])
            b.act(mag[:], tA[:], AF.Exp, [tA], [mag])
            b.tt("dve", th[:], ai[:], dt_[:], ALU.mult, [ai, dt_], [th])

            def sin_of(dst, src, shift):
                b.ts("dve", tB[:], src[:], shift, None, ALU.add, None, [src], [tB])
                b.copy("dve", tC[:], tB[:], [tB], [tC])
                for m in range(1, 9):
                    b.ts("dve", tA[:], tB[:], (2 * m - 1) * math.pi, TWO_PI, ALU.is_gt, ALU.mult, [tB], [tA])
                    b.tt("dve", tC[:], tC[:], tA[:], ALU.subtract, [tC, tA], [tC])
                b.act(dst[:], tC[:], AF.Sin, [tC], [dst])

            sin_of(abi, th, 0.0)
            sin_of(abr, th, 0.5 * math.pi)
            b.tt("dve", abr[:], abr[:], mag[:], ALU.mult, [abr, mag], [abr])
            b.tt("dve", abi[:], abi[:], mag[:], ALU.mult, [abi, mag], [abi])
            b.tt("dve", tA[:], ar[:], ar[:], ALU.mult, [ar], [tA])
            b.tt("dve", tB[:], ai[:], ai[:], ALU.mult, [ai], [tB])
            b.tt("dve", tA[:], tA[:], tB[:], ALU.add, [tA, tB], [tA])
            P.op("dve", lambda e: e.reciprocal(out=tA[:], in_=tA[:]), [tA], [tA])
            b.ts("dve", tB[:], abr[:], -1.0, None, ALU.add, None, [abr], [tB])
            b.tt("dve", cr[:], tB[:], ar[:], ALU.mult, [tB, ar], [cr])
            b.tt("dve", tC[:], abi[:], ai[:], ALU.mult, [abi, ai], [tC])
            b.tt("dve", cr[:], cr[:], tC[:], ALU.add, [cr, tC], [cr])
            b.tt("dve", cr[:], cr[:], tA[:], ALU.mult, [cr, tA], [cr])
            b.tt("dve", ci[:], abi[:], ar[:], ALU.mult, [abi, ar], [ci])
            b.tt("dve", tC[:], tB[:], ai[:], ALU.mult, [tB, ai], [tC])
            b.tt("dve", ci[:], ci[:], tC[:], ALU.subtract, [ci, tC], [ci])
            b.tt("dve", ci[:], ci[:], tA[:], ALU.mult, [ci, tA], [ci])
            b.ts("dve", ci[:], ci[:], sgn[:, 1:2], None, ALU.mult, None, [ci, sgn], [ci])

            b.copy("dve", pw1[:, 0, :], abr[:], [abr], [pw1])
            b.ts("dve", pw2[:, 0, :], abi[:], sgn[:, 0:1], None, ALU.mult, None, [abi, sgn], [pw2])
            for lv in range(1, 12):
                b.tt("dve", tA[:], pw1[:, lv - 1, :], pw1[:, lv - 1, :], ALU.mult, [pw1], [tA])
                b.tt("dve", tB[:], pw2[:, lv - 1, :], pw2[:, lv - 1, :], ALU.mult, [pw2], [tB])
                b.tt("dve", pw1[:, lv, :], tA[:], tB[:], ALU.subtract, [tA, tB], [pw1])
                b.tt("dve", tC[:], pw1[:, lv - 1, :], pw2[:, lv - 1, :], ALU.mult, [pw1, pw2], [tC])
                b.ts("dve", pw2[:, lv, :], tC[:], 2.0, None, ALU.mult, None, [tC], [pw2])

            bri = sb("bri", [128, 32, 16], F32)
            bir = sb("bir", [128, 32, 16], F32)
            ccs = sb("ccs", [128, 32, 16], F32)
            bbpad = sb("bbpad", [128, 32, 128], BF16)
            b.dma("sp", bri[0:64], I["s5_b_reT"][l], [], [bri])
            b.dma("sp", bri[64:128], I["s5_b_imT"][l], [], [bri])
            b.dma("sp", bir[0:64], I["s5_b_imT"][l], [], [bir])
            b.dma("sp", bir[64:128], I["s5_b_reT"][l], [], [bir])
            b.dma("sp", ccs[0:64], I["s5_c_reT"][l], [], [ccs])
            b.dma("sp", ccs[64:128], I["s5_c_imT"][l], [], [ccs])
            b.dma("sp", dcol[:], I["s5_dT"][l], [], [dcol])
            b.dma("sp", glub[:], I["s5_glu_bT"][l], [], [glub])
            b.memset("pool", bbpad[:], 0.0, [bbpad])
            b.memset("pool", ccpad[:], 0.0, [ccpad])
            for g in range(32):
                c0 = 16 * (g % 8)
                b.ts("dve", bri[:, g, :], bri[:, g, :], cr[:, g:g + 1], None, ALU.mult, None, [bri, cr], [bri])
                b.stt(bbpad[:, g, c0:c0 + 16], bir[:, g, :], ci[:, g:g + 1], bri[:, g, :], ALU.mult, ALU.add,
                      [bir, ci, bri], [bbpad])
            for k in range(8):
                b.ts("dve", ccpad[:, k::8, 16 * k:16 * k + 16], ccs[:, k::8, :], sgn[:, 0:1], None, ALU.mult, None,
                     [ccs, sgn], [ccpad])
            pT = [ps(f"pT{i}", [128, 8, 128], BF16) for i in range(2)]
            for q in range(4):
                p_ = pT[q % 2]
                for j in range(8):
                    b.tr(p_[:, j, :], bbpad[:, q * 8 + j, :], ident[:], [bbpad, ident], [p_])
                b.evac(lhsB[:, q * 8:(q + 1) * 8, :], p_[:], [p_], [lhsB])

            P.flush()
            st.close()
            st = st_outer.enter_context(contextlib.ExitStack())
            ut = [sb(f"ut{i}", [128, S], BF16) for i in range(1)]
            Hs = [sb(f"H{i}", [128, S], BF16) for i in range(8)]
            Mms = [sb(f"Mm{i}", [128, 8, 12, 128], BF16) for i in range(2)]
            ya = [sb(f"ya{i}", [128, 512], F32) for i in range(2)]
            yb = [sb(f"yb{i}", [128, 512], F32) for i in range(2)]
            mt1 = [sb(f"mt1{i}", [128, 12, 128], BF16) for i in range(2)]
            mt2 = [sb(f"mt2{i}", [128, 12, 128], BF16) for i in range(2)]
            nx = [0]
            nsc = [0]

            def nextp():
                p_ = pX[nx[0] % 4]
                nx[0] += 1
                return p_

            idB = identf[:].rearrange("p (o c) -> p o c", o=1).to_broadcast([128, 12, 128])
            swB = swapf[:].rearrange("p (o c) -> p o c", o=1).to_broadcast([128, 12, 128])

            def build_M(jj):
                Mm_ = Mms[jj % 2]
                for gi in range(8):
                    g = jj * 8 + gi
                    m1, m2 = mt1[gi % 2], mt2[gi % 2]
                    b.tt("pool", m1[:], idB, pw1[:, :, g:g + 1].to_broadcast([128, 12, 128]), ALU.mult,
                         [identf, pw1], [m1])
                    b.tt("pool", m2[:], swB, pw2[:, :, g:g + 1].to_broadcast([128, 12, 128]), ALU.mult,
                         [swapf, pw2], [m2])
                    b.tt("pool", Mm_[:, gi, :, :], m1[:], m2[:], ALU.add, [m1, m2], [Mm_])

            build_M(0)
            for j in range(4):
                u_t = ut[0]
                Mm = Mms[j % 2]
                b.dma("sp", u_t[:], sc["us5T"].t[j * 128:(j + 1) * 128, :], [sc["us5T"]], [u_t])
                for gi in range(8):
                    g = j * 8 + gi
                    for tt_ in range(8):
                        p_ = nextp()
                        b.mm(p_[:], lhsB[:, g, :], u_t[:, tt_ * 512:(tt_ + 1) * 512], True, True, [lhsB, u_t], [p_])
                        b.evac(Hs[gi][:, tt_ * 512:(tt_ + 1) * 512], p_[:], [p_], [Hs[gi]])

                if j + 1 < 4:
                    build_M(j + 1)

                def level(lv, down):
                    s_ = 1 << lv
                    n2 = S // (2 * s_)
                    for gi in range(8):
                        Hv = Hs[gi][:].rearrange("p (k s) -> p k s", s=2 * s_)
                        if not down:
                            k0, k1, doff, soff, dk = 0, n2, 2 * s_ - 1, s_ - 1, 0
                        else:
                            k0, k1, doff, soff, dk = 0, n2 - 1, s_ - 1, 2 * s_ - 1, 1
                        kk = k0
                        while kk < k1:
                            ke = min(kk + 512, k1)
                            p_ = nextp()
                            dst = Hv[:, kk + dk:ke + dk, doff]
                            nsc[0] += 1
                            if nsc[0] % 2 == 0:
                                b.mm(p_[:, 0:ke - kk], Mm[:, gi, lv, :], Hv[:, kk:ke, soff], True, True, [Mm, Hs[gi]], [p_])
                                b.tt("dve", dst, p_[:, 0:ke - kk], dst, ALU.add, [p_, Hs[gi]], [Hs[gi]])
                            else:
                                b.mm(p_[:, 0:ke - kk], Mm[:, gi, lv, :], Hv[:, kk:ke, soff], True, False, [Mm, Hs[gi]], [p_])
                                b.mm(p_[:, 0:ke - kk], ident[:], dst, False, True, [ident, Hs[gi]], [p_])
                                b.copy("act", dst, p_[:, 0:ke - kk], [p_], [Hs[gi]])
                            kk = ke

                for lv in range(12):
                    level(lv, False)
                for lv in range(10, -1, -1):
                    level(lv, True)

                for tt_ in range(8):
                    tsl = slice(tt_ * 512, (tt_ + 1) * 512)
                    p_ = nextp()
                    for gi in range(8):
                        b.mm(p_[:], ccpad[:, j * 8 + gi, :], Hs[gi][:, tsl], gi == 0, gi == 7, [ccpad, Hs[gi]], [p_])
                    a_, b_ = ya[tt_ % 2], yb[tt_ % 2]
                    b.stt(a_[:], u_t[:, tsl], dcol[:, j:j + 1], p_[:], ALU.mult, ALU.add, [u_t, dcol, p_], [a_])
                    b.tt("pool", b_[:], a_[:], a_[:], ALU.mult, [a_], [b_])
                    b.ts("pool", b_[:], b_[:], 0.044715, 1.0, ALU.mult, ALU.add, [b_], [b_])
                    b.tt("pool", b_[:], b_[:], a_[:], ALU.mult, [b_, a_], [b_])
                    b.act(b_[:], b_[:], AF.Sigmoid, [b_], [b_], scale=1.5957691216057308)
                    b.tt("pool", yT[:, j, tsl], a_[:], b_[:], ALU.mult, [a_, b_], [yT])

            P.flush()
            st.close()
            st = st_outer.enter_context(contextlib.ExitStack())
            wg = sb("wg", [128, 4, 1024], BF16)
            b.dma("pool", wg[:], I["s5_glu_w"][l].rearrange("(j p) e -> p j e", p=128), [], [wg])
            zt = [sb(f"zt{i}", [128, 512], BF16) for i in range(2)]
            sg = [sb(f"sg{i}", [128, 512], F32) for i in range(2)]
            og = [sb(f"og{i}", [128, 512], BF16) for i in range(2)]
            n_ = 0
            for c in range(4):
                for tt_ in range(8):
                    tsl = slice(tt_ * 512, (tt_ + 1) * 512)
                    z_, s_g, o_ = zt[n_ % 2], sg[n_ % 2], og[n_ % 2]
                    n_ += 1
                    b.dma("sp", z_[:], sc["zs5T"].t[c * 128:(c + 1) * 128, tsl], [sc["zs5T"]], [z_])
                    pa_, pg_ = nextp(), nextp()
                    for jj in range(4):
                        b.mm(pg_[:], wg[:, jj, 512 + c * 128:512 + (c + 1) * 128], yT[:, jj, tsl], jj == 0, jj == 3,
                             [wg, yT], [pg_])
                    for jj in range(4):
                        b.mm(pa_[:], wg[:, jj, c * 128:(c + 1) * 128], yT[:, jj, tsl], jj == 0, jj == 3, [wg, yT], [pa_])
                    b.act(s_g[:], pg_[:], AF.Sigmoid, [pg_, glub], [s_g], bias=glub[:, 4 + c:5 + c])
                    b.stt(s_g[:], pa_[:], glub[:, c:c + 1], s_g[:], ALU.add, ALU.mult, [pa_, glub, s_g], [s_g])
                    b.tt("pool", o_[:], s_g[:], z_[:], ALU.mult, [s_g, z_], [o_])
                    b.dma("sp", sc["mixT"].t[4 * tt_:4 * tt_ + 4, :, 8 + c, :].rearrange("a p q -> p a q"),
                          o_[:].rearrange("p (a q) -> p a q", a=4), [o_], [sc["mixT"]])
            P.flush()

    def phase_D(self, l):
        nc, P, b, I, sc = self.nc, self.P, self.b, self.I, self.sc
        with contextlib.ExitStack() as st:
            def sb(name, shape, dt):
                return T(st.enter_context(nc.sbuf_tensor(f"D{l}_{name}", shape, dt)), name)

            def ps(name, shape, dt):
                return T(st.enter_context(nc.psum_tensor(f"D{l}_{name}", shape, dt)), name, True)

            pw = sb("pw", [128, 4, 128], BF16)
            pscale = sb("pscale", [128, 4], F32)
            rc = sb("rc", [128, 4, 16], F32)
            b.dma("pool", pw[:], I["pool_w"][l].rearrange("k c d -> c k d"), [], [pw])
            b.dma("sp", pscale[:], I["pool_scaleT"][l], [], [pscale])
            b.dma("sp", rc[:], I["poolrc"][:, :, :], [], [rc])
            u = [sb(f"u{i}", [128, S], F32) for i in range(2)]
            sa = sb("sa", [128, S], F32)
            sbb = sb("sbb", [128, S], F32)
            pl = sb("pl", [128, S], BF16)
            sc2 = sb("sc2", [128, S // 2], F32)
            sa_lo, sa_hi, sb_lo, sb_hi, pl_lo, pl_hi = (Res() for _ in range(6))
            zt = [sb(f"zt{i}", [128, 512], BF16) for i in range(4)]
            og = [sb(f"og{i}", [128, 512], BF16) for i in range(4)]
            pX = [ps(f"pX{i}", [128, 512], F32) for i in range(4)]
            n_ = 0

            def load_u(kk):
                b.dma("sp", u[kk % 2][:], sc["upoolT"].t[kk * 128:(kk + 1) * 128, :], [sc["upoolT"]], [u[kk % 2]])

            load_u(0)
            for k in range(4):
                w = 2 << k
                u_ = u[k % 2]
                if k + 1 < 4:
                    load_u(k + 1)
                H_ = S // 2
                cur, cur_r = u_, (u_, u_)
                bufs = [(sa, (sa_lo, sa_hi)), (sbb, (sb_lo, sb_hi))]
                sh = 1
                step = 0
                while sh < w:
                    nxt, nxt_r = bufs[step % 2]
                    b.copy("dve", nxt[:, 0:sh], cur[:, 0:sh], [cur_r[0]], [nxt_r[0]])
                    b.tt("dve", nxt[:, sh:H_], cur[:, sh:H_], cur[:, 0:H_ - sh], ALU.add, [cur_r[0]], [nxt_r[0]])
                    b.tt("pool", nxt[:, H_:S], cur[:, H_:S], cur[:, H_ - sh:S - sh], ALU.add, [cur_r[0], cur_r[1]], [nxt_r[1]])
                    cur, cur_r = nxt, nxt_r
                    sh *= 2
                    step += 1
                b.tt("dve", pl[:, 0:16], cur[:, 0:16], rc[:, k, :], ALU.mult, [cur_r[0], rc], [pl_lo])
                b.tt("dve", pl[:, 0:16], pl[:, 0:16], u_[:, 0:16], ALU.subtract, [pl_lo, u_], [pl_lo])
                b.stt(pl[:, 16:H_], cur[:, 16:H_], 1.0 / w, u_[:, 16:H_], ALU.mult, ALU.subtract, [cur_r[0], u_], [pl_lo])
                b.ts("pool", sc2[:], cur[:, H_:S], 1.0 / w, 0.0, ALU.mult, ALU.add, [cur_r[1]], [sc2])
                b.tt("pool", pl[:, H_:S], sc2[:], u_[:, H_:S], ALU.subtract, [sc2, u_], [pl_hi])
                for tt_ in range(8):
                    tsl = slice(tt_ * 512, (tt_ + 1) * 512)
                    z_, o_ = zt[n_ % 4], og[n_ % 4]
                    p_ = pX[n_ % 4]
                    n_ += 1
                    b.dma("sp", z_[:], sc["zpoolT"].t[k * 128:(k + 1) * 128, tsl], [sc["zpoolT"]], [z_])
                    b.mm(p_[:], pw[:, k, :], pl[:, tsl], True, True, [pw, pl_lo if tt_ < 4 else pl_hi], [p_])
                    b.stt(o_[:], p_[:], pscale[:, k:k + 1], z_[:], ALU.mult, ALU.mult, [p_, pscale, z_], [o_])
                    b.dma("sp", sc["mixT"].t[4 * tt_:4 * tt_ + 4, :, 12 + k, :].rearrange("a p q -> p a q"),
                          o_[:].rearrange("p (a q) -> p a q", a=4), [o_], [sc["mixT"]])
            P.flush()

    def phase_E(self, l, xin, xout):
        nc, P, b, I, sc = self.nc, self.P, self.b, self.I, self.sc
        xin_ap = xin.t if isinstance(xin, T) else xin
        xin_res = [xin] if isinstance(xin, T) else []
        with contextlib.ExitStack() as st:
            def sb(name, shape, dt):
                return T(st.enter_context(nc.sbuf_tensor(f"E{l}_{name}", shape, dt)), name)

            def ps(name, shape, dt):
                return T(st.enter_context(nc.psum_tensor(f"E{l}_{name}", shape, dt)), name, True)

            wo = sb("wo", [128, 16, D], BF16)
            gb = sb("gb", [128, D], F32)
            mx = [sb(f"mx{i}", [128, 16, 128], BF16) for i in range(3)]
            xt = [sb(f"xt{i}", [128, D], F32) for i in range(3)]
            ot = [sb(f"ot{i}", [128, D], F32) for i in range(3)]
            sq = sb("sq", [128, D], BF16)
            st1 = [sb(f"st{i}", [128, 4], F32) for i in range(3)]
            pacc = [ps(f"pacc{i}", [128, 512], F32) for i in range(8)]
            for kq in range(4):
                b.dma("pool", wo[:, kq * 4:(kq + 1) * 4, :],
                      I["w_out"][l, kq * 512:(kq + 1) * 512, :].rearrange("(kc p) n -> p kc n", p=128),
                      [], [wo])
            b.dma("sp", gb[:], I["norm_post"][l:l + 1, :].to_broadcast([128, D]), [], [gb])
            def loads(t):
                tsl = slice(t * 128, (t + 1) * 128)
                b.dma("sp", mx[t % 3][:], sc["mixT"].t[t], [sc["mixT"]], [mx[t % 3]])
                b.dma("sp", xt[t % 3][:], xin_ap[tsl, :], xin_res, [xt[t % 3]])

            loads(0)
            loads(1)
            for t in range(NT):
                tsl = slice(t * 128, (t + 1) * 128)
                m_t, x_t, o_t, s_t = mx[t % 3], xt[t % 3], ot[t % 3], st1[t % 3]
                if t + 2 < NT:
                    loads(t + 2)
                for n in range(4):
                    pa = pacc[(t % 2) * 4 + n]
                    for kc in range(16):
                        b.mm(pa[:], m_t[:, kc, :], wo[:, kc, n * 512:(n + 1) * 512], kc == 0, kc == 15,
                             [m_t, wo], [pa])
                    b.evac(o_t[:, n * 512:(n + 1) * 512], pa[:], [pa], [o_t])
                b.act(sq[:], o_t[:], AF.Square, [o_t], [sq, s_t], accum_out=s_t[:, 0:1])
                b.ts("dve", s_t[:, 1:2], s_t[:, 0:1], 1.0 / D, 1e-6, ALU.mult, ALU.add, [s_t], [s_t])
                b.act(s_t[:, 2:3], s_t[:, 1:2], AF.Sqrt, [s_t], [s_t])
                P.op("dve", lambda e, s_t=s_t: e.reciprocal(out=s_t[:, 3:4], in_=s_t[:, 2:3]), [s_t], [s_t])
                b.stt(o_t[:], o_t[:], s_t[:, 3:4], gb[:], ALU.mult, ALU.mult, [o_t, s_t, gb], [o_t])
                b.tt("pool", o_t[:], o_t[:], x_t[:], ALU.add, [o_t, x_t], [o_t])
                b.dma("sp", xout.t[tsl, :], o_t[:], [o_t], [xout])
            P.flush()


def _host_inputs(inputs):
    perm = _perm_cols()
    w_in = np.asarray(inputs["w_in"])
    wp = np.zeros((DEPTH, D, WCOLS), np.float32)
    ok = perm >= 0
    wp[:, :, ok] = w_in[:, :, perm[ok]]
    shared = dict(_consts())
    shared["w_in"] = wp
    shared["norm_pre"] = np.ascontiguousarray(inputs["norm_pre"], dtype=np.float32)
    shared["norm_post"] = np.ascontiguousarray(inputs["norm_post"], dtype=np.float32)
    shared["w_out"] = np.ascontiguousarray(inputs["w_out"], dtype=np.float32)
    for kv in ("k", "v"):
        shared[f"cmp_w1_{kv}"] = np.ascontiguousarray(
            np.asarray(inputs[f"cmp_w1_{kv}"], dtype=np.float32).transpose(0, 2, 1, 3))
        shared[f"cmp_w2_{kv}"] = np.ascontiguousarray(inputs[f"cmp_w2_{kv}"], dtype=np.float32)
        shared[f"cmp_posT_{kv}"] = np.ascontiguousarray(
            np.asarray(inputs[f"cmp_pos_{kv}"], dtype=np.float32).transpose(0, 2, 1))
    f32 = lambda a: np.ascontiguousarray(np.asarray(a, dtype=np.float32))
    shared["s5_a_reT"] = f32(np.asarray(inputs["s5_a_re"]).transpose(0, 2, 1))
    shared["s5_a_imT"] = f32(np.asarray(inputs["s5_a_im"]).transpose(0, 2, 1))
    shared["s5_log_dt"] = f32(inputs["s5_log_dt"])
    shared["s5_b_reT"] = f32(np.asarray(inputs["s5_b_re"]).transpose(0, 2, 1, 3))
    shared["s5_b_imT"] = f32(np.asarray(inputs["s5_b_im"]).transpose(0, 2, 1, 3))
    shared["s5_c_reT"] = f32(np.asarray(inputs["s5_c_re"]).transpose(0, 3, 1, 2))
    shared["s5_c_imT"] = f32(np.asarray(inputs["s5_c_im"]).transpose(0, 3, 1, 2))
    shared["s5_dT"] = f32(np.asarray(inputs["s5_d"]).reshape(DEPTH, 4, 128).transpose(0, 2, 1))
    shared["s5_glu_bT"] = f32(np.asarray(inputs["s5_glu_b"]).reshape(DEPTH, 8, 128).transpose(0, 2, 1))
    shared["s5_glu_w"] = f32(inputs["s5_glu_w"])
    shared["pool_w"] = f32(inputs["pool_w"])
    shared["pool_scaleT"] = f32(np.asarray(inputs["pool_scale"]).reshape(DEPTH, 4, 128).transpose(0, 2, 1))
    return shared


def kernel(**inputs):
    x = np.asarray(inputs["x"], dtype=np.float32)
    shared = _host_inputs(inputs)
    k = Kern()
    nc = k.build()
    nb = x.shape[0]
    in_maps = []
    for bi in range(nb):
        m = dict(shared)
        m["x"] = np.ascontiguousarray(x[bi])
        in_maps.append(m)
    res = run_bass_kernel_spmd(nc, in_maps, core_ids=list(range(nb)))
    return np.stack([np.asarray(r["y"]) for r in res.results], 0).astype(np.float32)
```

```python
import contextlib
import math
import numpy as np
import ml_dtypes
import concourse.bass as bass
import concourse.mybir as mybir
from concourse.bass_utils import run_bass_kernel_spmd

F32 = mybir.dt.float32
BF16 = mybir.dt.bfloat16
AF = mybir.ActivationFunctionType
ALU = mybir.AluOpType
AX = mybir.AxisListType

EPOCH = 7990
NDMASEM = 16
NPOOLSEM = 8

S = 4096
D = 2048
NT = S // 128
NQ = S // 512
DEPTH = 2
INW = 5656
HD = 128
SCALE = HD ** -0.5
NEGB = -30000.0


class Res:
    __slots__ = ("name", "lw", "rd", "excl")

    def __init__(self, name="", excl=False):
        self.name = name
        self.lw = None
        self.rd = {}
        self.excl = excl


class T:
    def __init__(self, t, name="", excl=False):
        self.t = t
        self.r = Res(name, excl)

    def __getitem__(self, k):
        return self.t[k]


def _res(x):
    return x.r if isinstance(x, T) else x


class Prog:
    ENGS = ("pe", "act", "dve", "pool", "sp")

    def __init__(self, nc, stack):
        self.nc = nc
        self.stack = stack
        self.ops = {e: [] for e in self.ENGS}
        self.sems = {e: [stack.enter_context(nc.semaphore(f"s_{e}_0"))] for e in self.ENGS}
        self.cnt = {e: 0 for e in self.ENGS}
        self.seen = {e: {} for e in self.ENGS}
        self.dsems = {"sp": [stack.enter_context(nc.semaphore(f"s_dma_{i}")) for i in range(NDMASEM)],
                      "pool": [stack.enter_context(nc.semaphore(f"s_dmap_{i}")) for i in range(NPOOLSEM)]}
        self.ndma = {"sp": 0, "pool": 0}
        self.strict = True
        self.nops = 0

    def _deps(self, reads, writes):
        deps = []
        for r in reads:
            r = _res(r)
            if r.lw is not None:
                deps.append(r.lw)
        for w in writes:
            w = _res(w)
            if w.lw is not None:
                deps.append(w.lw)
            deps.extend(w.rd.values())
        return deps

    def _waits(self, eng, deps):
        waits = {}
        seen = self.seen[eng]
        for (sem, val, src) in deps:
            if src == eng and (eng in ("pe", "sp") or not self.strict):
                continue
            k = id(sem)
            if seen.get(k, 0) >= val:
                continue
            if k not in waits or waits[k][1] < val:
                waits[k] = (sem, val)
        for k, (sem, val) in waits.items():
            seen[k] = val
        return list(waits.values())

    def _record(self, ev, reads, writes):
        key = id(ev[0])
        for r in reads:
            _res(r).rd[key] = ev
        for w in writes:
            w = _res(w)
            w.lw = ev
            w.rd = {}

    @staticmethod
    def _split(reads, writes):
        rd, wr = [], list(writes)
        for r in reads:
            if _res(r).excl:
                wr.append(r)
            else:
                rd.append(r)
        return rd, wr

    def op(self, eng, fn, reads=(), writes=()):
        reads, writes = self._split(reads, writes)
        deps = self._deps(reads, writes)
        waits = self._waits(eng, deps)
        if self.cnt[eng] >= EPOCH:
            self.sems[eng].append(
                self.stack.enter_context(self.nc.semaphore(f"s_{eng}_{len(self.sems[eng])}")))
            self.cnt[eng] = 0
        sem = self.sems[eng][-1]
        self.cnt[eng] += 1
        ev = (sem, self.cnt[eng], eng)
        self.ops[eng].append((waits, fn, sem, 1))
        self._record(ev, reads, writes)
        self.nops += 1
        return ev

    def dma(self, q, out, in_, reads=(), writes=(), **kw):
        reads, writes = self._split(reads, writes)
        deps = self._deps(reads, writes)
        i = self.ndma[q]
        self.ndma[q] += 1
        nsem = len(self.dsems[q])
        sem = self.dsems[q][i % nsem]
        tgt = 16 * (i // nsem + 1)
        if i >= nsem:
            deps.append((sem, tgt - 16, None))
        waits = self._waits(q, deps)
        fn = lambda e, out=out, in_=in_, kw=kw: e.dma_start(out=out, in_=in_, **kw)
        self.ops[q].append((waits, fn, sem, 16))
        ev = (sem, tgt, None)
        self._record(ev, reads, writes)
        self.nops += 1
        return ev

    def flush(self):
        deps = []
        for q in ("sp", "pool"):
            nsem = len(self.dsems[q])
            for j, sem in enumerate(self.dsems[q]):
                n = (self.ndma[q] - j + nsem - 1) // nsem if self.ndma[q] > j else 0
                if n > 0:
                    deps.append((sem, 16 * n, None))
        waits = self._waits("sp", deps)
        self.ops["sp"].append((waits, None, None, 0))
        nc = self.nc
        ops = self.ops
        self.ops = {e: [] for e in self.ENGS}
        with nc.Block() as block:
            def run(e, lst):
                for (waits, fn, sem, inc) in lst:
                    for (s, v) in waits:
                        e.wait_ge(s, v)
                    if fn is not None:
                        fn(e).then_inc(sem, inc)

            @block.tensor
            def _(e):
                run(e, ops["pe"])

            @block.scalar
            def _(e):
                run(e, ops["act"])

            @block.vector
            def _(e):
                run(e, ops["dve"])

            @block.gpsimd
            def _(e):
                run(e, ops["pool"])

            @block.sync
            def _(e):
                run(e, ops["sp"])


class B:
    def __init__(self, nc, P):
        self.nc = nc
        self.P = P
        self.flip = 0

    def act(self, out, in_, func, reads, writes, **kw):
        self.P.op("act", lambda e: e.activation(out=out, in_=in_, func=func, **kw), reads, writes)

    def mm(self, out, lhsT, rhs, start, stop, reads, writes, **kw):
        self.P.op("pe", lambda e: e.matmul(out, lhsT=lhsT, rhs=rhs, start=start, stop=stop, **kw),
                  reads, writes)

    def memset(self, eng, ap, val, writes):
        self.P.op(eng, lambda e: e.memset(ap, val), [], writes)

    def tr(self, out, in_, ident, reads, writes):
        self.P.op("pe", lambda e: e.transpose(out=out, in_=in_, identity=ident), reads, writes)

    def tt(self, eng, out, in0, in1, op, reads, writes):
        self.P.op(eng, lambda e: e.tensor_tensor(out=out, in0=in0, in1=in1, op=op), reads, writes)

    def ts(self, eng, out, in0, s1, s2, op0, op1, reads, writes):
        if op1 is None:
            self.P.op(eng, lambda e: e.tensor_scalar(out=out, in0=in0, scalar1=s1, scalar2=None, op0=op0),
                      reads, writes)
        else:
            self.P.op(eng, lambda e: e.tensor_scalar(out=out, in0=in0, scalar1=s1, scalar2=s2,
                                                      op0=op0, op1=op1), reads, writes)

    def stt(self, out, in0, scalar, in1, op0, op1, reads, writes):
        self.P.op("dve", lambda e: e.scalar_tensor_tensor(out=out, in0=in0, scalar=scalar, in1=in1,
                                                           op0=op0, op1=op1), reads, writes)

    def copy(self, eng, out, in_, reads, writes):
        if eng == "act":
            self.act(out, in_, AF.Copy, reads, writes)
        else:
            self.P.op(eng, lambda e: e.tensor_copy(out=out, in_=in_), reads, writes)

    def evac(self, out, in_, reads, writes):
        self.flip ^= 1
        self.copy("act" if self.flip else "dve", out, in_, reads, writes)

    def dma(self, q, out, in_, reads, writes, **kw):
        self.P.dma(q, out, in_, reads, writes, **kw)


OFF = {"q": 0, "kc": 1024, "vc": 1280, "ksl": 1536, "vsl": 1792, "kwn": 2048, "vwn": 2304,
       "gl": 2560, "zatt": 2584, "us5": 3608, "zs5": 4120, "upool": 4632, "zpool": 5144}
NWG = 12
WCOLS = 11 * 512 + 32


def _perm_cols():
    fm = []
    for h in range(8):
        fm.append(OFF["q"] + 128 * h)
    for nm in ("kc", "vc", "ksl", "kwn"):
        fm += [OFF[nm], OFF[nm] + 128]
    for nm in ("us5", "zs5", "upool", "zpool"):
        fm += [OFF[nm] + 128 * i for i in range(4)]
    cols = []
    for c in fm:
        cols += list(range(c, c + 128))
    cols += list(range(OFF["vsl"], OFF["vsl"] + 256)) + list(range(OFF["vwn"], OFF["vwn"] + 256))
    cols += list(range(OFF["zatt"], OFF["zatt"] + 1024))
    cols += list(range(OFF["gl"], OFF["gl"] + 24)) + [-1] * 8
    return np.array(cols)


def _consts():
    c = {}
    c["ident"] = np.eye(128, dtype=np.float32).astype(ml_dtypes.bfloat16)
    sw = np.zeros((128, 128), np.float32)
    for m in range(128):
        sw[(m + 64) % 128, m] = 1.0
    c["swapi"] = sw.astype(ml_dtypes.bfloat16)
    half = 64
    inv = 10000.0 ** (-np.arange(half, dtype=np.float32) / half)
    ang = np.arange(S, dtype=np.float32)[:, None] * inv[None, :]
    cos = np.cos(ang).astype(np.float32).T
    sin = np.sin(ang).astype(np.float32).T
    c["ropec"] = np.ascontiguousarray(np.concatenate([cos, cos], 0))
    c["ropes"] = np.ascontiguousarray(np.concatenate([-sin, sin], 0))
    bf = ml_dtypes.bfloat16
    p = np.arange(128)
    c["tribc"] = np.where(p[:, None] > p[None, :], NEGB, 0.0).astype(bf)
    c["tribw"] = np.where(p[:, None] <= p[None, :], NEGB, 0.0).astype(bf)
    u = np.arange(2560)
    c["tbc"] = np.where(16 * p[:, None] + 31 > u[None, :], NEGB, 0.0).astype(bf)
    c["blkexp"] = (np.arange(S)[None, :] // 64 == np.arange(64)[:, None]).astype(np.float32).astype(bf)
    n = np.arange(256)
    cs = n * 16
    js = np.arange(64) * 64
    ov = ((cs[:, None] < js[None, :] + 64) & (cs[:, None] + 32 > js[None, :])).astype(np.float32)
    ov = np.concatenate([ov, np.ones((256, 1), np.float32)], 1)
    ov[255] = 0.0
    c["ovl1"] = np.ascontiguousarray(ov.reshape(2, 128, 65).transpose(1, 0, 2)).astype(bf)
    q = np.arange(S)
    cur = q // 64
    j = np.arange(64)
    forced = (j[None, :] == cur[:, None]) | (j[None, :] == 0)
    causal = j[None, :] <= cur[:, None]
    sg = np.ones((128, 2), np.float32)
    sg[64:, 0] = -1.0
    sg[:64, 1] = -1.0
    c["sgn"] = sg
    t16 = np.arange(16)
    rc = np.stack([1.0 / np.minimum(t16 + 1, w) for w in (2, 4, 8, 16)], 0).astype(np.float32)
    c["poolrc"] = np.ascontiguousarray(np.broadcast_to(rc[None], (128, 4, 16))).astype(np.float32)
    tm = lambda a: np.ascontiguousarray(a.reshape(NQ, 4, 128, 64).transpose(0, 2, 1, 3))
    c["impA"] = tm((causal & ~forced).astype(np.float32))
    c["impB"] = tm(np.where(forced, 1e4, np.where(causal, 0.0, -1.0)).astype(np.float32))
    return c


class Kern:
    def __init__(self, debug=(), nlayers=DEPTH, phases=None, bstop=99):
        self.bstop = bstop
        self.debug = set(debug)
        self.nlayers = nlayers
        self.phases = phases
        self.nc = bass.Bass("TRN2", target_bir_lowering=False)
        self.gst = contextlib.ExitStack()
        self.P = None

    def din(self, name, shape, dt=F32):
        return self.nc.dram_tensor(name, list(shape), dt, kind="ExternalInput").ap()

    def dscr(self, name, shape, dt):
        kind = "ExternalOutput" if name in self.debug else "Internal"
        t = self.nc.dram_tensor(name, list(shape), dt, kind=kind).ap()
        return T(t, name)

    def build(self):
        nc = self.nc
        with self.gst as gst:
            self.P = Prog(nc, gst)
            self.b = B(nc, self.P)
            I = self.I = {}
            I["x"] = self.din("x", [S, D])
            I["norm_pre"] = self.din("norm_pre", [DEPTH, D])
            I["norm_post"] = self.din("norm_post", [DEPTH, D])
            I["w_in"] = self.din("w_in", [DEPTH, D, WCOLS])
            I["w_out"] = self.din("w_out", [DEPTH, D, D])
            I["ident"] = self.din("ident", [128, 128], BF16)
            I["swapi"] = self.din("swapi", [128, 128], BF16)
            I["ropec"] = self.din("ropec", [128, S])
            I["ropes"] = self.din("ropes", [128, S])
            for nm in ("tribc", "tribw"):
                I[nm] = self.din(nm, [128, 128], BF16)
            I["tbc"] = self.din("tbc", [128, 2560], BF16)
            I["blkexp"] = self.din("blkexp", [64, S], BF16)
            I["ovl1"] = self.din("ovl1", [128, 2, 65], BF16)
            I["impA"] = self.din("impA", [NQ, 128, 4, 64])
            I["impB"] = self.din("impB", [NQ, 128, 4, 64])
            for kv in ("k", "v"):
                I[f"cmp_w1_{kv}"] = self.din(f"cmp_w1_{kv}", [DEPTH, 128, 32, 256])
                I[f"cmp_w2_{kv}"] = self.din(f"cmp_w2_{kv}", [DEPTH, 256, 128])
                I[f"cmp_posT_{kv}"] = self.din(f"cmp_posT_{kv}", [DEPTH, 128, 32])
            I["sgn"] = self.din("sgn", [128, 2])
            I["s5_a_reT"] = self.din("s5_a_reT", [DEPTH, 64, 32])
            I["s5_a_imT"] = self.din("s5_a_imT", [DEPTH, 64, 32])
            I["s5_log_dt"] = self.din("s5_log_dt", [DEPTH, 32])
            for nm in ("s5_b_reT", "s5_b_imT", "s5_c_reT", "s5_c_imT"):
                I[nm] = self.din(nm, [DEPTH, 64, 32, 16])
            I["s5_dT"] = self.din("s5_dT", [DEPTH, 128, 4])
            I["s5_glu_bT"] = self.din("s5_glu_bT", [DEPTH, 128, 8])
            I["s5_glu_w"] = self.din("s5_glu_w", [DEPTH, 512, 1024])
            I["pool_w"] = self.din("pool_w", [DEPTH, 4, 128, 128])
            I["pool_scaleT"] = self.din("pool_scaleT", [DEPTH, 128, 4])
            I["poolrc"] = self.din("poolrc", [128, 4, 16])
            self.y = T(nc.dram_tensor("y", [S, D], F32, kind="ExternalOutput").ap(), "y")
            sc = self.sc = {}
            sc["qT"] = self.dscr("qT", [8, 128, S], BF16)
            sc["qrT"] = self.dscr("qrT", [8, 128, S], BF16)
            sc["kcT"] = self.dscr("kcT", [2, 128, S], BF16)
            sc["vcT"] = self.dscr("vcT", [2, 128, S], BF16)
            sc["kselT"] = self.dscr("kselT", [2, 128, S], BF16)
            sc["kwinT"] = self.dscr("kwinT", [2, 128, S], BF16)
            sc["us5T"] = self.dscr("us5T", [512, S], BF16)
            sc["zs5T"] = self.dscr("zs5T", [512, S], BF16)
            sc["upoolT"] = self.dscr("upoolT", [512, S], F32)
            sc["zpoolT"] = self.dscr("zpoolT", [512, S], BF16)
            sc["v"] = self.dscr("v", [4, 128, NT, 128], BF16)
            sc["zatt"] = self.dscr("zatt", [S, 1024], BF16)
            sc["gl"] = self.dscr("gl", [S, 32], F32)
            sc["mixT"] = self.dscr("mixT", [NT, 128, 16, 128], BF16)
            sc["x1"] = self.dscr("x1", [S, D], F32)
            for l in range(self.nlayers):
                xin = I["x"] if l == 0 else sc["x1"]
                xout = self.y if l == self.nlayers - 1 else sc["x1"]
                if self.phases is None or "A" in self.phases:
                    self.phase_A(l, xin)
                if self.phases is None or "B" in self.phases:
                    self.phase_B(l)
                if self.phases is None or "C" in self.phases:
                    self.phase_C(l)
                if self.phases is None or "D" in self.phases:
                    self.phase_D(l)
                if self.phases is None or "E" in self.phases:
                    self.phase_E(l, xin, xout)
        return nc

    def phase_A(self, l, xin):
        nc, P, b, I, sc = self.nc, self.P, self.b, self.I, self.sc
        xin_ap = xin.t if isinstance(xin, T) else xin
        xin_res = [xin] if isinstance(xin, T) else []
        with contextlib.ExitStack() as st:
            def sb(name, shape, dt):
                return T(st.enter_context(nc.sbuf_tensor(f"A{l}_{name}", shape, dt)), name)

            def ps(name, shape, dt):
                return T(st.enter_context(nc.psum_tensor(f"A{l}_{name}", shape, dt)), name, True)

            hT = sb("hT", [128, 16, S], BF16)
            ident = sb("ident", [128, 128], BF16)
            swapi = sb("swapi", [128, 128], BF16)
            st_outer = st
            st = st_outer.enter_context(contextlib.ExitStack())
            gb = sb("gb", [128, D], F32)
            xt = [sb(f"xt{i}", [128, D], F32) for i in range(3)]
            sq = sb("sq", [128, D], BF16)
            hb = [sb(f"hb{i}", [128, D], BF16) for i in range(3)]
            st1 = [sb(f"st{i}", [128, 4], F32) for i in range(3)]
            ptr = [ps(f"ptr{i}", [128, 8, 128], BF16) for i in range(4)]

            b.dma("sp", ident[:], I["ident"][:, :], [], [ident])
            b.dma("sp", swapi[:], I["swapi"][:, :], [], [swapi])
            b.dma("sp", gb[:], I["norm_pre"][l:l + 1, :].to_broadcast([128, D]), [], [gb])

            for t in range(NT):
                x_t, h_t, s_t = xt[t % 3], hb[t % 3], st1[t % 3]
                b.dma("sp", x_t[:], xin_ap[t * 128:(t + 1) * 128, :], xin_res, [x_t])
                b.act(sq[:], x_t[:], AF.Square, [x_t], [sq, s_t], accum_out=s_t[:, 0:1])
                b.ts("dve", s_t[:, 1:2], s_t[:, 0:1], 1.0 / D, 1e-6, ALU.mult, ALU.add, [s_t], [s_t])
                b.act(s_t[:, 2:3], s_t[:, 1:2], AF.Sqrt, [s_t], [s_t])
                P.op("dve", lambda e, s_t=s_t: e.reciprocal(out=s_t[:, 3:4], in_=s_t[:, 2:3]), [s_t], [s_t])
                b.stt(h_t[:], x_t[:], s_t[:, 3:4], gb[:], ALU.mult, ALU.mult, [x_t, s_t, gb], [h_t])
                for half in range(2):
                    p_t = ptr[(2 * t + half) % 4]
                    for j in range(8):
                        kc = half * 8 + j
                        b.tr(p_t[:, j, :], h_t[:, kc * 128:(kc + 1) * 128], ident[:], [h_t, ident], [p_t])
                    b.evac(hT[:, half * 8:(half + 1) * 8, t * 128:(t + 1) * 128], p_t[:], [p_t], [hT])

            P.flush()
            st.close()
            st = st_outer
            wb = [sb(f"wb{i}", [128, 16, 512], BF16) for i in range(2)]
            rope = [sb(f"rope{i}", [128, 2, 512], F32) for i in range(2)]
            NPA = 5
            NQU = 6
            pacc = [ps(f"pacc{i}", [128, 512], F32) for i in range(NPA)]
            prot = [ps(f"prot{i}", [128, 512], F32) for i in range(2)]
            qun = [sb(f"qun{i}", [128, 512], BF16) for i in range(NQU)]
            t1 = [sb(f"t1{i}", [128, 512], F32) for i in range(2)]
            t2 = [sb(f"t2{i}", [128, 512], F32) for i in range(2)]
            qro = [sb(f"qro{i}", [128, 512], BF16) for i in range(2)]
            o32 = [sb(f"o32{i}", [128, 512], F32) for i in range(2)]
            ogl = [sb(f"ogl{i}", [128, 32], F32) for i in range(2)]
            win = I["w_in"]
            nacc = [0]
            nrot = [0]

            def load_w(gi):
                w_t = wb[gi % 2]
                ncol = 512 if gi < 11 else 32
                src = win[l, :, gi * 512:gi * 512 + ncol].rearrange("(kc p) n -> p kc n", p=128)
                b.dma("pool", w_t[:, :, 0:ncol], src, [], [w_t])

            load_w(0)
            for gi in range(NWG):
                if gi + 1 < NWG:
                    load_w(gi + 1)
                w_t = wb[gi % 2]
                if gi < 8:
                    do_rope = gi in (0, 1, 3)
                    for tq in range(NQ):
                        tsl = slice(tq * 512, (tq + 1) * 512)
                        if do_rope:
                            rp = rope[tq % 2]
                            b.dma("sp", rp[:, 0, :], I["ropec"][:, tsl], [], [rp])
                            b.dma("sp", rp[:, 1, :], I["ropes"][:, tsl], [], [rp])
                        for c in range(4):
                            pa = pacc[nacc[0] % NPA]
                            nacc[0] += 1
                            for kc in range(16):
                                b.mm(pa[:], w_t[:, kc, c * 128:(c + 1) * 128], hT[:, kc, tsl],
                                     kc == 0, kc == 15, [w_t, hT], [pa])
                            ch = gi * 4 + c
                            if gi in (0, 1):
                                qu = qun[nacc[0] % NQU]
                                b.act(qu[:], pa[:], AF.Copy, [pa], [qu])
                                b.dma("sp", sc["qT"][ch, :, tsl], qu[:], [qu], [sc["qT"]])
                                self._rope(b, qu, rp, swapi, prot, t1, t2, qro, nrot,
                                           sc["qrT"], sc["qrT"][ch, :, tsl])
                            elif gi == 2:
                                qu = qun[nacc[0] % NQU]
                                b.act(qu[:], pa[:], AF.Copy, [pa], [qu])
                                dst = sc["kcT"] if c < 2 else sc["vcT"]
                                b.dma("sp", dst[c % 2, :, tsl], qu[:], [qu], [dst])
                            elif gi == 3:
                                qu = qun[nacc[0] % NQU]
                                b.act(qu[:], pa[:], AF.Copy, [pa], [qu])
                                dst = sc["kselT"] if c < 2 else sc["kwinT"]
                                self._rope(b, qu, rp, swapi, prot, t1, t2, qro, nrot, dst, dst[c % 2, :, tsl])
                            elif gi in (4, 5, 7):
                                qu = qun[nacc[0] % NQU]
                                dst = {4: sc["us5T"], 5: sc["zs5T"], 7: sc["zpoolT"]}[gi]
                                b.act(qu[:], pa[:], AF.Copy if gi == 4 else AF.Silu, [pa], [qu])
                                b.dma("sp", dst[c * 128:(c + 1) * 128, tsl], qu[:], [qu], [dst])
                            else:
                                o = o32[nacc[0] % 2]
                                b.act(o[:], pa[:], AF.Copy, [pa], [o])
                                b.dma("sp", sc["upoolT"][c * 128:(c + 1) * 128, tsl], o[:], [o], [sc["upoolT"]])
                else:
                    ncol = 512 if gi < 11 else 32
                    for t in range(NT):
                        tsl = slice(t * 128, (t + 1) * 128)
                        pa = pacc[nacc[0] % NPA]
                        nacc[0] += 1
                        for kc in range(16):
                            b.mm(pa[:, 0:ncol], hT[:, kc, tsl], w_t[:, kc, 0:ncol], kc == 0, kc == 15,
                                 [w_t, hT], [pa])
                        if gi == 8:
                            qu = qun[nacc[0] % NQU]
                            b.act(qu[:], pa[:], AF.Copy, [pa], [qu])
                            for c4 in range(4):
                                b.dma("sp", sc["v"].t[c4, :, t, :], qu[:, c4 * 128:(c4 + 1) * 128], [qu], [sc["v"]])
                        elif gi in (9, 10):
                            qu = qun[nacc[0] % NQU]
                            b.act(qu[:], pa[:], AF.Silu, [pa], [qu])
                            b.dma("sp", sc["zatt"][tsl, (gi - 9) * 512:(gi - 8) * 512], qu[:], [qu], [sc["zatt"]])
                        else:
                            o = ogl[nacc[0] % 2]
                            b.act(o[:], pa[:, 0:32], AF.Sigmoid, [pa], [o])
                            b.dma("sp", sc["gl"][tsl, :], o[:], [o], [sc["gl"]])
            P.flush()

    def _rope(self, b, qu, rp, swapi, prot, t1, t2, qro, nrot, dst_res, dst_ap):
        i = nrot[0]
        nrot[0] += 1
        pr, a1, a2, qo = prot[i % 2], t1[i % 2], t2[i % 2], qro[i % 2]
        b.mm(pr[:], swapi[:], qu[:], True, True, [swapi, qu], [pr])
        b.tt("dve", a1[:], pr[:], rp[:, 1, :], ALU.mult, [pr, rp], [a1])
        b.tt("pool", a2[:], qu[:], rp[:, 0, :], ALU.mult, [qu, rp], [a2])
        b.tt("dve", qo[:], a1[:], a2[:], ALU.add, [a1, a2], [qo])
        b.dma("sp", dst_ap, qo[:], [qo], [dst_res])


    def phase_B(self, l):
        nc, P, b, I, sc = self.nc, self.P, self.b, self.I, self.sc
        with contextlib.ExitStack() as st:
            def sb(name, shape, dt):
                return T(st.enter_context(nc.sbuf_tensor(f"B{l}_{name}", shape, dt)), name)

            def ps(name, shape, dt):
                return T(st.enter_context(nc.psum_tensor(f"B{l}_{name}", shape, dt)), name, True)

            ident = sb("ident", [128, 128], BF16)
            tribc = sb("tribc", [128, 128], BF16)
            tribw = sb("tribw", [128, 128], BF16)
            tbc = sb("tbc", [128, 2560], BF16)
            blkexp = sb("blkexp", [128, S], BF16)
            zl = sb("zl", [128, 128], BF16)
            zr = sb("zr", [128, 512], BF16)
            for (t_, nm) in ((ident, "ident"), (tribc, "tribc"), (tribw, "tribw"), (tbc, "tbc")):
                b.dma("sp", t_[:], I[nm][:, :], [], [t_])
            b.memset("pool", blkexp[:], 0.0, [blkexp])
            b.dma("sp", blkexp[0:64, :], I["blkexp"][:, :], [], [blkexp])
            b.memset("pool", zl[:], 0.0, [zl])
            b.memset("pool", zr[:], 0.0, [zr])

            NS_ = 3
            pS = [ps(f"pS{i}", [128, 512], F32) for i in range(NS_)]
            pO = [[ps(f"pO{a}{j}", [128, 2, 256], F32) for j in range(2)] for a in range(2)]
            pT = [ps(f"pT{i}", [128, 1024], BF16) for i in range(1)]

            qT = sb("qT", [128, 4, S], BF16)
            qrT = sb("qrT", [128, 4, S], BF16)
            kselT = sb("kselT", [128, S], BF16)
            kwinT = sb("kwinT", [128, S], BF16)
            vsel = sb("vsel", [128, NT, 129], BF16)
            vwin = sb("vwin", [128, NT, 129], BF16)
            kin = sb("kin", [128, S], BF16)
            vin = sb("vin", [128, S], BF16)
            w1k = sb("w1k", [128, 32, 256], BF16)
            w1v = sb("w1v", [128, 32, 256], BF16)
            w2k = sb("w2k", [128, 2, 128], BF16)
            w2v = sb("w2v", [128, 2, 128], BF16)
            posk = sb("posk", [128, 32], BF16)
            posv = sb("posv", [128, 32], BF16)
            hk = sb("hk", [128, 2, 256], BF16)
            hv = sb("hv", [128, 2, 256], BF16)
            cbias = sb("cbias", [128, 4], F32)
            kcmp = sb("kcmp", [128, 256], BF16)
            vcx = sb("vcx", [128, 2, 193], BF16)
            NP_ = 5
            pt = [sb(f"pt{i}", [128, 512], BF16) for i in range(NP_)]
            zat = [sb(f"zat{i}", [128, 4, 512], BF16) for i in range(2)]
            glt = [sb(f"glt{i}", [128, 4, 32], F32) for i in range(2)]
            iA = [sb(f"iA{i}", [128, 4, 64], F32) for i in range(2)]
            iB = [sb(f"iB{i}", [128, 4, 64], F32) for i in range(2)]
            accO = [sb(f"accO{i}", [128, 4, 4, 128], F32) for i in range(2)]
            impacc = [sb(f"impacc{i}", [128, 4, 64], F32) for i in range(2)]
            selbT = [sb(f"selbT{i}", [128, 512], BF16) for i in range(2)]
            imp2 = sb("imp2", [128, 64], F32)
            impw = sb("impw", [128, 64], F32)
            m8 = sb("m8", [128, 16], F32)
            s01 = sb("s01", [128, 64], F32)
            sb128 = sb("sb128", [128, 128], BF16)
            dn = [sb(f"dn{i}", [128, 8], F32) for i in range(4)]
            attb = [sb(f"attb{i}", [128, 512], BF16) for i in range(2)]
            attT = [sb(f"attT{i}", [128, 4, 128], BF16) for i in range(2)]

            b.memset("pool", sb128[:], 0.0, [sb128])
            b.memset("pool", hk[:], 0.0, [hk])
            b.memset("pool", hv[:], 0.0, [hv])
            b.memset("pool", kcmp[:], 0.0, [kcmp])
            b.memset("pool", vsel[:, :, 128:129], 1.0, [vsel])
            b.memset("pool", vwin[:, :, 128:129], 1.0, [vwin])
            b.dma("sp", vcx[:, :, 128:193], I["ovl1"][:, :, :], [], [vcx])

            cnt = {"S": 0, "O": 0, "P": 0, "T": 0, "dn": 0}

            for g in range(2):
                b.dma("sp", qT[:], sc["qT"].t[g * 4:(g + 1) * 4].rearrange("r p n -> p r n"), [sc["qT"]], [qT])
                b.dma("sp", qrT[:], sc["qrT"].t[g * 4:(g + 1) * 4].rearrange("r p n -> p r n"), [sc["qrT"]], [qrT])
                b.dma("sp", kselT[:], sc["kselT"].t[g], [sc["kselT"]], [kselT])
                b.dma("sp", kwinT[:], sc["kwinT"].t[g], [sc["kwinT"]], [kwinT])
                b.dma("sp", kin[:], sc["kcT"].t[g], [sc["kcT"]], [kin])
                b.dma("sp", vin[:], sc["vcT"].t[g], [sc["vcT"]], [vin])
                b.dma("sp", vsel[:, :, 0:128], sc["v"].t[g], [sc["v"]], [vsel])
                b.dma("sp", vwin[:, :, 0:128], sc["v"].t[2 + g], [sc["v"]], [vwin])
                if g == 0:
                    b.dma("pool", w1k[:], I["cmp_w1_k"][l], [], [w1k], max_dma_last_dim=4096)
                    b.dma("pool", w1v[:], I["cmp_w1_v"][l], [], [w1v], max_dma_last_dim=4096)
                    b.dma("pool", w2k[:], I["cmp_w2_k"][l].rearrange("(c p) d -> p c d", p=128), [], [w2k])
                    b.dma("pool", w2v[:], I["cmp_w2_v"][l].rearrange("(c p) d -> p c d", p=128), [], [w2v])
                    b.dma("pool", posk[:], I["cmp_posT_k"][l], [], [posk])
                    b.dma("pool", posv[:], I["cmp_posT_v"][l], [], [posv])
                    for kv, (w1, pos) in enumerate(((w1k, posk), (w1v, posv))):
                        for hc in range(2):
                            pa = pS[cnt["S"] % NS_]
                            cnt["S"] += 1
                            for li in range(32):
                                b.mm(pa[:, 0:1], w1[:, li, hc * 128:(hc + 1) * 128], pos[:, li:li + 1],
                                     li == 0, li == 31, [w1, pos], [pa])
                            b.copy("dve", cbias[:, kv * 2 + hc:kv * 2 + hc + 1], pa[:, 0:1], [pa], [cbias])

                if self.bstop <= 1:
                    break
                for kv, (src, w1, w2, hbuf) in enumerate(((kin, w1k, w2k, hk), (vin, w1v, w2v, hv))):
                    srcv = src[:].rearrange("p (n s) -> p n s", s=16)
                    for hc in range(2):
                        pa = pS[cnt["S"] % NS_]
                        cnt["S"] += 1
                        for li in range(32):
                            rhs = srcv[:, 0:255, li] if li < 16 else srcv[:, 1:256, li - 16]
                            b.mm(pa[:, 0:255], w1[:, li, hc * 128:(hc + 1) * 128], rhs, li == 0, li == 31,
                                 [w1, src], [pa])
                        b.act(hbuf[:, hc, 0:255], pa[:, 0:255], AF.Silu, [pa, cbias], [hbuf],
                              bias=cbias[:, kv * 2 + hc:kv * 2 + hc + 1])
                    if kv == 0:
                        pa = pS[cnt["S"] % NS_]
                        cnt["S"] += 1
                        for hc in range(2):
                            b.mm(pa[:, 0:255], w2[:, hc, :], hbuf[:, hc, 0:255], hc == 0, hc == 1, [w2, hbuf], [pa])
                        b.copy("dve", kcmp[:, 0:255], pa[:, 0:255], [pa], [kcmp])
                    else:
                        for a in range(2):
                            pa = pS[cnt["S"] % NS_]
                            cnt["S"] += 1
                            for hc in range(2):
                                b.mm(pa[:, 0:128], hbuf[:, hc, a * 128:(a + 1) * 128], w2[:, hc, :], hc == 0, hc == 1,
                                     [w2, hbuf], [pa])
                            b.copy("dve", vcx[:, a, 0:128], pa[:, 0:128], [pa], [vcx])

                if self.bstop <= 2:
                    break
                for i in range(NQ):
                    if self.bstop <= 7 and i >= 1:
                        break
                    q0 = i * 512
                    za, gt, A_, B_ = zat[i % 2], glt[i % 2], iA[i % 2], iB[i % 2]
                    ao, ia, sbt = accO[i % 2], impacc[i % 2], selbT[i % 2]

                    def tile_loads(ii):
                        qq = ii * 512
                        b.dma("sp", zat[ii % 2][:],
                              sc["zatt"].t[qq:qq + 512, g * 512:(g + 1) * 512].rearrange("(b p) c -> p b c", p=128),
                              [sc["zatt"]], [zat[ii % 2]])
                        b.dma("sp", glt[ii % 2][:], sc["gl"].t[qq:qq + 512, :].rearrange("(b p) c -> p b c", p=128),
                              [sc["gl"]], [glt[ii % 2]])
                        b.dma("sp", iA[ii % 2][:], I["impA"][ii], [], [iA[ii % 2]])
                        b.dma("sp", iB[ii % 2][:], I["impB"][ii], [], [iB[ii % 2]])

                    if i == 0:
                        tile_loads(0)

                    def norm_scales(Oset, width, gcol):
                        d_ = dn[cnt["dn"] % 4]
                        cnt["dn"] += 1
                        for j in range(2):
                            b.ts("dve", d_[:, 2 * j:2 * j + 2], Oset[j][:, :, width - 1], 1e-30, None, ALU.max, None,
                                 [Oset[j]], [d_])
                        P.op("dve", lambda e, d_=d_: e.reciprocal(out=d_[:, 0:4], in_=d_[:, 0:4]), [d_], [d_])
                        b.tt("dve", d_[:, 4:8], d_[:, 0:4], gt[:, :, gcol], ALU.mult, [d_, gt], [d_])
                        return d_

                    if self.bstop <= 2.5:
                        break
                    for r in range(4):
                        Oset = pO[cnt["O"] % 2]
                        cnt["O"] += 1
                        ets = []
                        for a in range(2):
                            u0 = q0 - 2048 * a
                            if u0 + 511 < 31:
                                continue
                            pa = pS[cnt["S"] % NS_]
                            cnt["S"] += 1
                            partial = u0 < 2063
                            b.mm(pa[:], kcmp[:, a * 128:(a + 1) * 128], qT[:, r, q0:q0 + 512], True, not partial,
                                 [kcmp, qT], [pa])
                            if partial:
                                b.mm(pa[:], ident[:], tbc[:, u0:u0 + 512], False, True, [ident, tbc], [pa])
                            e_ = pt[cnt["P"] % NP_]
                            cnt["P"] += 1
                            b.act(e_[:], pa[:], AF.Exp, [pa], [e_], scale=SCALE)
                            ets.append((a, e_))
                        if self.bstop <= 2.6:
                            continue
                        for bb in range(4):
                            for k_, (a, e_) in enumerate(ets):
                                b.mm(Oset[bb // 2][:, bb % 2, 0:193], e_[:, bb * 128:(bb + 1) * 128], vcx[:, a, :],
                                     k_ == 0, k_ == len(ets) - 1, [e_, vcx], [Oset[bb // 2]], skip_group_check=True)
                        if self.bstop <= 2.7:
                            continue
                        d_ = norm_scales(Oset, 193, (g * 4 + r) * 3 + 0)
                        if self.bstop <= 2.8:
                            continue
                        for bb in range(4):
                            Ob = Oset[bb // 2]
                            if self.bstop == 2.95:
                                pass
                            elif r == 0:
                                b.ts("dve", ia[:, bb, :], Ob[:, bb % 2, 128:192], d_[:, bb:bb + 1], None, ALU.mult, None,
                                     [Ob, d_], [ia])
                            else:
                                b.stt(ia[:, bb, :], Ob[:, bb % 2, 128:192], d_[:, bb:bb + 1], ia[:, bb, :],
                                      ALU.mult, ALU.add, [Ob, d_, ia], [ia])
                            if self.bstop == 2.9:
                                continue
                            b.act(ao[:, bb, r, :], Ob[:, bb % 2, 0:128], AF.Identity, [Ob, d_], [ao],
                                  scale=d_[:, 4 + bb:5 + bb])

                    if i + 1 < NQ:
                        tile_loads(i + 1)
                    if self.bstop <= 3:
                        break
                    for bb in range(4):
                        b.tt("dve", imp2[:], ia[:, bb, :], A_[:, bb, :], ALU.mult, [ia, A_], [imp2])
                        b.tt("dve", imp2[:], imp2[:], B_[:, bb, :], ALU.add, [imp2, B_], [imp2])
                        P.op("dve", lambda e: e.max(out=m8[:, 0:8], in_=imp2[:]), [imp2], [m8])
                        P.op("dve", lambda e: e.match_replace(out=impw[:], in_to_replace=m8[:, 0:8], in_values=imp2[:],
                                                              imm_value=-2.0), [imp2, m8], [impw])
                        P.op("dve", lambda e: e.max(out=m8[:, 8:16], in_=impw[:]), [impw], [m8])
                        b.ts("dve", s01[:], imp2[:], m8[:, 15:16], None, ALU.is_ge, None, [imp2, m8], [s01])
                        b.ts("dve", sb128[:, 0:64], s01[:], -NEGB, NEGB, ALU.mult, ALU.add, [s01], [sb128])
                        ptr_ = pT[0]
                        cnt["T"] += 1
                        b.tr(ptr_[:, 0:128], sb128[:], ident[:], [sb128, ident], [ptr_])
                        b.copy("dve", sbt[:, bb * 128:(bb + 1) * 128], ptr_[:, 0:128], [ptr_], [sbt])

                    if self.bstop <= 4:
                        break
                    def branch(r, kts, rng, masks, kT, vext, sel, gcol):
                        Oset = pO[cnt["O"] % 2]
                        cnt["O"] += 1
                        for j in range(2):
                            b.mm(Oset[j][:].rearrange("p a c -> p (a c)"), zl[:], zr[:], True, False, [zl, zr], [Oset[j]],
                                 skip_group_check=True)
                        last_kt = {}
                        for kt in kts:
                            lo, hi = rng(kt)
                            for bb in range(lo, hi + 1):
                                last_kt[bb] = kt

                        def qk(kt):
                            lo, hi = rng(kt)
                            c0, c1 = lo * 128, (hi + 1) * 128
                            pa = pS[cnt["S"] % NS_]
                            cnt["S"] += 1
                            mms = [(pa[:, c0:c1], kT[:, kt * 128:(kt + 1) * 128], qrT[:, r, q0 + c0:q0 + c1], [kT, qrT])]
                            if sel:
                                mms.append((pa[:, c0:c1], blkexp[:, kt * 128:(kt + 1) * 128], sbt[:, c0:c1], [blkexp, sbt]))
                            for (bb, tri) in masks(kt):
                                mms.append((pa[:, bb * 128:(bb + 1) * 128], ident[:], tri[:], [ident, tri]))
                            for n_, (o_, l_, r_, rd) in enumerate(mms):
                                b.mm(o_, l_, r_, n_ == 0, n_ == len(mms) - 1, rd, [pa])
                            return pa, lo, hi

                        pend = [qk(kts[0])]
                        if len(kts) > 1:
                            pend.append(qk(kts[1]))
                        for n_, kt in enumerate(kts):
                            pa, lo, hi = pend.pop(0)
                            if n_ + 2 < len(kts):
                                pend.append(qk(kts[n_ + 2]))
                            c0, c1 = lo * 128, (hi + 1) * 128
                            p_ = pt[cnt["P"] % NP_]
                            cnt["P"] += 1
                            b.act(p_[:, c0:c1], pa[:, c0:c1], AF.Exp, [pa], [p_], scale=SCALE)
                            for bb in range(lo, hi + 1):
                                b.mm(Oset[bb // 2][:, bb % 2, 0:129], p_[:, bb * 128:(bb + 1) * 128], vext[:, kt, :],
                                     False, last_kt[bb] == kt, [p_, vext], [Oset[bb // 2]], skip_group_check=True)
                        d_ = norm_scales(Oset, 129, gcol)
                        for bb in range(4):
                            Ob = Oset[bb // 2]
                            b.stt(ao[:, bb, r, :], Ob[:, bb % 2, 0:128], d_[:, 4 + bb:5 + bb], ao[:, bb, r, :],
                                  ALU.mult, ALU.add, [Ob, d_, ao], [ao])

                    for r in range(4):
                        kts = list(range(max(0, 4 * i - 4), 4 * i + 4))
                        rng = lambda kt: (max(0, kt - 4 * i), min(3, kt - 4 * i + 4))

                        def masks(kt):
                            m = []
                            if 0 <= kt - 4 * i <= 3:
                                m.append((kt - 4 * i, tribc))
                            if 0 <= kt + 4 - 4 * i <= 3:
                                m.append((kt + 4 - 4 * i, tribw))
                            return m
                        branch(r, kts, rng, masks, kwinT, vwin, False, (g * 4 + r) * 3 + 2)
                    if self.bstop <= 5:
                        break
                    for r in range(4):
                        kts = list(range(0, 4 * i + 4))
                        rng = lambda kt: (max(0, kt - 4 * i), 3)
                        masks = lambda kt: [(kt - 4 * i, tribc)] if kt >= 4 * i else []
                        branch(r, kts, rng, masks, kselT, vsel, True, (g * 4 + r) * 3 + 1)

                    if self.bstop <= 6:
                        break
                    for bb in range(4):
                        ab, aT = attb[bb % 2], attT[bb % 2]
                        b.tt("pool", ab[:], ao[:, bb, :, :].rearrange("p r d -> p (r d)"), za[:, bb, :], ALU.mult,
                             [ao, za], [ab])
                        ptr_ = pT[0]
                        cnt["T"] += 1
                        for r in range(4):
                            b.tr(ptr_[:, r * 128:(r + 1) * 128], ab[:, r * 128:(r + 1) * 128], ident[:], [ab, ident], [ptr_])
                        b.evac(aT[:].rearrange("p r q -> p (r q)"), ptr_[:, 0:512], [ptr_], [aT])
                        c0 = q0 + bb * 128
                        b.dma("sp", sc["mixT"].t[c0 // 128, :, g * 4:(g + 1) * 4, :], aT[:], [aT], [sc["mixT"]])
                if self.bstop <= 8:
                    break
            P.flush()


    def phase_C(self, l):
        nc, P, b, I, sc = self.nc, self.P, self.b, self.I, self.sc
        TWO_PI = 2.0 * math.pi
        with contextlib.ExitStack() as st:
            def sb(name, shape, dt):
                return T(st.enter_context(nc.sbuf_tensor(f"C{l}_{name}", shape, dt)), name)

            def ps(name, shape, dt):
                return T(st.enter_context(nc.psum_tensor(f"C{l}_{name}", shape, dt)), name, True)

            ident = sb("ident", [128, 128], BF16)
            identf = sb("identf", [128, 128], F32)
            swapf = sb("swapf", [128, 128], F32)
            sgn = sb("sgn", [128, 2], F32)
            b.dma("sp", ident[:], I["ident"][:, :], [], [ident])
            b.dma("pool", identf[:], I["ident"][:, :], [], [identf])
            b.dma("pool", swapf[:], I["swapi"][:, :], [], [swapf])
            b.dma("sp", sgn[:], I["sgn"][:, :], [], [sgn])

            pw1 = sb("pw1", [128, 12, 32], F32)
            pw2 = sb("pw2", [128, 12, 32], F32)
            ccpad = sb("ccpad", [128, 32, 128], BF16)
            lhsB = sb("lhsB", [128, 32, 128], BF16)
            dcol = sb("dcol", [128, 4], F32)
            glub = sb("glub", [128, 8], F32)
            yT = sb("yT", [128, 4, S], BF16)
            pX = [ps(f"pX{i}", [128, 512], F32) for i in range(4)]
            st_outer = st
            st = st_outer.enter_context(contextlib.ExitStack())
            ar = sb("ar", [128, 32], F32)
            ai = sb("ai", [128, 32], F32)
            dt_ = sb("dt", [128, 32], F32)
            tA = sb("tA", [128, 32], F32)
            tB = sb("tB", [128, 32], F32)
            tC = sb("tC", [128, 32], F32)
            th = sb("th", [128, 32], F32)
            mag = sb("mag", [128, 32], F32)
            abr = sb("abr", [128, 32], F32)
            abi = sb("abi", [128, 32], F32)
            cr = sb("cr", [128, 32], F32)
            ci = sb("ci", [128, 32], F32)
            for h in range(2):
                b.dma("sp", ar[h * 64:(h + 1) * 64, :], I["s5_a_reT"][l], [], [ar])
                b.dma("sp", ai[h * 64:(h + 1) * 64, :], I["s5_a_imT"][l], [], [ai])
            b.dma("sp", dt_[:], I["s5_log_dt"][l:l + 1, :].to_broadcast([128, 32]), [], [dt_])
            b.act(dt_[:], dt_[:], AF.Exp, [dt_], [dt_])
            b.tt("dve", tA[:], ar[:], dt_[:], ALU.mult, [ar, dt_], [tA])
            b.act(mag[:], tA[:], AF.Exp, [tA], [mag])
            b.tt("dve", th[:], ai[:], dt_[:], ALU.mult, [ai, dt_], [th])

            def sin_of(dst, src, shift):
                b.ts("dve", tB[:], src[:], shift, None, ALU.add, None, [src], [tB])
                b.copy("dve", tC[:], tB[:], [tB], [tC])
                for m in range(1, 9):
                    b.ts("dve", tA[:], tB[:], (2 * m - 1) * math.pi, TWO_PI, ALU.is_gt, ALU.mult, [tB], [tA])
                    b.tt("dve", tC[:], tC[:], tA[:], ALU.subtract, [tC, tA], [tC])
                b.act(dst[:], tC[:], AF.Sin, [tC], [dst])

            sin_of(abi, th, 0.0)
            sin_of(abr, th, 0.5 * math.pi)
            b.tt("dve", abr[:], abr[:], mag[:], ALU.mult, [abr, mag], [abr])
            b.tt("dve", abi[:], abi[:], mag[:], ALU.mult, [abi, mag], [abi])
            b.tt("dve", tA[:], ar[:], ar[:], ALU.mult, [ar], [tA])
            b.tt("dve", tB[:], ai[:], ai[:], ALU.mult, [ai], [tB])
            b.tt("dve", tA[:], tA[:], tB[:], ALU.add, [tA, tB], [tA])
            P.op("dve", lambda e: e.reciprocal(out=tA[:], in_=tA[:]), [tA], [tA])
            b.ts("dve", tB[:], abr[:], -1.0, None, ALU.add, None, [abr], [tB])
            b.tt("dve", cr[:], tB[:], ar[:], ALU.mult, [tB, ar], [cr])
            b.tt("dve", tC[:], abi[:], ai[:], ALU.mult, [abi, ai], [tC])
            b.tt("dve", cr[:], cr[:], tC[:], ALU.add, [cr, tC], [cr])
            b.tt("dve", cr[:], cr[:], tA[:], ALU.mult, [cr, tA], [cr])
            b.tt("dve", ci[:], abi[:], ar[:], ALU.mult, [abi, ar], [ci])
            b.tt("dve", tC[:], tB[:], ai[:], ALU.mult, [tB, ai], [tC])
            b.tt("dve", ci[:], ci[:], tC[:], ALU.subtract, [ci, tC], [ci])
            b.tt("dve", ci[:], ci[:], tA[:], ALU.mult, [ci, tA], [ci])
            b.ts("dve", ci[:], ci[:], sgn[:, 1:2], None, ALU.mult, None, [ci, sgn], [ci])

            b.copy("dve", pw1[:, 0, :], abr[:], [abr], [pw1])
            b.ts("dve", pw2[:, 0, :], abi[:], sgn[:, 0:1], None, ALU.mult, None, [abi, sgn], [pw2])
            for lv in range(1, 12):
                b.tt("dve", tA[:], pw1[:, lv - 1, :], pw1[:, lv - 1, :], ALU.mult, [pw1], [tA])
                b.tt("dve", tB[:], pw2[:, lv - 1, :], pw2[:, lv - 1, :], ALU.mult, [pw2], [tB])
                b.tt("dve", pw1[:, lv, :], tA[:], tB[:], ALU.subtract, [tA, tB], [pw1])
                b.tt("dve", tC[:], pw1[:, lv - 1, :], pw2[:, lv - 1, :], ALU.mult, [pw1, pw2], [tC])
                b.ts("dve", pw2[:, lv, :], tC[:], 2.0, None, ALU.mult, None, [tC], [pw2])

            bri = sb("bri", [128, 32, 16], F32)
            bir = sb("bir", [128, 32, 16], F32)
            ccs = sb("ccs", [128, 32, 16], F32)
            bbpad = sb("bbpad", [128, 32, 128], BF16)
            b.dma("sp", bri[0:64], I["s5_b_reT"][l], [], [bri])
            b.dma("sp", bri[64:128], I["s5_b_imT"][l], [], [bri])
            b.dma("sp", bir[0:64], I["s5_b_imT"][l], [], [bir])
            b.dma("sp", bir[64:128], I["s5_b_reT"][l], [], [bir])
            b.dma("sp", ccs[0:64], I["s5_c_reT"][l], [], [ccs])
            b.dma("sp", ccs[64:128], I["s5_c_imT"][l], [], [ccs])
            b.dma("sp", dcol[:], I["s5_dT"][l], [], [dcol])
            b.dma("sp", glub[:], I["s5_glu_bT"][l], [], [glub])
            b.memset("pool", bbpad[:], 0.0, [bbpad])
            b.memset("pool", ccpad[:], 0.0, [ccpad])
            for g in range(32):
                c0 = 16 * (g % 8)
                b.ts("dve", bri[:, g, :], bri[:, g, :], cr[:, g:g + 1], None, ALU.mult, None, [bri, cr], [bri])
                b.stt(bbpad[:, g, c0:c0 + 16], bir[:, g, :], ci[:, g:g + 1], bri[:, g, :], ALU.mult, ALU.add,
                      [bir, ci, bri], [bbpad])
            for k in range(8):
                b.ts("dve", ccpad[:, k::8, 16 * k:16 * k + 16], ccs[:, k::8, :], sgn[:, 0:1], None, ALU.mult, None,
                     [ccs, sgn], [ccpad])
            pT = [ps(f"pT{i}", [128, 8, 128], BF16) for i in range(2)]
            for q in range(4):
                p_ = pT[q % 2]
                for j in range(8):
                    b.tr(p_[:, j, :], bbpad[:, q * 8 + j, :], ident[:], [bbpad, ident], [p_])
                b.evac(lhsB[:, q * 8:(q + 1) * 8, :], p_[:], [p_], [lhsB])

            P.flush()
            st.close()
            st = st_outer.enter_context(contextlib.ExitStack())
            ut = [sb(f"ut{i}", [128, S], BF16) for i in range(1)]
            Hs = [sb(f"H{i}", [128, S], BF16) for i in range(8)]
            Mms = [sb(f"Mm{i}", [128, 8, 12, 128], BF16) for i in range(2)]
            pX2 = pX + [ps(f"pXs{i}", [128, 512], F32) for i in range(4)]
            ya = [sb(f"ya{i}", [128, 512], F32) for i in range(2)]
            yb = [sb(f"yb{i}", [128, 512], F32) for i in range(2)]
            mt1 = [sb(f"mt1{i}", [128, 12, 128], BF16) for i in range(2)]
            mt2 = [sb(f"mt2{i}", [128, 12, 128], BF16) for i in range(2)]
            nx = [0]
            nsc = [0]

            pXcur = [pX2]

            def nextp():
                p_ = pXcur[0][nx[0] % len(pXcur[0])]
                nx[0] += 1
                return p_

            idB = identf[:].rearrange("p (o c) -> p o c", o=1).to_broadcast([128, 12, 128])
            swB = swapf[:].rearrange("p (o c) -> p o c", o=1).to_broadcast([128, 12, 128])

            def build_M(jj):
                Mm_ = Mms[jj % 2]
                for gi in range(8):
                    g = jj * 8 + gi
                    m1, m2 = mt1[gi % 2], mt2[gi % 2]
                    b.tt("pool", m1[:], idB, pw1[:, :, g:g + 1].to_broadcast([128, 12, 128]), ALU.mult,
                         [identf, pw1], [m1])
                    b.tt("pool", m2[:], swB, pw2[:, :, g:g + 1].to_broadcast([128, 12, 128]), ALU.mult,
                         [swapf, pw2], [m2])
                    b.tt("pool", Mm_[:, gi, :, :], m1[:], m2[:], ALU.add, [m1, m2], [Mm_])

            build_M(0)
            for j in range(4):
                u_t = ut[0]
                Mm = Mms[j % 2]
                b.dma("sp", u_t[:], sc["us5T"].t[j * 128:(j + 1) * 128, :], [sc["us5T"]], [u_t])
                for gi in range(8):
                    g = j * 8 + gi
                    for tt_ in range(8):
                        p_ = nextp()
                        b.mm(p_[:], lhsB[:, g, :], u_t[:, tt_ * 512:(tt_ + 1) * 512], True, True, [lhsB, u_t], [p_])
                        b.evac(Hs[gi][:, tt_ * 512:(tt_ + 1) * 512], p_[:], [p_], [Hs[gi]])

                if j + 1 < 4:
                    build_M(j + 1)

                def level(lv, down):
                    s_ = 1 << lv
                    n2 = S // (2 * s_)
                    for gi in range(8):
                        Hv = Hs[gi][:].rearrange("p (k s) -> p k s", s=2 * s_)
                        if not down:
                            k0, k1, doff, soff, dk = 0, n2, 2 * s_ - 1, s_ - 1, 0
                        else:
                            k0, k1, doff, soff, dk = 0, n2 - 1, s_ - 1, 2 * s_ - 1, 1
                        kk = k0
                        while kk < k1:
                            ke = min(kk + 512, k1)
                            p_ = nextp()
                            dst = Hv[:, kk + dk:ke + dk, doff]
                            nsc[0] += 1
                            if nsc[0] % 2 == 0:
                                b.mm(p_[:, 0:ke - kk], Mm[:, gi, lv, :], Hv[:, kk:ke, soff], True, True, [Mm, Hs[gi]], [p_])
                                b.tt("dve", dst, p_[:, 0:ke - kk], dst, ALU.add, [p_, Hs[gi]], [Hs[gi]])
                            else:
                                b.mm(p_[:, 0:ke - kk], Mm[:, gi, lv, :], Hv[:, kk:ke, soff], True, False, [Mm, Hs[gi]], [p_])
                                b.mm(p_[:, 0:ke - kk], ident[:], dst, False, True, [ident, Hs[gi]], [p_])
                                b.copy("act", dst, p_[:, 0:ke - kk], [p_], [Hs[gi]])
                            kk = ke

                for lv in range(12):
                    level(lv, False)
                for lv in range(10, -1, -1):
                    level(lv, True)

                for tt_ in range(8):
                    tsl = slice(tt_ * 512, (tt_ + 1) * 512)
                    p_ = nextp()
                    for gi in range(8):
                        b.mm(p_[:], ccpad[:, j * 8 + gi, :], Hs[gi][:, tsl], gi == 0, gi == 7, [ccpad, Hs[gi]], [p_])
                    a_, b_ = ya[tt_ % 2], yb[tt_ % 2]
                    b.stt(a_[:], u_t[:, tsl], dcol[:, j:j + 1], p_[:], ALU.mult, ALU.add, [u_t, dcol, p_], [a_])
                    b.tt("pool", b_[:], a_[:], a_[:], ALU.mult, [a_], [b_])
                    b.ts("pool", b_[:], b_[:], 0.044715, 1.0, ALU.mult, ALU.add, [b_], [b_])
                    b.tt("pool", b_[:], b_[:], a_[:], ALU.mult, [b_, a_], [b_])
                    b.act(b_[:], b_[:], AF.Sigmoid, [b_], [b_], scale=1.5957691216057308)
                    b.tt("pool", yT[:, j, tsl], a_[:], b_[:], ALU.mult, [a_, b_], [yT])

            P.flush()
            st.close()
            st = st_outer.enter_context(contextlib.ExitStack())
            pXcur[0] = pX
            wg = sb("wg", [128, 4, 1024], BF16)
            b.dma("pool", wg[:], I["s5_glu_w"][l].rearrange("(j p) e -> p j e", p=128), [], [wg])
            zt = [sb(f"zt{i}", [128, 512], BF16) for i in range(2)]
            sg = [sb(f"sg{i}", [128, 512], F32) for i in range(2)]
            og = [sb(f"og{i}", [128, 512], BF16) for i in range(2)]
            n_ = 0
            for c in range(4):
                for tt_ in range(8):
                    tsl = slice(tt_ * 512, (tt_ + 1) * 512)
                    z_, s_g, o_ = zt[n_ % 2], sg[n_ % 2], og[n_ % 2]
                    n_ += 1
                    b.dma("sp", z_[:], sc["zs5T"].t[c * 128:(c + 1) * 128, tsl], [sc["zs5T"]], [z_])
                    pa_, pg_ = nextp(), nextp()
                    for jj in range(4):
                        b.mm(pg_[:], wg[:, jj, 512 + c * 128:512 + (c + 1) * 128], yT[:, jj, tsl], jj == 0, jj == 3,
                             [wg, yT], [pg_])
                    for jj in range(4):
                        b.mm(pa_[:], wg[:, jj, c * 128:(c + 1) * 128], yT[:, jj, tsl], jj == 0, jj == 3, [wg, yT], [pa_])
                    b.act(s_g[:], pg_[:], AF.Sigmoid, [pg_, glub], [s_g], bias=glub[:, 4 + c:5 + c])
                    b.stt(s_g[:], pa_[:], glub[:, c:c + 1], s_g[:], ALU.add, ALU.mult, [pa_, glub, s_g], [s_g])
                    b.tt("pool", o_[:], s_g[:], z_[:], ALU.mult, [s_g, z_], [o_])
                    b.dma("sp", sc["mixT"].t[4 * tt_:4 * tt_ + 4, :, 8 + c, :].rearrange("a p q -> p a q"),
                          o_[:].rearrange("p (a q) -> p a q", a=4), [o_], [sc["mixT"]])
            P.flush()

    def phase_D(self, l):
        nc, P, b, I, sc = self.nc, self.P, self.b, self.I, self.sc
        with contextlib.ExitStack() as st:
            def sb(name, shape, dt):
                return T(st.enter_context(nc.sbuf_tensor(f"D{l}_{name}", shape, dt)), name)

            def ps(name, shape, dt):
                return T(st.enter_context(nc.psum_tensor(f"D{l}_{name}", shape, dt)), name, True)

            pw = sb("pw", [128, 4, 128], BF16)
            pscale = sb("pscale", [128, 4], F32)
            rc = sb("rc", [128, 4, 16], F32)
            b.dma("pool", pw[:], I["pool_w"][l].rearrange("k c d -> c k d"), [], [pw])
            b.dma("sp", pscale[:], I["pool_scaleT"][l], [], [pscale])
            b.dma("sp", rc[:], I["poolrc"][:, :, :], [], [rc])
            u = [sb(f"u{i}", [128, S], F32) for i in range(2)]
            sa = sb("sa", [128, S], F32)
            sbb = sb("sbb", [128, S], F32)
            pl = sb("pl", [128, S], BF16)
            sc2 = sb("sc2", [128, S // 2], F32)
            sa_lo, sa_hi, sb_lo, sb_hi, pl_lo, pl_hi = (Res() for _ in range(6))
            zt = [sb(f"zt{i}", [128, 512], BF16) for i in range(4)]
            og = [sb(f"og{i}", [128, 512], BF16) for i in range(4)]
            pX = [ps(f"pX{i}", [128, 512], F32) for i in range(4)]
            n_ = 0

            def load_u(kk):
                b.dma("sp", u[kk % 2][:], sc["upoolT"].t[kk * 128:(kk + 1) * 128, :], [sc["upoolT"]], [u[kk % 2]])

            load_u(0)
            for k in range(4):
                w = 2 << k
                u_ = u[k % 2]
                if k + 1 < 4:
                    load_u(k + 1)
                H_ = S // 2
                cur, cur_r = u_, (u_, u_)
                bufs = [(sa, (sa_lo, sa_hi)), (sbb, (sb_lo, sb_hi))]
                sh = 1
                step = 0
                while sh < w:
                    nxt, nxt_r = bufs[step % 2]
                    b.copy("dve", nxt[:, 0:sh], cur[:, 0:sh], [cur_r[0]], [nxt_r[0]])
                    b.tt("dve", nxt[:, sh:H_], cur[:, sh:H_], cur[:, 0:H_ - sh], ALU.add, [cur_r[0]], [nxt_r[0]])
                    b.tt("pool", nxt[:, H_:S], cur[:, H_:S], cur[:, H_ - sh:S - sh], ALU.add, [cur_r[0], cur_r[1]], [nxt_r[1]])
                    cur, cur_r = nxt, nxt_r
                    sh *= 2
                    step += 1
                b.tt("dve", pl[:, 0:16], cur[:, 0:16], rc[:, k, :], ALU.mult, [cur_r[0], rc], [pl_lo])
                b.tt("dve", pl[:, 0:16], pl[:, 0:16], u_[:, 0:16], ALU.subtract, [pl_lo, u_], [pl_lo])
                b.stt(pl[:, 16:H_], cur[:, 16:H_], 1.0 / w, u_[:, 16:H_], ALU.mult, ALU.subtract, [cur_r[0], u_], [pl_lo])
                b.ts("pool", sc2[:], cur[:, H_:S], 1.0 / w, 0.0, ALU.mult, ALU.add, [cur_r[1]], [sc2])
                b.tt("pool", pl[:, H_:S], sc2[:], u_[:, H_:S], ALU.subtract, [sc2, u_], [pl_hi])
                for tt_ in range(8):
                    tsl = slice(tt_ * 512, (tt_ + 1) * 512)
                    z_, o_ = zt[n_ % 4], og[n_ % 4]
                    p_ = pX[n_ % 4]
                    n_ += 1
                    b.dma("sp", z_[:], sc["zpoolT"].t[k * 128:(k + 1) * 128, tsl], [sc["zpoolT"]], [z_])
                    b.mm(p_[:], pw[:, k, :], pl[:, tsl], True, True, [pw, pl_lo if tt_ < 4 else pl_hi], [p_])
                    b.stt(o_[:], p_[:], pscale[:, k:k + 1], z_[:], ALU.mult, ALU.mult, [p_, pscale, z_], [o_])
                    b.dma("sp", sc["mixT"].t[4 * tt_:4 * tt_ + 4, :, 12 + k, :].rearrange("a p q -> p a q"),
                          o_[:].rearrange("p (a q) -> p a q", a=4), [o_], [sc["mixT"]])
            P.flush()

    def phase_E(self, l, xin, xout):
        nc, P, b, I, sc = self.nc, self.P, self.b, self.I, self.sc
        xin_ap = xin.t if isinstance(xin, T) else xin
        xin_res = [xin] if isinstance(xin, T) else []
        with contextlib.ExitStack() as st:
            def sb(name, shape, dt):
                return T(st.enter_context(nc.sbuf_tensor(f"E{l}_{name}", shape, dt)), name)

            def ps(name, shape, dt):
                return T(st.enter_context(nc.psum_tensor(f"E{l}_{name}", shape, dt)), name, True)

            wo = sb("wo", [128, 16, D], BF16)
            gb = sb("gb", [128, D], F32)
            mx = [sb(f"mx{i}", [128, 16, 128], BF16) for i in range(3)]
            xt = [sb(f"xt{i}", [128, D], F32) for i in range(3)]
            ot = [sb(f"ot{i}", [128, D], F32) for i in range(3)]
            sq = sb("sq", [128, D], BF16)
            st1 = [sb(f"st{i}", [128, 4], F32) for i in range(3)]
            pacc = [ps(f"pacc{i}", [128, 512], F32) for i in range(8)]
            for kq in range(4):
                b.dma("pool", wo[:, kq * 4:(kq + 1) * 4, :],
                      I["w_out"][l, kq * 512:(kq + 1) * 512, :].rearrange("(kc p) n -> p kc n", p=128),
                      [], [wo])
            b.dma("sp", gb[:], I["norm_post"][l:l + 1, :].to_broadcast([128, D]), [], [gb])
            def loads(t):
                tsl = slice(t * 128, (t + 1) * 128)
                b.dma("sp", mx[t % 3][:], sc["mixT"].t[t], [sc["mixT"]], [mx[t % 3]])
                b.dma("sp", xt[t % 3][:], xin_ap[tsl, :], xin_res, [xt[t % 3]])

            loads(0)
            loads(1)
            for t in range(NT):
                tsl = slice(t * 128, (t + 1) * 128)
                m_t, x_t, o_t, s_t = mx[t % 3], xt[t % 3], ot[t % 3], st1[t % 3]
                if t + 2 < NT:
                    loads(t + 2)
                for n in range(4):
                    pa = pacc[(t % 2) * 4 + n]
                    for kc in range(16):
                        b.mm(pa[:], m_t[:, kc, :], wo[:, kc, n * 512:(n + 1) * 512], kc == 0, kc == 15,
                             [m_t, wo], [pa])
                    b.evac(o_t[:, n * 512:(n + 1) * 512], pa[:], [pa], [o_t])
                b.act(sq[:], o_t[:], AF.Square, [o_t], [sq, s_t], accum_out=s_t[:, 0:1])
                b.ts("dve", s_t[:, 1:2], s_t[:, 0:1], 1.0 / D, 1e-6, ALU.mult, ALU.add, [s_t], [s_t])
                b.act(s_t[:, 2:3], s_t[:, 1:2], AF.Sqrt, [s_t], [s_t])
                P.op("dve", lambda e, s_t=s_t: e.reciprocal(out=s_t[:, 3:4], in_=s_t[:, 2:3]), [s_t], [s_t])
                b.stt(o_t[:], o_t[:], s_t[:, 3:4], gb[:], ALU.mult, ALU.mult, [o_t, s_t, gb], [o_t])
                b.tt("pool", o_t[:], o_t[:], x_t[:], ALU.add, [o_t, x_t], [o_t])
                b.dma("sp", xout.t[tsl, :], o_t[:], [o_t], [xout])
            P.flush()


def _host_inputs(inputs):
    perm = _perm_cols()
    w_in = np.asarray(inputs["w_in"])
    wp = np.zeros((DEPTH, D, WCOLS), np.float32)
    ok = perm >= 0
    wp[:, :, ok] = w_in[:, :, perm[ok]]
    shared = dict(_consts())
    shared["w_in"] = wp
    shared["norm_pre"] = np.ascontiguousarray(inputs["norm_pre"], dtype=np.float32)
    shared["norm_post"] = np.ascontiguousarray(inputs["norm_post"], dtype=np.float32)
    shared["w_out"] = np.ascontiguousarray(inputs["w_out"], dtype=np.float32)
    for kv in ("k", "v"):
        shared[f"cmp_w1_{kv}"] = np.ascontiguousarray(
            np.asarray(inputs[f"cmp_w1_{kv}"], dtype=np.float32).transpose(0, 2, 1, 3))
        shared[f"cmp_w2_{kv}"] = np.ascontiguousarray(inputs[f"cmp_w2_{kv}"], dtype=np.float32)
        shared[f"cmp_posT_{kv}"] = np.ascontiguousarray(
            np.asarray(inputs[f"cmp_pos_{kv}"], dtype=np.float32).transpose(0, 2, 1))
    f32 = lambda a: np.ascontiguousarray(np.asarray(a, dtype=np.float32))
    shared["s5_a_reT"] = f32(np.asarray(inputs["s5_a_re"]).transpose(0, 2, 1))
    shared["s5_a_imT"] = f32(np.asarray(inputs["s5_a_im"]).transpose(0, 2, 1))
    shared["s5_log_dt"] = f32(inputs["s5_log_dt"])
    shared["s5_b_reT"] = f32(np.asarray(inputs["s5_b_re"]).transpose(0, 2, 1, 3))
    shared["s5_b_imT"] = f32(np.asarray(inputs["s5_b_im"]).transpose(0, 2, 1, 3))
    shared["s5_c_reT"] = f32(np.asarray(inputs["s5_c_re"]).transpose(0, 3, 1, 2))
    shared["s5_c_imT"] = f32(np.asarray(inputs["s5_c_im"]).transpose(0, 3, 1, 2))
    shared["s5_dT"] = f32(np.asarray(inputs["s5_d"]).reshape(DEPTH, 4, 128).transpose(0, 2, 1))
    shared["s5_glu_bT"] = f32(np.asarray(inputs["s5_glu_b"]).reshape(DEPTH, 8, 128).transpose(0, 2, 1))
    shared["s5_glu_w"] = f32(inputs["s5_glu_w"])
    shared["pool_w"] = f32(inputs["pool_w"])
    shared["pool_scaleT"] = f32(np.asarray(inputs["pool_scale"]).reshape(DEPTH, 4, 128).transpose(0, 2, 1))
    return shared


def kernel(**inputs):
    x = np.asarray(inputs["x"], dtype=np.float32)
    shared = _host_inputs(inputs)
    k = Kern()
    nc = k.build()
    nb = x.shape[0]
    in_maps = []
    for bi in range(nb):
        m = dict(shared)
        m["x"] = np.ascontiguousarray(x[bi])
        in_maps.append(m)
    res = run_bass_kernel_spmd(nc, in_maps, core_ids=list(range(nb)))
    return np.stack([np.asarray(r["y"]) for r in res.results], 0).astype(np.float32)
```
